# Optimizing a Trainium2 kernel written in Bass

```python
import math
import jax, jax.numpy as jnp
from jax import lax
import numpy as np

D_MODEL = 2048
BATCH = 4
SEQ = 4096
DEPTH = 1

N_META = 16
MIX_WIDTH = D_MODEL
SSM_WIDTH = MIX_WIDTH // 2
ATTN_WIDTH = MIX_WIDTH - SSM_WIDTH
SSM_GROUP_CH = 16
SSM_GROUPS = SSM_WIDTH // SSM_GROUP_CH
SSM_STATE = 64
ATTN_HEAD_DIM = 64
ATTN_HEADS = ATTN_WIDTH // ATTN_HEAD_DIM
Q_BLOCK = 128
D_FF = (11 * D_MODEL) // 4
CONV_W = 3
IN_PROJ = SSM_WIDTH + 3 * ATTN_WIDTH
RMS_EPS = 1e-6
DT_MIN = 1e-3
DT_MAX = 1e-1

kernel_name = "hymba_s5_stickbreak_convffn"


def rmsnorm(x, g):
    xf = x.astype(jnp.float32)
    y = xf * lax.rsqrt(jnp.mean(xf * xf, axis=-1, keepdims=True) + RMS_EPS)
    return (y * g.astype(jnp.float32)).astype(x.dtype)


def _complex_scan_combine(e1, e2):
    a1r, a1i, b1r, b1i = e1
    a2r, a2i, b2r, b2i = e2
    ar = a2r * a1r - a2i * a1i
    ai = a2r * a1i + a2i * a1r
    br = a2r * b1r - a2i * b1i + b2r
    bi = a2r * b1i + a2i * b1r + b2i
    return (ar, ai, br, bi)


def s5_mixer(u, lam_re, lam_im, log_dt, b_re, b_im, c_re, c_im, d_skip, w_glu, b_glu):
    f32 = jnp.float32
    bsz, L, _ = u.shape
    uf = u.astype(f32).reshape(bsz, L, SSM_GROUPS, SSM_GROUP_CH)
    lr = jnp.minimum(lam_re.astype(f32), -1e-4)
    li = lam_im.astype(f32)
    delta = jnp.exp(log_dt.astype(f32))[:, None]
    mag = jnp.exp(lr * delta)
    ar = mag * jnp.cos(li * delta)
    ai = mag * jnp.sin(li * delta)
    den = lr * lr + li * li
    nr = ar - 1.0
    ni = ai
    fr = (nr * lr + ni * li) / den
    fi = (ni * lr - nr * li) / den
    brf = b_re.astype(f32)
    bif = b_im.astype(f32)
    bbar_r = fr[..., None] * brf - fi[..., None] * bif
    bbar_i = fr[..., None] * bif + fi[..., None] * brf
    bu_r = jnp.einsum('blgh,gph->blgp', uf, bbar_r)
    bu_i = jnp.einsum('blgh,gph->blgp', uf, bbar_i)
    a_r = jnp.broadcast_to(ar[None, None], (1, L, SSM_GROUPS, SSM_STATE))
    a_i = jnp.broadcast_to(ai[None, None], (1, L, SSM_GROUPS, SSM_STATE))
    _, _, h_r, h_i = lax.associative_scan(_complex_scan_combine, (a_r, a_i, bu_r, bu_i), axis=1)
    y = (jnp.einsum('blgp,ghp->blgh', h_r, c_re.astype(f32))
         - jnp.einsum('blgp,ghp->blgh', h_i, c_im.astype(f32))
         + d_skip.astype(f32) * uf)
    y = jax.nn.gelu(y.reshape(bsz, L, SSM_WIDTH))
    y = y * jax.nn.sigmoid(y @ w_glu.astype(f32) + b_glu.astype(f32))
    return y.astype(u.dtype)


def _stick_breaking_block(q_blk, qpos, k, v, kpos):
    scale = 1.0 / math.sqrt(ATTN_HEAD_DIM)
    z = jnp.einsum('bqhd,bkhd->bhqk', q_blk, k).astype(jnp.float32) * scale
    valid = (kpos[None, :] < qpos[:, None])[None, None]
    log_keep = jnp.where(valid, jax.nn.log_sigmoid(-z), 0.0)
    suffix = lax.cumsum(log_keep, axis=3, reverse=True) - log_keep
    w = jnp.where(valid, jnp.exp(jax.nn.log_sigmoid(z) + suffix), 0.0)
    return jnp.einsum('bhqk,bkhd->bqhd', w.astype(v.dtype), v)


def stick_breaking_attention(q, k, v):
    bsz, L, H, Dh = q.shape
    kpos = jnp.arange(L, dtype=jnp.int32)
    out_meta = _stick_breaking_block(q[:, :N_META], jnp.arange(N_META, dtype=jnp.int32), k, v, kpos)
    n_real = L - N_META
    nb = n_real // Q_BLOCK
    qb = q[:, N_META:].reshape(bsz, nb, Q_BLOCK, H, Dh).transpose(1, 0, 2, 3, 4)
    qpos = (N_META + jnp.arange(n_real, dtype=jnp.int32)).reshape(nb, Q_BLOCK)
    out = lax.map(lambda a: _stick_breaking_block(a[0], a[1], k, v, kpos), (qb, qpos))
    out = out.transpose(1, 0, 2, 3, 4).reshape(bsz, n_real, H, Dh)
    return jnp.concatenate([out_meta, out], axis=1)


def causal_depthwise_conv(x, w, b):
    c = x.shape[-1]
    y = lax.conv_general_dilated(x, w[:, None, :], window_strides=(1,), padding=[(CONV_W - 1, 0)],
                                 dimension_numbers=('NWC', 'WIO', 'NWC'), feature_group_count=c)
    return y + b


def setup_inputs(seed: int = 0) -> dict:
    key = jax.random.key(seed)
    ks = jax.random.split(key, 24)
    f32 = jnp.float32
    n = lambda k, s, sc: jax.random.normal(k, s, f32) * sc
    x = n(ks[0], (BATCH, SEQ, D_MODEL), 1.0)
    meta_tokens = n(ks[1], (N_META, D_MODEL), 1.0)
    norm_mix_g = 1.0 + n(ks[2], (DEPTH, D_MODEL), 0.02)
    w_in = n(ks[3], (DEPTH, D_MODEL, IN_PROJ), D_MODEL ** -0.5)
    ssm_lambda_re = -0.5 + n(ks[4], (DEPTH, SSM_GROUPS, SSM_STATE), 0.01)
    ssm_lambda_im = jnp.broadcast_to(jnp.pi * jnp.arange(SSM_STATE, dtype=f32),
                                     (DEPTH, SSM_GROUPS, SSM_STATE)) + 0.0
    ssm_log_dt = jax.random.uniform(ks[5], (DEPTH, SSM_GROUPS), f32,
                                    math.log(DT_MIN), math.log(DT_MAX))
    b_scale = (2.0 * SSM_GROUP_CH) ** -0.5
    ssm_b_re = n(ks[6], (DEPTH, SSM_GROUPS, SSM_STATE, SSM_GROUP_CH), b_scale)
    ssm_b_im = n(ks[7], (DEPTH, SSM_GROUPS, SSM_STATE, SSM_GROUP_CH), b_scale)
    c_scale = (2.0 * SSM_STATE) ** -0.5
    ssm_c_re = n(ks[8], (DEPTH, SSM_GROUPS, SSM_GROUP_CH, SSM_STATE), c_scale)
    ssm_c_im = n(ks[9], (DEPTH, SSM_GROUPS, SSM_GROUP_CH, SSM_STATE), c_scale)
    ssm_d = n(ks[10], (DEPTH, SSM_GROUPS, SSM_GROUP_CH), 1.0)
    w_glu = n(ks[11], (DEPTH, SSM_WIDTH, SSM_WIDTH), SSM_WIDTH ** -0.5)
    b_glu = n(ks[12], (DEPTH, SSM_WIDTH), 0.01)
    g_ssm_out = 1.0 + n(ks[13], (DEPTH, SSM_WIDTH), 0.02)
    g_attn_out = 1.0 + n(ks[14], (DEPTH, ATTN_WIDTH), 0.02)
    w_out = n(ks[15], (DEPTH, MIX_WIDTH, D_MODEL), MIX_WIDTH ** -0.5)
    norm_ffn_g = 1.0 + n(ks[16], (DEPTH, D_MODEL), 0.02)
    w_up = n(ks[17], (DEPTH, D_MODEL, 2 * D_FF), D_MODEL ** -0.5)
    conv_w = n(ks[18], (DEPTH, CONV_W, 2 * D_FF), CONV_W ** -0.5)
    conv_b = n(ks[19], (DEPTH, 2 * D_FF), 0.01)
    w_down = n(ks[20], (DEPTH, D_FF, D_MODEL), D_FF ** -0.5)
    norm_final_g = 1.0 + n(ks[21], (D_MODEL,), 0.02)
    return {"x": x, "meta_tokens": meta_tokens, "norm_mix_g": norm_mix_g, "w_in": w_in,
            "ssm_lambda_re": ssm_lambda_re, "ssm_lambda_im": ssm_lambda_im, "ssm_log_dt": ssm_log_dt,
            "ssm_b_re": ssm_b_re, "ssm_b_im": ssm_b_im, "ssm_c_re": ssm_c_re, "ssm_c_im": ssm_c_im,
            "ssm_d": ssm_d, "w_glu": w_glu, "b_glu": b_glu, "g_ssm_out": g_ssm_out,
            "g_attn_out": g_attn_out, "w_out": w_out, "norm_ffn_g": norm_ffn_g, "w_up": w_up,
            "conv_w": conv_w, "conv_b": conv_b, "w_down": w_down, "norm_final_g": norm_final_g}


def reference(x, meta_tokens, norm_mix_g, w_in, ssm_lambda_re, ssm_lambda_im, ssm_log_dt,
              ssm_b_re, ssm_b_im, ssm_c_re, ssm_c_im, ssm_d, w_glu, b_glu, g_ssm_out,
              g_attn_out, w_out, norm_ffn_g, w_up, conv_w, conv_b, w_down, norm_final_g):
    bsz = x.shape[0]
    meta = jnp.broadcast_to(meta_tokens[None].astype(x.dtype), (bsz, N_META, D_MODEL))
    h = jnp.concatenate([meta, x], axis=1)
    L = h.shape[1]
    for i in range(DEPTH):
        hn = rmsnorm(h, norm_mix_g[i])
        proj = hn @ w_in[i]
        u = proj[..., :SSM_WIDTH]
        q, k, v = jnp.split(proj[..., SSM_WIDTH:], 3, axis=-1)
        q = q.reshape(bsz, L, ATTN_HEADS, ATTN_HEAD_DIM)
        k = k.reshape(bsz, L, ATTN_HEADS, ATTN_HEAD_DIM)
        v = v.reshape(bsz, L, ATTN_HEADS, ATTN_HEAD_DIM)
        y_ssm = s5_mixer(u, ssm_lambda_re[i], ssm_lambda_im[i], ssm_log_dt[i], ssm_b_re[i],
                         ssm_b_im[i], ssm_c_re[i], ssm_c_im[i], ssm_d[i], w_glu[i], b_glu[i])
        y_attn = stick_breaking_attention(q, k, v).reshape(bsz, L, ATTN_WIDTH)
        mixed = jnp.concatenate([rmsnorm(y_ssm, g_ssm_out[i]), rmsnorm(y_attn, g_attn_out[i])], axis=-1)
        h = h + mixed @ w_out[i]
        hn = rmsnorm(h, norm_ffn_g[i])
        up = causal_depthwise_conv(hn @ w_up[i], conv_w[i], conv_b[i])
        gate, val = jnp.split(up, 2, axis=-1)
        h = h + (jax.nn.silu(gate) * val) @ w_down[i]
    out = rmsnorm(h, norm_final_g)
    return out[:, N_META:]
```

```python
import bisect
from contextlib import ExitStack
import numpy as np
import ml_dtypes
import concourse.bass as bass
import concourse.mybir as mybir
from concourse.bass_utils import run_bass_kernel_spmd

F32 = mybir.dt.float32
BF16 = mybir.dt.bfloat16
I32 = mybir.dt.int32
AF = mybir.ActivationFunctionType
ALU = mybir.AluOpType

DSIZE = {F32: 4, BF16: 2, I32: 4}


class Acc:
    __slots__ = ("eng", "idx", "dtok")

    def __init__(self, eng, idx, dtok):
        self.eng = eng
        self.idx = idx
        self.dtok = dtok


class Buf:
    __slots__ = ("name", "w", "rs")

    def __init__(self, name):
        self.name = name
        self.w = None
        self.rs = {}


class EngS:
    def __init__(self, name, nslots=0):
        self.name = name
        self.recs = []
        self.ops = []
        self.nops = 0
        self.count = 0
        self.sig_idx = []
        self.sig_val = []
        self.seen = {}
        self.nslots = nslots
        self.dslot = 0
        self.dvals = [0] * nslots


class Prog:
    COMPUTE = ("pe", "act", "dve", "pool")

    def __init__(self, nc, sp_slots=40, pool_slots=24):
        self.nc = nc
        self.E = {n: EngS(n) for n in self.COMPUTE}
        self.E["sp"] = EngS("sp", sp_slots)
        self.E["pool"].nslots = pool_slots
        self.E["pool"].dvals = [0] * pool_slots
        self.ndma = 0

    def resolve(self, acc):
        if acc.dtok is not None:
            return acc.dtok
        e = self.E[acc.eng]
        i = bisect.bisect_left(e.sig_idx, acc.idx)
        if i < len(e.sig_idx):
            return (e.name, e.sig_val[i])
        e.count += 1
        e.ops[-1][1] = True
        e.sig_idx.append(e.nops - 1)
        e.sig_val.append(e.count)
        return (e.name, e.count)

    def _waits(self, E, deps):
        cur = E.nops
        for acc in deps:
            if acc.dtok is None and acc.eng == E.name:
                if E.name == "pe":
                    continue
                if cur - acc.idx > 3:
                    continue
            key, val = self.resolve(acc)
            if E.seen.get(key, 0) < val:
                E.recs.append(("wait", key, val))
                E.seen[key] = val

    @staticmethod
    def _deps(reads, writes):
        deps = []
        for b in reads:
            if b.w is not None:
                deps.append(b.w)
        for b in writes:
            if b.w is not None:
                deps.append(b.w)
            deps.extend(b.rs.values())
        return deps

    def op(self, eng, fn, reads=(), writes=()):
        E = self.E[eng]
        self._waits(E, self._deps(reads, writes))
        rec = [fn, False]
        E.recs.append(("op", rec))
        E.ops.append(rec)
        acc = Acc(eng, E.nops, None)
        E.nops += 1
        for b in reads:
            b.rs[eng] = acc
        for b in writes:
            b.w = acc
            b.rs = {}
        return acc

    def dma(self, q, out, in_, reads=(), writes=()):
        Q = self.E[q]
        self._waits(Q, self._deps(reads, writes))
        slot = Q.dslot
        Q.dslot = (Q.dslot + 1) % Q.nslots
        key = ("D", q, slot)
        prev = Q.dvals[slot]
        if prev > 0 and Q.seen.get(key, 0) < prev:
            Q.recs.append(("wait", key, prev))
            Q.seen[key] = prev
        val = prev + 16
        Q.dvals[slot] = val
        Q.recs.append(("dma", out, in_, key, val))
        acc = Acc(q, None, (key, val))
        for b in reads:
            b.rs[key] = acc
        for b in writes:
            b.w = acc
            b.rs = {}
        self.ndma += 1
        return acc

    def barrier(self):
        toks = []
        for n in self.COMPUTE:
            e = self.E[n]
            if e.nops > 0:
                toks.append(self.resolve(Acc(n, e.nops - 1, None)))
        for q in ("sp", "pool"):
            Q = self.E[q]
            for s in range(Q.nslots):
                if Q.dvals[s] > 0:
                    toks.append((("D", q, s), Q.dvals[s]))
        for n, E in self.E.items():
            for key, val in toks:
                if key == n:
                    continue
                if E.seen.get(key, 0) < val:
                    E.recs.append(("wait", key, val))
                    E.seen[key] = val

    def finish(self):
        self.barrier()

    def replay(self, stack):
        nc = self.nc
        sems = {}
        for n in self.COMPUTE:
            sems[n] = stack.enter_context(nc.semaphore("s_" + n))
        for q in ("sp", "pool"):
            for s in range(self.E[q].nslots):
                sems[("D", q, s)] = stack.enter_context(nc.semaphore("d_%s_%d" % (q, s)))
        block = stack.enter_context(nc.Block())

        def run(name):
            def f(eng):
                E = self.E[name]
                for r in E.recs:
                    if r[0] == "wait":
                        eng.wait_ge(sems[r[1]], r[2])
                    elif r[0] == "op":
                        ins = r[1][0](eng)
                        if r[1][1]:
                            ins.then_inc(sems[name], 1)
                    else:
                        eng.dma_start(out=r[1], in_=r[2]).then_inc(sems[r[3]], 16)
            return f

        block.sync(run("sp"))
        block.tensor(run("pe"))
        block.scalar(run("act"))
        block.vector(run("dve"))
        block.gpsimd(run("pool"))


class Arena:
    LO = 16512
    HI = 229344

    def __init__(self, nc):
        self.nc = nc
        self.lo = self.LO
        self.hi = self.HI
        self.n = 0

    @staticmethod
    def _size(shape, dtype):
        n = 1
        for s in shape[1:]:
            n *= s
        return (n * DSIZE[dtype] + 63) // 64 * 64

    def left(self, name, shape, dtype):
        sz = self._size(shape, dtype)
        off = self.lo
        self.lo += sz
        assert self.lo <= self.hi, ("sbuf overflow", name, self.lo, self.hi)
        self.n += 1
        return self.nc.alloc_sbuf_tensor_at("%s_%d" % (name, self.n), list(shape), dtype, offset=off)

    def right(self, name, shape, dtype):
        sz = self._size(shape, dtype)
        self.hi -= sz
        assert self.lo <= self.hi, ("sbuf overflow", name, self.lo, self.hi)
        self.n += 1
        return self.nc.alloc_sbuf_tensor_at("%s_%d" % (name, self.n), list(shape), dtype, offset=self.hi)

D = 2048
NPRE = 2048
NLEAD = 16
NOWN = 2048
NO = NLEAD + NOWN
NK = NPRE + NO
DFF = 5632
NJ = DFF // 128
CH_PRE = NPRE // 8
CH_OWN = NO // 8
EPS = 1e-6
TWO_PI = 2.0 * np.pi

V_GMIX = 0
V_GFFN = 16
V_GSSM = 32
V_GATT = 40
V_BGLU = 48
V_DSK = 56
V_CW = 64
V_CB = V_CW + 264
V_PBIAS = V_CB + 88
V_PSCALE = V_PBIAS + 1
V_ZERO = V_PSCALE + 1
V_ONE = V_ZERO + 1
V_EPS = V_ONE + 1
V_NEGPI = V_EPS + 1
V_BAND = V_NEGPI + 1
NV = V_BAND + 4


class BufMap:
    def __init__(self):
        self.d = {}

    def __call__(self, *key):
        b = self.d.get(key)
        if b is None:
            b = Buf(str(key))
            self.d[key] = b
        return b


def bcast_last(ap2d, n):
    a = [list(x) for x in ap2d.ap]
    return bass.AP(ap2d.tensor, ap2d.offset, a + [[0, n]])


class Ctx:
    pass


def derive_trig(C, n, lam_re_d, lam_im_d, ldt_d, T, TI, tag):
    P, B, vecs = C.P, C.B, C.vecs
    b = [B("dT", tag, i) for i in range(11)]
    bi = B("dTI", tag)
    bv = B("vecs")

    def tt(o, a, c, op):
        P.op("dve", lambda e: e.tensor_tensor(out=T[o][:], in0=T[a][:], in1=T[c][:], op=op), reads=[b[a], b[c]], writes=[b[o]])

    def ts(o, a, s1, s2, op0, op1=None):
        if op1 is None:
            P.op("dve", lambda e: e.tensor_scalar(out=T[o][:], in0=T[a][:], scalar1=s1, scalar2=None, op0=op0), reads=[b[a]], writes=[b[o]])
        else:
            P.op("dve", lambda e: e.tensor_scalar(out=T[o][:], in0=T[a][:], scalar1=s1, scalar2=s2, op0=op0, op1=op1), reads=[b[a]], writes=[b[o]])

    def act(o, a, func, scale=1.0, bias=None):
        if bias is None:
            P.op("act", lambda e: e.activation(out=T[o][:], in_=T[a][:], func=func, scale=scale), reads=[b[a]], writes=[b[o]])
        else:
            P.op("act", lambda e: e.activation(out=T[o][:], in_=T[a][:], func=func, scale=scale, bias=bias), reads=[b[a], bv], writes=[b[o]])

    P.dma("sp", T[0][:], lam_re_d, writes=[b[0]])
    P.dma("sp", T[1][:], lam_im_d, writes=[b[1]])
    P.dma("sp", T[2][:], ldt_d, writes=[b[2]])
    ts(0, 0, -1e-4, None, ALU.min)
    act(2, 2, AF.Exp)
    tt(3, 0, 2, ALU.mult)
    act(3, 3, AF.Exp)
    tt(4, 1, 2, ALU.mult)
    ts(4, 4, 1.0 / TWO_PI, 64.5, ALU.mult, ALU.add)

    def sin_of(r, o, t2):
        P.op("dve", lambda e: e.tensor_copy(out=TI[:], in_=T[r][:]), reads=[b[r]], writes=[bi])
        P.op("dve", lambda e: e.tensor_copy(out=T[o][:], in_=TI[:]), reads=[bi], writes=[b[o]])
        tt(o, r, o, ALU.subtract)
        ts(t2, o, 0.0, None, ALU.is_lt)
        tt(o, o, t2, ALU.add)
        ts(o, o, TWO_PI, None, ALU.mult)
        act(o, o, AF.Sin, 1.0, vecs[:, V_NEGPI:V_NEGPI + 1])

    sin_of(4, 5, 6)
    ts(4, 4, 0.25, None, ALU.add)
    sin_of(4, 6, 7)
    tt(7, 3, 6, ALU.mult)
    tt(8, 3, 5, ALU.mult)
    tt(4, 0, 0, ALU.mult)
    tt(5, 1, 1, ALU.mult)
    tt(4, 4, 5, ALU.add)
    P.op("dve", lambda e: e.reciprocal(out=T[4][:], in_=T[4][:]), reads=[b[4]], writes=[b[4]])
    ts(3, 7, -1.0, None, ALU.add)
    tt(5, 3, 0, ALU.mult)
    tt(6, 8, 1, ALU.mult)
    tt(5, 5, 6, ALU.add)
    tt(5, 5, 4, ALU.mult)
    tt(6, 8, 0, ALU.mult)
    tt(9, 3, 1, ALU.mult)
    tt(6, 6, 9, ALU.subtract)
    tt(6, 6, 4, ALU.mult)
    return b


def cmul_pool(C, eng, outr, outi, ar, ai, br, bi_, t1, t2, rd, wr):
    P = C.P
    P.op(eng, lambda e: e.tensor_tensor(out=t1, in0=ar, in1=br, op=ALU.mult), reads=rd, writes=[wr[2]])
    P.op(eng, lambda e: e.tensor_tensor(out=t2, in0=ai, in1=bi_, op=ALU.mult), reads=rd, writes=[wr[3]])
    P.op(eng, lambda e: e.tensor_tensor(out=outr, in0=t1, in1=t2, op=ALU.subtract), reads=[wr[2], wr[3]], writes=[wr[0]])
    P.op(eng, lambda e: e.tensor_tensor(out=t1, in0=ar, in1=bi_, op=ALU.mult), reads=rd, writes=[wr[2]])
    P.op(eng, lambda e: e.tensor_tensor(out=t2, in0=ai, in1=br, op=ALU.mult), reads=rd, writes=[wr[3]])
    P.op(eng, lambda e: e.tensor_tensor(out=outi, in0=t1, in1=t2, op=ALU.add), reads=[wr[2], wr[3]], writes=[wr[1]])


def derive_X(C, LG, ssmX_d, BX_d, base_off):
    P, B, nc = C.P, C.B, C.nc
    n = 256
    T = [nc.alloc_sbuf_tensor_at("dX%d" % i, [128, n], F32, offset=base_off + i * n * 4) for i in range(11)]
    TI = nc.alloc_sbuf_tensor_at("dXi", [128, n], I32, offset=base_off + 11 * n * 4)
    Bre = nc.alloc_sbuf_tensor_at("dXbr", [128, n], F32, offset=base_off + 12 * n * 4)
    Bim = nc.alloc_sbuf_tensor_at("dXbi", [128, n], F32, offset=base_off + 13 * n * 4)
    used = 14 * n * 4
    allb = []
    for h in range(4):
        c0 = 256 * h
        b = derive_trig(C, n, ssmX_d[:, c0:c0 + n], ssmX_d[:, 1024 + c0:1024 + c0 + n], ssmX_d[:, 2048 + c0:2048 + c0 + n], T, TI, "X")
        bbr, bbi = B("dXbr"), B("dXbi")
        P.dma("sp", Bre[:], BX_d[:, c0:c0 + n], writes=[bbr])
        P.dma("sp", Bim[:], BX_d[:, 1024 + c0:1024 + c0 + n], writes=[bbi])
        cur = (5, 6)
        nxt = (4, 9)
        for k in range(8):
            j = 7 - k
            lgr = LG[:, 0, j, 2 * h:2 * h + 2, :].rearrange("p a b -> p (a b)")
            lgi = LG[:, 1, j, 2 * h:2 * h + 2, :].rearrange("p a b -> p (a b)")
            cmul_pool(C, "pool", lgr, lgi, T[cur[0]][:], T[cur[1]][:], Bre[:], Bim[:], T[0][:], T[1][:],
                      [b[cur[0]], b[cur[1]], bbr, bbi], [B("LG"), B("LG"), b[0], b[1]])
            if k < 7:
                cmul_pool(C, "pool", T[nxt[0]][:], T[nxt[1]][:], T[cur[0]][:], T[cur[1]][:], T[7][:], T[8][:], T[2][:], T[3][:],
                          [b[cur[0]], b[cur[1]], b[7], b[8]], [b[nxt[0]], b[nxt[1]], b[2], b[3]])
                cur, nxt = nxt, cur
        allb = b + [bbr, bbi, B("dTI", "X")]
    return allb, used


def g_matmuls(C, uT, nch, GS, LG, uTm):
    P, B, ps, bank, vecs = C.P, C.B, C.ps, C.bank, C.vecs
    for t in range(8):
        for q in range(4):
            P.op("act", lambda e, t=t, q=q: e.activation(out=uTm[q][:, :, 0:nch], in_=uT[:, t, :, 0:nch], func=AF.Copy,
                                                          scale=vecs[:, V_BAND + q:V_BAND + q + 1]),
                 reads=[B("uT"), B("vecs")], writes=[B("uTm", q)])
        for q in range(4):
            pair = 4 * t + q
            for part in range(2):
                bk = bank()
                for j in range(8):
                    P.op("pe", lambda e, j=j, t=t, q=q, part=part, bk=bk: e.matmul(
                        ps[:, bk, 0:nch], lhsT=LG[:, part, j, t, :], rhs=uTm[q][:, j, 0:nch], start=(j == 0), stop=(j == 7)),
                        reads=[B("LG"), B("uTm", q)], writes=[B("ps", bk)])
                P.op("dve", lambda e, bk=bk, part=part, pair=pair: e.tensor_copy(out=GS[:, part, pair, 1:1 + nch], in_=ps[:, bk, 0:nch]),
                     reads=[B("ps", bk)], writes=[B("GSall")])


def scan_steps(C, GS, S2, M1, M2, T1, T2, nsteps, keep_hist):
    P, B = C.P, C.B
    for s in range(1, nsteps + 1):
        a = S2[(s - 1) % 2]
        o = S2[s % 2]
        ba, bo = B("S2", (s - 1) % 2), B("S2", s % 2)
        P.op("pool", lambda e, a=a: e.tensor_tensor(out=T1[:], in0=M1[:], in1=a[:], op=ALU.mult), reads=[B("M12"), ba], writes=[B("scT1")])
        P.op("pool", lambda e, a=a: e.tensor_tensor(out=T2[:, 0, :], in0=M2[:, 0, :], in1=a[:, 1, :], op=ALU.mult), reads=[B("M12"), ba], writes=[B("scT2")])
        P.op("pool", lambda e, a=a: e.tensor_tensor(out=T2[:, 1, :], in0=M2[:, 1, :], in1=a[:, 0, :], op=ALU.mult), reads=[B("M12"), ba], writes=[B("scT2")])
        P.op("pool", lambda e: e.tensor_tensor(out=T1[:], in0=T1[:], in1=T2[:], op=ALU.add), reads=[B("scT1"), B("scT2")], writes=[B("scT1")])
        P.op("pool", lambda e, o=o, s=s: e.tensor_tensor(out=o[:], in0=T1[:], in1=GS[:, :, :, s], op=ALU.add), reads=[B("scT1"), B("GSall")], writes=[bo])
        if keep_hist:
            P.op("act", lambda e, o=o, s=s: e.activation(out=GS[:, :, :, s], in_=o[:], func=AF.Copy), reads=[bo], writes=[B("GSh", s)])
        yield s


def derive_P_small(C, ssmP_d, EP, FP, M1, M2):
    P, B, A = C.P, C.B, C.A
    n = 32
    T = [A.right("dP%d" % i, [128, n], F32) for i in range(11)]
    TI = A.right("dPi", [128, n], I32)
    b = derive_trig(C, n, ssmP_d[:, 0:32], ssmP_d[:, 32:64], ssmP_d[:, 64:96], T, TI, "P")
    be = B("EP")
    P.op("dve", lambda e: e.memset(EP[:, 0, 0, :], 1.0), writes=[be])
    P.op("dve", lambda e: e.memset(EP[:, 1, 0, :], 0.0), writes=[be])
    P.op("dve", lambda e: e.tensor_copy(out=EP[:, 0, 1, :], in_=T[7][:]), reads=[b[7]], writes=[be])
    P.op("dve", lambda e: e.tensor_copy(out=EP[:, 1, 1, :], in_=T[8][:]), reads=[b[8]], writes=[be])
    P.op("dve", lambda e: e.tensor_copy(out=FP[:, 0, :], in_=T[5][:]), reads=[b[5]], writes=[B("FP")])
    P.op("dve", lambda e: e.tensor_copy(out=FP[:, 1, :], in_=T[6][:]), reads=[b[6]], writes=[B("FP")])
    for k in range(1, 8):
        cmul_pool(C, "dve", EP[:, 0, k + 1, :], EP[:, 1, k + 1, :], EP[:, 0, k, :], EP[:, 1, k, :], T[7][:], T[8][:], T[0][:], T[1][:],
                  [be, b[7], b[8]], [be, be, b[0], b[1]])
    bm = B("M12")
    P.op("dve", lambda e: e.tensor_copy(out=M1[:, 0, :], in_=EP[:, 0, 8, :]), reads=[be], writes=[bm])
    P.op("dve", lambda e: e.tensor_copy(out=M1[:, 1, :], in_=EP[:, 0, 8, :]), reads=[be], writes=[bm])
    P.op("dve", lambda e: e.tensor_scalar(out=M2[:, 0, :], in0=EP[:, 1, 8, :], scalar1=-1.0, scalar2=None, op0=ALU.mult), reads=[be], writes=[bm])
    P.op("dve", lambda e: e.tensor_copy(out=M2[:, 1, :], in_=EP[:, 1, 8, :]), reads=[be], writes=[bm])


def derive_LZ_K(C, EP, FP, BP_d, CZ_d, LZ, KTi, BbP, tmp_off):
    P, B, nc, ps, bank = C.P, C.B, C.nc, C.ps, C.bank
    n = 1024
    Cre = nc.alloc_sbuf_tensor_at("dZcr", [128, 32, 32], F32, offset=tmp_off)
    Cim = nc.alloc_sbuf_tensor_at("dZci", [128, 32, 32], F32, offset=tmp_off + 4096)
    t1 = nc.alloc_sbuf_tensor_at("dZt1", [128, 32, 32], F32, offset=tmp_off + 8192)
    t2 = nc.alloc_sbuf_tensor_at("dZt2", [128, 32, 32], F32, offset=tmp_off + 12288)
    bcr, bci, bt1, bt2 = B("dZcr"), B("dZci"), B("dZt1"), B("dZt2")
    be, bl = B("EP"), B("LZ")
    P.dma("sp", Cre[:].rearrange("p a b -> p (a b)"), CZ_d[:, 0:1024], writes=[bcr])
    P.dma("sp", Cim[:].rearrange("p a b -> p (a b)"), CZ_d[:, 1024:2048], writes=[bci])
    eng = "pool"
    for k in range(9):
        er = bcast_last(EP[:, 0, k, :], 32)
        ei = bcast_last(EP[:, 1, k, :], 32)
        P.op(eng, lambda e, er=er: e.tensor_tensor(out=t1[:], in0=Cre[:], in1=er, op=ALU.mult), reads=[bcr, be], writes=[bt1])
        P.op(eng, lambda e, ei=ei: e.tensor_tensor(out=t2[:], in0=Cim[:], in1=ei, op=ALU.mult), reads=[bci, be], writes=[bt2])
        P.op(eng, lambda e, k=k: e.tensor_tensor(out=LZ[:, 0, k, :, :], in0=t1[:], in1=t2[:], op=ALU.subtract), reads=[bt1, bt2], writes=[bl])
        P.op(eng, lambda e, ei=ei: e.tensor_tensor(out=t1[:], in0=Cre[:], in1=ei, op=ALU.mult), reads=[bcr, be], writes=[bt1])
        P.op(eng, lambda e, er=er: e.tensor_tensor(out=t2[:], in0=Cim[:], in1=er, op=ALU.mult), reads=[bci, be], writes=[bt2])
        P.op(eng, lambda e: e.tensor_tensor(out=t1[:], in0=t1[:], in1=t2[:], op=ALU.add), reads=[bt1, bt2], writes=[bt1])
        P.op(eng, lambda e, k=k: e.tensor_scalar(out=LZ[:, 1, k, :, :], in0=t1[:], scalar1=-1.0, scalar2=None, op0=ALU.mult), reads=[bt1], writes=[bl])
    bb = B("BbP")
    P.dma("sp", Cre[:].rearrange("p a b -> p (a b)"), BP_d[:, 0:1024], reads=[], writes=[bcr])
    P.dma("sp", Cim[:].rearrange("p a b -> p (a b)"), BP_d[:, 1024:2048], reads=[], writes=[bci])
    fr = bcast_last(FP[:, 0, :], 32)
    fi = bcast_last(FP[:, 1, :], 32)
    bf = B("FP")
    P.op(eng, lambda e: e.tensor_tensor(out=t1[:], in0=Cre[:], in1=fr, op=ALU.mult), reads=[bcr, bf], writes=[bt1])
    P.op(eng, lambda e: e.tensor_tensor(out=t2[:], in0=Cim[:], in1=fi, op=ALU.mult), reads=[bci, bf], writes=[bt2])
    P.op(eng, lambda e: e.tensor_tensor(out=BbP[:, 0, :, :], in0=t1[:], in1=t2[:], op=ALU.subtract), reads=[bt1, bt2], writes=[bb])
    P.op(eng, lambda e: e.tensor_tensor(out=t1[:], in0=Cre[:], in1=fi, op=ALU.mult), reads=[bcr, bf], writes=[bt1])
    P.op(eng, lambda e: e.tensor_tensor(out=t2[:], in0=Cim[:], in1=fr, op=ALU.mult), reads=[bci, bf], writes=[bt2])
    P.op(eng, lambda e: e.tensor_tensor(out=BbP[:, 1, :, :], in0=t1[:], in1=t2[:], op=ALU.add), reads=[bt1, bt2], writes=[bb])
    bk_ = B("KTi")
    P.op("pool", lambda e: e.memset(KTi[:].rearrange("p a b c -> p (a b c)"), 0.0), writes=[bk_])
    for pair in range(32):
        t, q = pair // 4, pair % 4
        bk = bank()
        for part in range(2):
            P.op("pe", lambda e, pair=pair, part=part, q=q, bk=bk: e.matmul(
                ps[32 * q:32 * q + 32, bk, 0:256], lhsT=BbP[:, part, pair, :], rhs=LZ[:, part, 0:8, pair, :],
                start=(part == 0), stop=(part == 1), tile_position=(0, 32 * q)), reads=[bb, bl], writes=[B("ps", bk)])
        P.op("dve", lambda e, t=t, q=q, bk=bk: e.tensor_copy(
            out=KTi[32 * q:32 * q + 32, t, :, 32 * q:32 * q + 32],
            in_=ps[32 * q:32 * q + 32, bk, 0:256].rearrange("p (a b) -> p a b", a=8)), reads=[B("ps", bk)], writes=[bk_])
    return 16384


def build_program(debug=False, stop_after=None):
    nc = bass.Bass("TRN2", target_bir_lowering=False)
    P = Prog(nc)
    A = Arena(nc)
    B = BufMap()

    def din(name, shape, dt=F32):
        return nc.dram_tensor(name, list(shape), dt, kind="ExternalInput").ap()

    skind = "ExternalOutput" if debug else "Internal"

    def dscr(name, shape, dt):
        return nc.dram_tensor(name, list(shape), dt, kind=skind).ap()

    xpre = din("xpre", [NPRE, D])
    xown = din("xown", [NO, D])
    w_in = din("w_in", [D, 4096])
    w_glu = din("w_glu", [1024, 1024])
    w_out = din("w_out", [D, D])
    w_up = din("w_up", [D, 2 * DFF])
    w_down = din("w_down", [DFF, D])
    vecs_d = din("vecs", [128, NV])
    gfin_d = din("gfin", [128, D])
    cbf_d = din("cbf", [128, 512], BF16)
    maskd_d = din("maskd", [128, 128])
    ssmP_d = din("ssmP", [128, 96])
    ssmX_d = din("ssmX", [128, 3072])
    BX_d = din("BX", [128, 2048])
    BP_d = din("BP", [128, 2048])
    CZ_d = din("CZ", [128, 2048])
    out_d = nc.dram_tensor("out", [NOWN, D], F32, kind="ExternalOutput").ap()

    KTd = dscr("KTd", [8, 128, NK], BF16)
    Vd = dscr("Vd", [NK, 1024], BF16)
    QTd = dscr("QTd", [8, 128, NO], BF16)
    uTd = dscr("uTd", [128, 8 * 8 * CH_OWN], BF16)
    yad = dscr("yad", [8, 128, NO], BF16)
    ysd = dscr("ysd", [8, 128, NO], BF16)
    h1d = dscr("h1d", [NO, D], F32)

    vecs = A.right("vecs", [128, NV], F32)
    cbf = A.right("cbf", [128, 512], BF16)
    maskd = A.right("maskd", [128, 128], F32)
    gamB = A.right("gamB", [128, 16, 128], BF16)
    ident = cbf[:, 0:128]
    tri = cbf[:, 128:256]
    ones = cbf[:, 256:384]
    zer = cbf[:, 384:512]

    def vcol(c, n=128):
        return vecs[0:n, c:c + 1]

    psT = nc.alloc_psum_tensor("psT", [128, 16, 128], BF16)
    ps = nc.alloc_psum_tensor("ps", [128, 6, 512], F32)
    psctr = [0]

    def bank():
        b = psctr[0] % 6
        psctr[0] += 1
        return b

    def finish_prog():
        P.finish()
        with ExitStack() as st_:
            P.replay(st_)
        nc._prog = P
        return nc

    P.dma("sp", vecs[:], vecs_d[:, :], writes=[B("vecs")])
    P.dma("sp", cbf[:], cbf_d[:, :], writes=[B("cbf")])
    P.dma("sp", maskd[:], maskd_d[:, :], writes=[B("maskd")])

    def set_gam(col):
        P.op("dve", lambda e: e.tensor_copy(out=gamB[:], in_=bcast_last(vecs[:, col:col + 16], 128)),
             reads=[B("vecs")], writes=[B("gamB")])

    RM = A.hi

    class NormCtx:
        pass

    def make_norm_ctx():
        c = NormCtx()
        c.xt = [A.left("xt", [128, D], F32)] * 2
        c.xn = [A.left("xn", [128, D], BF16) for _ in range(2)]
        c.st = A.left("nst", [128, 8], F32)
        c.i = 0
        return c

    def norm_transpose(nctx, src_ap, nrows, hnT, hcol0, hbuf, from_dram=True, keep=None):
        i = nctx.i % 2
        nctx.i += 1
        xn = nctx.xn[i]
        st = nctx.st
        if from_dram:
            xt = nctx.xt[i]
            bx = B("xt", id(nctx))
            P.dma("sp", xt[0:nrows, :], src_ap, writes=[bx])
        else:
            xt, bx = keep
        bxn = B("xn", id(nctx), i)
        bst = B("nst", id(nctx), i)
        c0 = 4 * i
        P.op("dve", lambda e: e.memset(st[0:nrows, c0:c0 + 1], 0.0), writes=[bst])
        P.op("act", lambda e: e.activation(out=xn[0:nrows, :], in_=xt[0:nrows, :], func=AF.Square,
                                           accum_out=st[0:nrows, c0:c0 + 1]), reads=[bx], writes=[bxn, bst])
        P.op("act", lambda e: e.activation(out=st[0:nrows, c0 + 1:c0 + 2], in_=st[0:nrows, c0:c0 + 1], func=AF.Ln,
                                           scale=1.0 / D, bias=vcol(V_EPS, nrows)), reads=[bst, B("vecs")], writes=[bst])
        P.op("act", lambda e: e.activation(out=st[0:nrows, c0 + 2:c0 + 3], in_=st[0:nrows, c0 + 1:c0 + 2], func=AF.Exp,
                                           scale=-0.5), reads=[bst], writes=[bst])
        P.op("dve", lambda e: e.tensor_scalar(out=xn[0:nrows, :], in0=xt[0:nrows, :], scalar1=st[0:nrows, c0 + 2:c0 + 3],
                                              scalar2=None, op0=ALU.mult), reads=[bx, bst], writes=[bxn])
        for k in range(16):
            P.op("pe", lambda e, k=k: e.transpose(out=psT[:, k, 0:nrows], in_=xn[0:nrows, k * 128:(k + 1) * 128],
                                                  identity=ident[0:nrows, 0:nrows]),
                 reads=[bxn, B("cbf")], writes=[B("psT")])
        P.op("dve", lambda e: e.tensor_tensor(out=hnT[:, :, hcol0:hcol0 + nrows], in0=psT[:, :, 0:nrows],
                                              in1=gamB[:, :, 0:nrows], op=ALU.mult),
             reads=[B("psT"), B("gamB")], writes=[hbuf])

    class Ring:
        def __init__(self, name, shape, n):
            self.t = [A.left(name, shape, BF16) for _ in range(n)]
            self.n = n
            self.name = name
            self.i = 0

        def next(self):
            i = self.i % self.n
            self.i += 1
            return self.t[i], B(self.name, id(self), i)

    w_in_v = w_in.rearrange("(k p) c -> p k c", p=128)

    LM = A.lo
    nctx = make_norm_ctx()
    hnT = [A.left("hnT", [128, 16, 528], BF16) for _ in range(2)]
    wring = Ring("win", [128, 16, 256], 3)
    kst = [A.left("kst", [128, 528], BF16) for _ in range(2)]
    vst = [A.left("vst", [128, 256], BF16) for _ in range(3)]
    vctr = [0]
    uT_own = A.left("uTown", [128, 8, 8, CH_OWN], BF16)
    uT_pre = uT_own
    LG = A.left("LG", [128, 2, 8, 8, 128], BF16)
    R1M = A.lo
    kctr = [0]

    set_gam(V_GMIX)

    C = Ctx()
    C.P, C.A, C.B, C.nc, C.ps, C.bank, C.vecs, C.psT = P, A, B, nc, ps, bank, vecs, psT
    GS = A.right("GS", [128, 2, 32, CH_OWN + 1], BF16)
    S2 = [A.right("S2", [128, 2, 32], F32) for _ in range(2)]
    M1 = A.right("M1", [128, 2, 32], F32)
    M2 = A.right("M2", [128, 2, 32], F32)
    scT1 = A.right("scT1", [128, 2, 32], F32)
    scT2 = A.right("scT2", [128, 2, 32], F32)
    Hpf = A.right("Hpf", [128, 2, 32], F32)
    EP = A.right("EP", [128, 2, 9, 32], F32)
    FP = A.right("FP", [128, 2, 32], F32)
    dX_bufs, dX_used = derive_X(C, LG, ssmX_d, BX_d, R1M)
    uTm = [nc.alloc_sbuf_tensor_at("uTm%d" % q, [128, 8, CH_OWN], BF16, offset=R1M + q * 8 * CH_OWN * 2 + (q * 64)) for q in range(4)]
    r1_size = max(dX_used, 4 * (8 * CH_OWN * 2 + 64))
    A.lo = R1M + r1_size
    assert A.lo <= A.hi, ("R1 overflow", A.lo, A.hi)
    derive_P_small(C, ssmP_d, EP, FP, M1, M2)
    scan_gen = [None]

    def pump(nst):
        g = scan_gen[0]
        if g is None:
            return
        for _ in range(nst):
            try:
                next(g)
            except StopIteration:
                scan_gen[0] = None
                return

    def after_prefix():
        for q in range(4):
            P.op("act", lambda e, q=q: e.activation(out=uTm[q][:, 0, 0:1], in_=vecs[:, V_ZERO:V_ZERO + 1], func=AF.Copy),
                 reads=dX_bufs + [B("vecs")], writes=[B("uTm", q)])
        P.op("pool", lambda e: e.memset(GS[:, :, :, 0:1], 0.0), writes=[B("GSall")])
        P.op("pool", lambda e: e.memset(S2[0][:], 0.0), writes=[B("S2", 0)])
        g_matmuls(C, uT_pre, CH_PRE, GS, LG, uTm)
        scan_gen[0] = scan_steps(C, GS, S2, M1, M2, scT1, scT2, CH_PRE, False)

    def group_subtiles(kind, gi):
        subs = []
        if kind == "pre":
            for s in range(4):
                r0 = 512 * gi + 128 * s
                subs.append((xpre[r0:r0 + 128, :], 128, 16 + 128 * s, r0))
        else:
            if gi == 0:
                subs.append((xown[0:16, :], 16, 0, NPRE))
            for s in range(4):
                r0 = 16 + 512 * gi + 128 * s
                subs.append((xown[r0:r0 + 128, :], 128, 16 + 128 * s, NPRE + r0))
        return subs

    def emit_norm(kind, gi, hi):
        for (src, nrows, hcol0, _) in group_subtiles(kind, gi):
            norm_transpose(nctx, src, nrows, hnT[hi], hcol0, B("hnT", hi))

    groups = [("pre", g) for g in range(4)] + [("own", g) for g in range(4)]
    wsched = []
    for gidx, (kind, gi) in enumerate(groups):
        blocks = [0, 1, 2, 3, 8, 9, 10, 11, 12, 13, 14, 15] if kind == "pre" else list(range(16))
        for cb in blocks:
            wsched.append((gidx, cb))
    wtiles = {}
    wnext = [0]

    def issue_w():
        if wnext[0] >= len(wsched):
            return
        gidx, cb = wsched[wnext[0]]
        t, b = wring.next()
        P.dma("pool", t[:, :, :], w_in_v[:, :, cb * 256:(cb + 1) * 256], writes=[b])
        wtiles[wnext[0]] = (t, b)
        wnext[0] += 1

    for _ in range(3):
        issue_w()
    emit_norm(*groups[0], 0)
    wi = 0
    for gidx, (kind, gi) in enumerate(groups):
        hi = gidx % 2
        hT = hnT[hi]
        hb = B("hnT", hi)
        if gidx + 1 < len(groups):
            emit_norm(*groups[gidx + 1], (gidx + 1) % 2)
        subs = group_subtiles(kind, gi)
        lead = (kind == "own" and gi == 0)
        ranges = ([(0, 16)] if lead else []) + [(16, 512)]
        blocks = [0, 1, 2, 3, 8, 9, 10, 11, 12, 13, 14, 15] if kind == "pre" else list(range(16))
        uT = uT_pre if kind == "pre" else uT_own
        ub = B("uT")
        for cb in blocks:
            wt, wb = wtiles.pop(wi)
            wi += 1
            if cb < 12:
                for m in range(2):
                    oc = cb * 2 + m
                    for (c0, n) in ranges:
                        bk = bank()
                        for k in range(16):
                            P.op("pe", lambda e, k=k, m=m, c0=c0, n=n, bk=bk, wt=wt, hT=hT: e.matmul(
                                ps[:, bk, 0:n], lhsT=wt[:, k, m * 128:(m + 1) * 128], rhs=hT[:, k, c0:c0 + n],
                                start=(k == 0), stop=(k == 15)), reads=[wb, hb], writes=[B("ps", bk)])
                        if oc < 8:
                            if kind == "pre":
                                ch0 = 64 * gi
                            else:
                                ch0 = 0 if c0 == 0 else 2 + 64 * gi
                            nch = n // 8
                            P.op("act", lambda e, bk=bk, n=n, oc=oc, ch0=ch0, nch=nch, uT=uT: e.activation(
                                out=uT[:, oc, :, ch0:ch0 + nch], in_=ps[:, bk, 0:n].rearrange("p (c j) -> p j c", j=8),
                                func=AF.Copy), reads=[B("ps", bk)], writes=[ub])
                        else:
                            si = kctr[0] % 2
                            kctr[0] += 1
                            ks = kst[si]
                            P.op("act", lambda e, bk=bk, n=n, ks=ks: e.activation(out=ks[:, 0:n], in_=ps[:, bk, 0:n],
                                                                                 func=AF.Copy),
                                 reads=[B("ps", bk)], writes=[B("kst", si)])
                            if oc < 16:
                                hp = oc - 8
                                qc0 = 0 if c0 == 0 else 16 + 512 * gi
                                P.dma("sp", QTd[hp, :, qc0:qc0 + n], ks[:, 0:n], reads=[B("kst", si)],
                                      writes=[B("QTd", hp, gi, c0)])
                            else:
                                hp = oc - 16
                                if kind == "pre":
                                    kc0 = 512 * gi
                                else:
                                    kc0 = NPRE if c0 == 0 else NPRE + 16 + 512 * gi
                                P.dma("sp", KTd[hp, :, kc0:kc0 + n], ks[:, 0:n], reads=[B("kst", si)],
                                      writes=[B("KTd", hp, kind, gi, c0)])
            else:
                half = cb - 12
                for si, (_, nrows, hcol0, krow0) in enumerate(subs):
                    bk = bank()
                    vi = si if not lead else si
                    for k in range(16):
                        P.op("pe", lambda e, k=k, bk=bk, nrows=nrows, hcol0=hcol0, wt=wt, hT=hT: e.matmul(
                            ps[0:nrows, bk, 0:256], lhsT=hT[:, k, hcol0:hcol0 + nrows], rhs=wt[:, k, :],
                            start=(k == 0), stop=(k == 15)), reads=[wb, hb], writes=[B("ps", bk)])
                    vi = vctr[0] % 3
                    vctr[0] += 1
                    vs = vst[vi]
                    P.op("act", lambda e, bk=bk, nrows=nrows, vs=vs: e.activation(
                        out=vs[0:nrows, :], in_=ps[0:nrows, bk, 0:256], func=AF.Copy),
                        reads=[B("ps", bk)], writes=[B("vst", vi)])
                    P.dma("sp", Vd[krow0:krow0 + nrows, half * 256:(half + 1) * 256], vs[0:nrows, :], reads=[B("vst", vi)],
                          writes=[B("Vd", krow0, half)])
            issue_w()
            if gidx >= 4:
                pump(4)
        if gidx == 3:
            after_prefix()
    if stop_after == "inproj":
        P.dma("sp", uTd[:, :], uT_own[:].rearrange("p a b c -> p (a b c)"), reads=[B("uT", "own")], writes=[B("uTd")])
        return finish_prog()

    pump(10 ** 9)
    P.op("act", lambda e: e.activation(out=Hpf[:], in_=S2[CH_PRE % 2][:], func=AF.Copy, scale=vecs[:, V_PSCALE:V_PSCALE + 1]),
         reads=[B("S2", CH_PRE % 2), B("vecs")], writes=[B("Hpf")])
    g_matmuls(C, uT_own, CH_OWN, GS, LG, uTm)
    P.op("act", lambda e: e.activation(out=GS[:, :, :, 0], in_=Hpf[:], func=AF.Copy), reads=[B("Hpf")], writes=[B("GSall")])
    P.op("act", lambda e: e.activation(out=S2[0][:], in_=Hpf[:], func=AF.Copy), reads=[B("Hpf")], writes=[B("S2", 0)])
    P.dma("sp", uTd[:, :], uT_own[:].rearrange("p a b c -> p (a b c)"), reads=[B("uT")], writes=[B("uTd")])
    scan_gen[0] = scan_steps(C, GS, S2, M1, M2, scT1, scT2, CH_OWN, True)
    if stop_after == "gown":
        pump(10 ** 9)
        return finish_prog()

    P.barrier()
    A.lo = LM
    LZ = A.right("LZ", [128, 2, 9, 32, 32], BF16)
    KTi = A.right("KTi", [128, 8, 8, 128], BF16)
    BbP = A.right("BbP", [128, 2, 32, 32], BF16)
    tmpz = A.left("tmpz", [128, 4096], F32)
    derive_LZ_K(C, EP, FP, BP_d, CZ_d, LZ, KTi, BbP, LM)
    KTb = [A.left("KTb", [128, NK], BF16) for _ in range(2)]
    Vb = [A.left("Vb", [128, 33, 128], BF16) for _ in range(2)]
    QTb = [A.left("QTb", [128, NO], BF16) for _ in range(2)]
    pbuf = [[A.left("pb", [128, 512], F32) for _ in range(3)] for _ in range(2)]
    spb = [[A.left("spb", [128, 512], BF16) for _ in range(2)] for _ in range(2)]
    Rb = [A.left("Rb", [128, 512], BF16) for _ in range(2)]
    eb = [[A.left("eb", [128, 512], F32) for _ in range(2)] for _ in range(2)]
    wb_ = [[A.left("wb", [128, 512], BF16) for _ in range(2)] for _ in range(2)]
    yst = [A.left("yst", [128, 512], BF16) for _ in range(2)]
    SCALE = 0.125

    def load_hp(hp, i):
        P.dma("sp", KTb[i][:, :], KTd[hp, :, :], reads=[B("KTd", hp, k_, g_, c_) for (k_, g_, c_) in ktd_keys], writes=[B("KTb", i)])
        P.dma("sp", Vb[i][:, 0:16, :], Vd[0:NPRE, hp * 128:(hp + 1) * 128].rearrange("(b p) c -> p b c", p=128),
              reads=vd_bufs, writes=[B("Vb", i)])
        P.dma("sp", Vb[i][0:16, 16, :], Vd[NPRE:NPRE + 16, hp * 128:(hp + 1) * 128], reads=vd_bufs, writes=[B("Vb", i)])
        P.dma("sp", Vb[i][:, 17:33, :], Vd[NPRE + 16:NK, hp * 128:(hp + 1) * 128].rearrange("(b p) c -> p b c", p=128),
              reads=vd_bufs, writes=[B("Vb", i)])
        P.dma("sp", QTb[i][:, :], QTd[hp, :, :], reads=[B("QTd", hp, g_, c_) for (g_, c_) in qtd_keys], writes=[B("QTb", i)])

    ktd_keys = [("pre", g, 16) for g in range(4)] + [("own", g, 16) for g in range(4)] + [("own", 0, 0)]
    qtd_keys = [(g, 16) for g in range(4)] + [(0, 0)]
    vd_bufs = [b for k, b in B.d.items() if k[0] == "Vd"]

    def tiles_for_group(g):
        tl = []
        if g < 0:
            tl.append((NPRE, 16, 16, 0, (0, 16), False))
        else:
            for i in (3, 2, 1, 0):
                ob = 4 * g + i
                tl.append((NPRE + 16 + 128 * ob, 128, 17 + ob, 128 * i, (128 * i, 128 * i + 128), False))
            for ob in range(4 * g - 1, -1, -1):
                tl.append((NPRE + 16 + 128 * ob, 128, 17 + ob, 0, None, False))
            tl.append((NPRE, 16, 16, 0, None, False))
        for pb in range(15, -1, -1):
            tl.append((128 * pb, 128, pb, 0, None, True))
        return tl

    qgroups = [(-1, 0, 16)] + [(g, 16 + 512 * g, 512) for g in range(4)]
    ogrp = [0]

    def attn_group(hp, bi, g, qc0, ncols):
        tl = tiles_for_group(g)
        nt = len(tl)
        ob = 4 + (ogrp[0] % 2)
        ogrp[0] += 1
        kt, vb, qt = KTb[bi], Vb[bi], QTb[bi]
        bkt, bvb, bqt = B("KTb", bi), B("Vb", bi), B("QTb", bi)
        for e_ in range(2):
            P.op("pool", lambda e, e_=e_: e.memset(Rb[e_][:, 0:ncols], 0.0), writes=[B("Rb", e_)])
            P.op("pe", lambda e, e_=e_: e.matmul(ps[64 * e_:64 * e_ + 64, ob, 0:ncols], lhsT=zer[:, 0:64], rhs=kt[:, 0:ncols],
                                                   start=True, stop=False, skip_group_check=True, tile_position=(0, 64 * e_)),
                 reads=[B("cbf"), bkt], writes=[B("ps", ob)])

        def stageA(ti):
            kc0, nk, vs, c0, mk, ispre = tl[ti]
            for e_ in range(2):
                pe_ = 64 * e_
                pbt = pbuf[e_][ti % 3]
                bpb = B("pb", e_, ti % 3)
                sp_ = spb[e_][ti % 2]
                bsp = B("spb", e_, ti % 2)
                P.op("pe", lambda e, e_=e_, pe_=pe_, kc0=kc0, nk=nk, c0=c0: e.matmul(
                    ps[0:nk, e_, c0:ncols], lhsT=kt[pe_:pe_ + 64, kc0:kc0 + nk], rhs=qt[pe_:pe_ + 64, qc0 + c0:qc0 + ncols],
                    start=True, stop=True), reads=[bkt, bqt], writes=[B("ps", e_)])
                bcol = V_PBIAS if ispre else V_ZERO
                P.op("act", lambda e, e_=e_, nk=nk, c0=c0, pbt=pbt, bcol=bcol: e.activation(
                    out=pbt[0:nk, c0:ncols], in_=ps[0:nk, e_, c0:ncols], func=AF.Exp, scale=SCALE, bias=vecs[0:nk, bcol:bcol + 1]),
                    reads=[B("ps", e_), B("vecs")], writes=[bpb])
                if mk is not None:
                    m0, m1 = mk
                    P.op("dve", lambda e, nk=nk, m0=m0, m1=m1, pbt=pbt: e.tensor_tensor(
                        out=pbt[0:nk, m0:m1], in0=pbt[0:nk, m0:m1], in1=maskd[0:nk, 0:m1 - m0], op=ALU.mult),
                        reads=[bpb, B("maskd")], writes=[bpb])
                P.op("act", lambda e, nk=nk, c0=c0, pbt=pbt, sp_=sp_: e.activation(
                    out=sp_[0:nk, c0:ncols], in_=pbt[0:nk, c0:ncols], func=AF.Ln, scale=1.0, bias=vecs[0:nk, V_ONE:V_ONE + 1]),
                    reads=[bpb, B("vecs")], writes=[bsp])

        def stageB(ti):
            kc0, nk, vs, c0, mk, ispre = tl[ti]
            for e_ in range(2):
                pbt = pbuf[e_][ti % 3]
                bpb = B("pb", e_, ti % 3)
                sp_ = spb[e_][ti % 2]
                bsp = B("spb", e_, ti % 2)
                ib = 2 + e_
                P.op("pe", lambda e, e_=e_, nk=nk, c0=c0, ib=ib: e.matmul(
                    ps[0:nk, ib, c0:ncols], lhsT=ones[:, 0:nk], rhs=Rb[e_][:, c0:ncols], start=True, stop=False),
                    reads=[B("cbf"), B("Rb", e_)], writes=[B("ps", ib)])
                P.op("pe", lambda e, nk=nk, c0=c0, ib=ib, sp_=sp_: e.matmul(
                    ps[0:nk, ib, c0:ncols], lhsT=tri[0:nk, 0:nk], rhs=sp_[0:nk, c0:ncols], start=False, stop=True),
                    reads=[B("cbf"), bsp], writes=[B("ps", ib)])
                if ti < nt - 1:
                    P.op("pool", lambda e, e_=e_, nk=nk, c0=c0, sp_=sp_: e.tensor_tensor(
                        out=Rb[e_][0:nk, c0:ncols], in0=Rb[e_][0:nk, c0:ncols], in1=sp_[0:nk, c0:ncols], op=ALU.add),
                        reads=[B("Rb", e_), bsp], writes=[B("Rb", e_)])
                ebt = eb[e_][ti % 2]
                beb = B("eb", e_, ti % 2)
                P.op("act", lambda e, nk=nk, c0=c0, ib=ib, ebt=ebt: e.activation(
                    out=ebt[0:nk, c0:ncols], in_=ps[0:nk, ib, c0:ncols], func=AF.Exp, scale=-1.0),
                    reads=[B("ps", ib)], writes=[beb])
                wbt = wb_[e_][ti % 2]
                bwb = B("wb", e_, ti % 2)
                P.op("dve", lambda e, nk=nk, c0=c0, pbt=pbt, ebt=ebt, wbt=wbt: e.tensor_tensor(
                    out=wbt[0:nk, c0:ncols], in0=pbt[0:nk, c0:ncols], in1=ebt[0:nk, c0:ncols], op=ALU.mult),
                    reads=[bpb, beb], writes=[bwb])

        def stageC(ti):
            kc0, nk, vs, c0, mk, ispre = tl[ti]
            for e_ in range(2):
                wbt = wb_[e_][ti % 2]
                bwb = B("wb", e_, ti % 2)
                P.op("pe", lambda e, e_=e_, nk=nk, vs=vs, c0=c0, wbt=wbt: e.matmul(
                    ps[64 * e_:64 * e_ + 64, ob, c0:ncols], lhsT=vb[0:nk, vs, 64 * e_:64 * e_ + 64], rhs=wbt[0:nk, c0:ncols],
                    start=False, stop=(ti == nt - 1), skip_group_check=True, tile_position=(0, 64 * e_)),
                    reads=[bvb, bwb], writes=[B("ps", ob)])

        for it in range(nt + 2):
            if it < nt:
                stageA(it)
            if 0 <= it - 1 < nt:
                stageB(it - 1)
            if 0 <= it - 2 < nt:
                stageC(it - 2)
            pump(1)
        ys = yst[ogrp[0] % 2]
        bys = B("yst", ogrp[0] % 2)
        P.op("dve", lambda e: e.tensor_copy(out=ys[:, 0:ncols], in_=ps[:, ob, 0:ncols]), reads=[B("ps", ob)], writes=[bys])
        P.dma("sp", yad[hp, :, qc0:qc0 + ncols], ys[:, 0:ncols], reads=[bys], writes=[B("yad", hp, g)])

    load_hp(0, 0)
    for hp in range(8):
        if hp + 1 < 8:
            load_hp(hp + 1, (hp + 1) % 2)
        for (g, qc0, ncols) in qgroups:
            attn_group(hp, hp % 2, g, qc0, ncols)
    pump(10 ** 9)
    P.op("act", lambda e: e.activation(out=scT2[:, 0, 0:1], in_=vecs[:, V_ZERO:V_ZERO + 1], func=AF.Copy),
         reads=[B("GSh", s_) for s_ in range(1, CH_OWN + 1)] + [B("GSall"), B("vecs")], writes=[B("GShist"), B("scT2")])
    if stop_after == "attn":
        return finish_prog()

    P.barrier()
    A.lo = LM
    uTr = A.left("uTr", [128, 8, 8, CH_OWN], BF16)
    ygT = A.left("ygT", [128, 8, NO], BF16)
    wglu = A.left("wglu", [128, 8, 1024], BF16)
    yr = [A.left("yr", [128, CH_OWN], F32) for _ in range(2)]
    x2 = [A.left("x2", [128, CH_OWN], F32) for _ in range(2)]
    sgb = [A.left("sgb", [128, 512], F32) for _ in range(2)]
    ysb = [A.left("ysb", [128, 512], BF16) for _ in range(2)]
    P.dma("sp", uTr[:].rearrange("p a b c -> p (a b c)"), uTd[:, :], reads=[B("uTd")], writes=[B("uTr")])
    P.dma("pool", wglu[:, :, :], w_glu.rearrange("(k p) c -> p k c", p=128), writes=[B("wglu")])
    NC_ = CH_OWN
    GC1, GC2 = 1.5957691216057308, 0.044715
    cnt = 0
    for t in range(8):
        for j in range(8):
            bk = bank()
            nmm = (j + 1) + 8
            im = 0
            for ji in range(j + 1):
                P.op("pe", lambda e, t=t, j=j, ji=ji, bk=bk, im=im: e.matmul(
                    ps[:, bk, 0:NC_], lhsT=KTi[:, t, j - ji, :], rhs=uTr[:, t, ji, :], start=(im == 0), stop=False, skip_group_check=True),
                    reads=[B("KTi"), B("uTr")], writes=[B("ps", bk)])
                im += 1
            for q in range(4):
                pair = 4 * t + q
                for part in range(2):
                    last = (q == 3 and part == 1)
                    P.op("pe", lambda e, t=t, j=j, q=q, part=part, pair=pair, bk=bk, last=last: e.matmul(
                        ps[32 * q:32 * q + 32, bk, 0:NC_], lhsT=LZ[:, part, j + 1, pair, :], rhs=GS[:, part, pair, 0:NC_],
                        start=False, stop=last, skip_group_check=True, tile_position=(0, 32 * q)),
                        reads=[B("LZ"), B("GShist")], writes=[B("ps", bk)])
            i2 = cnt % 2
            cnt += 1
            yrt, x2t = yr[i2], x2[i2]
            byr, bx2 = B("yr", i2), B("x2", i2)
            P.op("dve", lambda e, t=t, j=j, bk=bk, yrt=yrt: e.scalar_tensor_tensor(
                out=yrt[:, :], in0=uTr[:, t, j, :], scalar=vecs[:, V_DSK + t:V_DSK + t + 1], in1=ps[:, bk, 0:NC_],
                op0=ALU.mult, op1=ALU.add), reads=[B("uTr"), B("vecs"), B("ps", bk)], writes=[byr])
            P.op("pool", lambda e, yrt=yrt, x2t=x2t: e.tensor_tensor(out=x2t[:, :], in0=yrt[:, :], in1=yrt[:, :], op=ALU.mult),
                 reads=[byr], writes=[bx2])
            P.op("pool", lambda e, x2t=x2t: e.tensor_scalar(out=x2t[:, :], in0=x2t[:, :], scalar1=GC2, scalar2=1.0, op0=ALU.mult, op1=ALU.add),
                 reads=[bx2], writes=[bx2])
            P.op("pool", lambda e, yrt=yrt, x2t=x2t: e.tensor_tensor(out=x2t[:, :], in0=x2t[:, :], in1=yrt[:, :], op=ALU.mult),
                 reads=[bx2, byr], writes=[bx2])
            P.op("act", lambda e, x2t=x2t: e.activation(out=x2t[:, :], in_=x2t[:, :], func=AF.Sigmoid, scale=GC1), reads=[bx2], writes=[bx2])
            P.op("dve", lambda e, t=t, j=j, yrt=yrt, x2t=x2t: e.tensor_tensor(
                out=ygT[:, t, :].rearrange("p (c j) -> p j c", j=8)[:, j, :], in0=yrt[:, :], in1=x2t[:, :], op=ALU.mult),
                reads=[byr, bx2], writes=[B("ygT")])
    colblocks = [(512 * i, 512) for i in range(4)] + [(2048, 16)]
    cnt = 0
    for (cb0, n) in colblocks:
        for m in range(8):
            bk = bank()
            for k in range(8):
                P.op("pe", lambda e, k=k, m=m, bk=bk, cb0=cb0, n=n: e.matmul(
                    ps[:, bk, 0:n], lhsT=wglu[:, k, m * 128:(m + 1) * 128], rhs=ygT[:, k, cb0:cb0 + n], start=(k == 0), stop=(k == 7)),
                    reads=[B("wglu"), B("ygT")], writes=[B("ps", bk)])
            i2 = cnt % 2
            cnt += 1
            sg, yb = sgb[i2], ysb[i2]
            P.op("act", lambda e, bk=bk, n=n, m=m, sg=sg: e.activation(out=sg[:, 0:n], in_=ps[:, bk, 0:n], func=AF.Sigmoid,
                                                                     bias=vecs[:, V_BGLU + m:V_BGLU + m + 1], scale=1.0),
                 reads=[B("ps", bk), B("vecs")], writes=[B("sgb", i2)])
            P.op("dve", lambda e, n=n, m=m, cb0=cb0, sg=sg, yb=yb: e.tensor_tensor(out=yb[:, 0:n], in0=ygT[:, m, cb0:cb0 + n], in1=sg[:, 0:n], op=ALU.mult),
                 reads=[B("ygT"), B("sgb", i2)], writes=[B("ysb", i2)])
            P.dma("sp", ysd[m, :, cb0:cb0 + n], yb[:, 0:n], reads=[B("ysb", i2)], writes=[B("ysd", m, cb0)])
    if stop_after == "ssm":
        return finish_prog()

    P.barrier()
    A.lo = LM
    A.hi = RM
    wout = A.left("wout", [128, 16, D], BF16)
    ytile = [A.left("ytile", [128, 16, 128], BF16) for _ in range(2)]
    sqt = [A.left("sqt", [128, 16, 128], BF16) for _ in range(2)]
    xres = [A.left("xres", [128, D], F32) for _ in range(2)]
    h1t = [A.left("h1t", [128, D], F32) for _ in range(2)]
    mst = A.left("mst", [128, 16], F32)
    w_out_v = w_out.rearrange("(k p) c -> p k c", p=128)
    for hh in range(4):
        P.dma("pool", wout[:, 4 * hh:4 * hh + 4, :], w_out_v[:, 4 * hh:4 * hh + 4, :], writes=[B("wout", hh)])
    for k in range(16):
        gc = (V_GSSM + k) if k < 8 else (V_GATT + k - 8)
        P.op("act", lambda e, k=k, gc=gc: e.activation(out=wout[:, k, :], in_=wout[:, k, :], func=AF.Copy, scale=vecs[:, gc:gc + 1]),
             reads=[B("wout", k // 4), B("vecs")], writes=[B("wout", k // 4)])
    ysd_bufs = [b for kk, b in B.d.items() if kk[0] == "ysd"]
    yad_bufs = [b for kk, b in B.d.items() if kk[0] == "yad"]
    wout_bufs = [B("wout", hh) for hh in range(4)]
    tiles = [(0, 16)] + [(16 + 128 * i, 128) for i in range(16)]
    for ti, (col0, nt) in enumerate(tiles):
        i2 = ti % 2
        yt, sq, xr, h1 = ytile[i2], sqt[i2], xres[i2], h1t[i2]
        byt, bsq, bxr, bh1 = B("ytile", i2), B("sqt", i2), B("xres", i2), B("h1t", i2)
        P.dma("sp", yt[:, 0:8, 0:nt], ysd[:, :, col0:col0 + nt].rearrange("h p c -> p h c"), reads=ysd_bufs, writes=[byt])
        P.dma("sp", yt[:, 8:16, 0:nt], yad[:, :, col0:col0 + nt].rearrange("h p c -> p h c"), reads=yad_bufs, writes=[byt])
        P.dma("sp", xr[0:nt, :], xown[col0:col0 + nt, :], writes=[bxr])
        P.op("dve", lambda e, yt=yt, sq=sq, nt=nt: e.tensor_tensor(out=sq[:, :, 0:nt], in0=yt[:, :, 0:nt], in1=yt[:, :, 0:nt], op=ALU.mult),
             reads=[byt], writes=[bsq])
        bk = bank()
        for half in range(2):
            for h in range(8):
                P.op("pe", lambda e, half=half, h=h, bk=bk, sq=sq, nt=nt: e.matmul(
                    ps[0:nt, bk, half:half + 1], lhsT=sq[:, 8 * half + h, 0:nt], rhs=ones[:, 0:1], start=(h == 0), stop=(h == 7),
                    skip_group_check=True), reads=[bsq, B("cbf")], writes=[B("ps", bk)])
        bms = B("mst", i2)
        c0 = 8 * i2
        P.op("act", lambda e, bk=bk, nt=nt, c0=c0: e.activation(out=mst[0:nt, c0:c0 + 2], in_=ps[0:nt, bk, 0:2], func=AF.Ln,
                                                              scale=1.0 / 1024, bias=vecs[0:nt, V_EPS:V_EPS + 1]),
             reads=[B("ps", bk), B("vecs")], writes=[bms])
        P.op("act", lambda e, nt=nt, c0=c0: e.activation(out=mst[0:nt, c0 + 2:c0 + 4], in_=mst[0:nt, c0:c0 + 2], func=AF.Exp, scale=-0.5),
             reads=[bms], writes=[bms])
        for n4 in range(4):
            bs, ba = bank(), bank()
            for half, bkk in ((0, bs), (1, ba)):
                for h in range(8):
                    P.op("pe", lambda e, half=half, h=h, bkk=bkk, n4=n4, yt=yt, nt=nt: e.matmul(
                        ps[0:nt, bkk, :], lhsT=yt[:, 8 * half + h, 0:nt], rhs=wout[:, 8 * half + h, n4 * 512:(n4 + 1) * 512],
                        start=(h == 0), stop=(h == 7)), reads=[byt] + wout_bufs, writes=[B("ps", bkk)])
            P.op("dve", lambda e, bs=bs, n4=n4, nt=nt, c0=c0, xr=xr, h1=h1: e.scalar_tensor_tensor(
                out=h1[0:nt, n4 * 512:(n4 + 1) * 512], in0=ps[0:nt, bs, :], scalar=mst[0:nt, c0 + 2:c0 + 3],
                in1=xr[0:nt, n4 * 512:(n4 + 1) * 512], op0=ALU.mult, op1=ALU.add),
                reads=[B("ps", bs), bms, bxr], writes=[bh1])
            P.op("dve", lambda e, ba=ba, n4=n4, nt=nt, c0=c0, h1=h1: e.scalar_tensor_tensor(
                out=h1[0:nt, n4 * 512:(n4 + 1) * 512], in0=ps[0:nt, ba, :], scalar=mst[0:nt, c0 + 3:c0 + 4],
                in1=h1[0:nt, n4 * 512:(n4 + 1) * 512], op0=ALU.mult, op1=ALU.add),
                reads=[B("ps", ba), bms, bh1], writes=[bh1])
        P.dma("sp", h1d[col0:col0 + nt, :], h1[0:nt, :], reads=[bh1], writes=[B("h1d", ti)])
    if stop_after == "mix":
        return finish_prog()

    P.barrier()
    A.lo = LM
    set_gam(V_GFFN)
    gfin = A.right("gfin", [128, D], F32)
    P.dma("sp", gfin[:], gfin_d[:, :], writes=[B("gfin")])
    nctx2 = NormCtx()
    nctx2.xt = [None, None]
    nctx2.xn = [A.left("xn2", [128, D], BF16) for _ in range(2)]
    nctx2.st = A.left("nst2", [128, 8], F32)
    nctx2.i = 0
    h1g = [A.left("h1g", [128, D], F32) for _ in range(4)]
    h1L = A.left("h1L", [16, D], F32)
    hn2T = A.left("hn2T", [128, 16, 528], BF16)
    actT = A.left("actT", [128, NJ, 512], BF16)
    wu = [A.left("wu", [128, 16, 256], BF16) for _ in range(2)]
    wd = [A.left("wd", [128, 22, 512], BF16) for _ in range(2)]
    rawb = [A.left("rawb", [128, 514], F32) for _ in range(4)]
    cvb = [A.left("cvb", [128, 512], F32) for _ in range(4)]
    carry = A.left("carry", [128, 88, 2], F32)
    fst = A.left("fst", [128, 16], F32)
    w_up_v = w_up.rearrange("(k p) c -> p k c", p=128)
    w_down_v = w_down.rearrange("(j p) c -> p j c", p=128)
    h1d_bufs = [b for kk, b in B.d.items() if kk[0] == "h1d"]

    wu_tiles = {}
    wu_next = [0]
    WU_TOTAL = 4 * NJ

    def issue_wu():
        i = wu_next[0]
        if i >= WU_TOTAL:
            return
        jj = i % NJ
        t_ = wu[i % 2]
        b_ = B("wu", i % 2)
        P.dma("pool", t_[:, :, 0:128], w_up_v[:, :, jj * 128:(jj + 1) * 128], writes=[b_])
        P.dma("pool", t_[:, :, 128:256], w_up_v[:, :, DFF + jj * 128:DFF + (jj + 1) * 128], writes=[b_])
        wu_tiles[i] = (t_, b_)
        wu_next[0] += 1

    wd_tiles = {}
    wd_next = [0]
    WD_TOTAL = 4 * 8

    def issue_wd():
        i = wd_next[0]
        if i >= WD_TOTAL:
            return
        n4, half = (i % 8) // 2, i % 2
        t_ = wd[i % 2]
        b_ = B("wd", i % 2)
        P.dma("pool", t_[:, :, :], w_down_v[:, 22 * half:22 * half + 22, n4 * 512:(n4 + 1) * 512], writes=[b_])
        wd_tiles[i] = (t_, b_)
        wd_next[0] += 1

    issue_wu()
    issue_wu()
    issue_wd()
    issue_wd()
    wui = 0
    wdi = 0
    rctr = [0]
    for gi in range(4):
        if gi == 0:
            P.dma("sp", h1L[0:16, :], h1d[0:16, :], reads=h1d_bufs, writes=[B("h1L")])
            norm_transpose(nctx2, None, 16, hn2T, 0, B("hn2T"), from_dram=False, keep=(h1L, B("h1L")))
        for s in range(4):
            r0 = 16 + 512 * gi + 128 * s
            P.dma("sp", h1g[s][:, :], h1d[r0:r0 + 128, :], reads=h1d_bufs, writes=[B("h1g", s)])
            norm_transpose(nctx2, None, 128, hn2T, 16 + 128 * s, B("hn2T"), from_dram=False, keep=(h1g[s], B("h1g", s)))
        for jj in range(NJ):
            wt, wb2 = wu_tiles.pop(wui)
            wui += 1
            cvs = []
            for gv in range(2):
                ch = gv * NJ + jj
                bk = bank()
                for k in range(16):
                    P.op("pe", lambda e, k=k, gv=gv, bk=bk, wt=wt: e.matmul(
                        ps[:, bk, :], lhsT=wt[:, k, 128 * gv:128 * gv + 128], rhs=hn2T[:, k, 16:528], start=(k == 0), stop=(k == 15)),
                        reads=[wb2, B("hn2T")], writes=[B("ps", bk)])
                ri = rctr[0] % 4
                rctr[0] += 1
                rb, cv = rawb[ri], cvb[ri]
                brb, bcv = B("rawb", ri), B("cvb", ri)
                if gi == 0:
                    bkl = bank()
                    for k in range(16):
                        P.op("pe", lambda e, k=k, gv=gv, bkl=bkl, wt=wt: e.matmul(
                            ps[:, bkl, 0:2], lhsT=wt[:, k, 128 * gv:128 * gv + 128], rhs=hn2T[:, k, 14:16], start=(k == 0), stop=(k == 15)),
                            reads=[wb2, B("hn2T")], writes=[B("ps", bkl)])
                    P.op("act", lambda e, bkl=bkl, rb=rb: e.activation(out=rb[:, 0:2], in_=ps[:, bkl, 0:2], func=AF.Copy),
                         reads=[B("ps", bkl)], writes=[brb])
                else:
                    P.op("pool", lambda e, ch=ch, rb=rb: e.tensor_copy(out=rb[:, 0:2], in_=carry[:, ch, :]), reads=[B("carry", ch)], writes=[brb])
                P.op("act", lambda e, bk=bk, rb=rb: e.activation(out=rb[:, 2:514], in_=ps[:, bk, :], func=AF.Copy),
                     reads=[B("ps", bk)], writes=[brb])
                if gi < 3:
                    P.op("pool", lambda e, ch=ch, rb=rb: e.tensor_copy(out=carry[:, ch, :], in_=rb[:, 512:514]), reads=[brb], writes=[B("carry", ch)])
                w0 = vecs[:, V_CW + 3 * ch + 0:V_CW + 3 * ch + 1]
                w1 = vecs[:, V_CW + 3 * ch + 1:V_CW + 3 * ch + 2]
                w2 = vecs[:, V_CW + 3 * ch + 2:V_CW + 3 * ch + 3]
                cb_ = vecs[:, V_CB + ch:V_CB + ch + 1]
                P.op("dve", lambda e, rb=rb, cv=cv, w2=w2, cb_=cb_: e.tensor_scalar(out=cv[:, :], in0=rb[:, 2:514], scalar1=w2, scalar2=cb_,
                                                                                  op0=ALU.mult, op1=ALU.add),
                     reads=[brb, B("vecs")], writes=[bcv])
                P.op("dve", lambda e, rb=rb, cv=cv, w1=w1: e.scalar_tensor_tensor(out=cv[:, :], in0=rb[:, 1:513], scalar=w1, in1=cv[:, :],
                                                                                op0=ALU.mult, op1=ALU.add),
                     reads=[brb, bcv, B("vecs")], writes=[bcv])
                P.op("dve", lambda e, rb=rb, cv=cv, w0=w0: e.scalar_tensor_tensor(out=cv[:, :], in0=rb[:, 0:512], scalar=w0, in1=cv[:, :],
                                                                                op0=ALU.mult, op1=ALU.add),
                     reads=[brb, bcv, B("vecs")], writes=[bcv])
                cvs.append((cv, bcv, rb, brb))
            (cg, bcg, rbg, brbg), (cvv, bcvv, _, _) = cvs
            P.op("act", lambda e, cg=cg, rbg=rbg: e.activation(out=rbg[:, 0:512], in_=cg[:, :], func=AF.Silu), reads=[bcg], writes=[brbg])
            P.op("dve", lambda e, jj=jj, rbg=rbg, cvv=cvv: e.tensor_tensor(out=actT[:, jj, :], in0=rbg[:, 0:512], in1=cvv[:, :], op=ALU.mult),
                 reads=[brbg, bcvv], writes=[B("actT", jj)])
            issue_wu()
        act_bufs = [B("actT", jj) for jj in range(NJ)]
        for n4 in range(4):
            bks = [bank() for _ in range(4)]
            for half in range(2):
                wt, wb2 = wd_tiles.pop(wdi)
                wdi += 1
                for s in range(4):
                    for j2 in range(22):
                        jj = 22 * half + j2
                        P.op("pe", lambda e, s=s, j2=j2, jj=jj, half=half, wt=wt, bks=bks: e.matmul(
                            ps[:, bks[s], :], lhsT=actT[:, jj, 128 * s:128 * s + 128], rhs=wt[:, j2, :],
                            start=(jj == 0), stop=(jj == NJ - 1), skip_group_check=True),
                            reads=[wb2, B("actT", jj)], writes=[B("ps", bks[s])])
                issue_wd()
            for s in range(4):
                P.op("dve", lambda e, s=s, n4=n4, bks=bks: e.tensor_tensor(
                    out=h1g[s][:, n4 * 512:(n4 + 1) * 512], in0=ps[:, bks[s], :], in1=h1g[s][:, n4 * 512:(n4 + 1) * 512], op=ALU.add),
                    reads=[B("ps", bks[s]), B("h1g", s)], writes=[B("h1g", s)])
        for s in range(4):
            c0 = 4 * s
            bfs = B("fst", s)
            xn_ = nctx2.xn[s % 2]
            bxn_ = B("xn", id(nctx2), s % 2)
            P.op("dve", lambda e, c0=c0: e.memset(fst[:, c0:c0 + 1], 0.0), writes=[bfs])
            P.op("act", lambda e, s=s, c0=c0, xn_=xn_: e.activation(out=xn_[:, :], in_=h1g[s][:, :], func=AF.Square, accum_out=fst[:, c0:c0 + 1]),
                 reads=[B("h1g", s)], writes=[bxn_, bfs])
            P.op("act", lambda e, c0=c0: e.activation(out=fst[:, c0 + 1:c0 + 2], in_=fst[:, c0:c0 + 1], func=AF.Ln, scale=1.0 / D,
                                                     bias=vecs[:, V_EPS:V_EPS + 1]), reads=[bfs, B("vecs")], writes=[bfs])
            P.op("act", lambda e, c0=c0: e.activation(out=fst[:, c0 + 2:c0 + 3], in_=fst[:, c0 + 1:c0 + 2], func=AF.Exp, scale=-0.5),
                 reads=[bfs], writes=[bfs])
            P.op("dve", lambda e, s=s, c0=c0: e.scalar_tensor_tensor(out=h1g[s][:, :], in0=h1g[s][:, :], scalar=fst[:, c0 + 2:c0 + 3], in1=gfin[:, :],
                                                                   op0=ALU.mult, op1=ALU.mult),
                 reads=[B("h1g", s), bfs, B("gfin")], writes=[B("h1g", s)])
            r0 = 512 * gi + 128 * s
            P.dma("sp", out_d[r0:r0 + 128, :], h1g[s][:, :], reads=[B("h1g", s)], writes=[B("out", gi, s)])
    return finish_prog()


def _bf16(a):
    return np.asarray(a).astype(ml_dtypes.bfloat16)


def prep_shared(inp):
    f = np.float32
    sh = {}
    sh["w_in"] = np.ascontiguousarray(inp["w_in"][0], dtype=f)
    sh["w_glu"] = np.ascontiguousarray(inp["w_glu"][0], dtype=f)
    sh["w_out"] = np.ascontiguousarray(inp["w_out"][0], dtype=f)
    sh["w_up"] = np.ascontiguousarray(inp["w_up"][0], dtype=f)
    sh["w_down"] = np.ascontiguousarray(inp["w_down"][0], dtype=f)
    vecs = np.zeros((128, NV), f)
    vecs[:, V_GMIX:V_GMIX + 16] = inp["norm_mix_g"][0].reshape(16, 128).T
    vecs[:, V_GFFN:V_GFFN + 16] = inp["norm_ffn_g"][0].reshape(16, 128).T
    vecs[:, V_GSSM:V_GSSM + 8] = inp["g_ssm_out"][0].reshape(8, 128).T
    vecs[:, V_GATT:V_GATT + 8] = inp["g_attn_out"][0].reshape(8, 128).T
    vecs[:, V_BGLU:V_BGLU + 8] = inp["b_glu"][0].reshape(8, 128).T
    vecs[:, V_DSK:V_DSK + 8] = inp["ssm_d"][0].reshape(8, 128).T
    cw = inp["conv_w"][0]
    vecs[:, V_CW:V_CW + 264] = cw.reshape(3, 88, 128).transpose(2, 1, 0).reshape(128, 264)
    vecs[:, V_CB:V_CB + 88] = inp["conv_b"][0].reshape(88, 128).T
    vecs[:, V_ZERO] = 0.0
    vecs[:, V_ONE] = 1.0
    vecs[:, V_EPS] = EPS
    vecs[:, V_NEGPI] = -np.pi
    for q in range(4):
        vecs[32 * q:32 * q + 32, V_BAND + q] = 1.0
    sh["vecs"] = vecs
    sh["gfin"] = np.ascontiguousarray(np.broadcast_to(inp["norm_final_g"][None, :], (128, D)), dtype=f)
    cb = np.zeros((128, 512), f)
    cb[:, 0:128] = np.eye(128)
    kk = np.arange(128)
    cb[:, 128:256] = (kk[:, None] >= kk[None, :])
    cb[:, 256:384] = 1.0
    sh["cbf"] = _bf16(cb)
    sh["maskd"] = (kk[:, None] < kk[None, :]).astype(f)
    lre = inp["ssm_lambda_re"][0]
    lim = inp["ssm_lambda_im"][0]
    ldt = inp["ssm_log_dt"][0]

    def playout(a):
        return a.reshape(32, 2, 64).transpose(1, 2, 0).reshape(128, 32)

    ldt_gp = np.broadcast_to(ldt[:, None], (64, 64))
    sh["ssmP"] = np.ascontiguousarray(np.concatenate([playout(lre), playout(lim), playout(ldt_gp)], 1), dtype=f)

    def xlayout(a):
        b = a.reshape(8, 4, 2, 64)
        b = b.transpose(1, 0, 2, 3).reshape(4, 1, 8, 128)
        b = np.broadcast_to(b, (4, 32, 8, 128)).reshape(128, 1024)
        return b

    sh["ssmX"] = np.ascontiguousarray(np.concatenate([xlayout(lre), xlayout(lim), xlayout(ldt_gp)], 1), dtype=f)

    def bx(bb):
        o = np.zeros((4, 2, 16, 8, 2, 64), f)
        b6 = bb.reshape(8, 4, 2, 64, 16)
        for gp in range(2):
            o[:, gp, :, :, gp, :] = b6[:, :, gp, :, :].transpose(1, 3, 0, 2)
        return o.reshape(128, 1024)

    sh["BX"] = np.concatenate([bx(inp["ssm_b_re"][0]), bx(inp["ssm_b_im"][0])], 1)

    def bp(bb):
        o = np.zeros((2, 64, 32, 2, 16), f)
        b5 = bb.reshape(32, 2, 64, 16)
        for gp in range(2):
            o[gp, :, :, gp, :] = b5[:, gp, :, :].transpose(1, 0, 2)
        return o.reshape(128, 1024)

    sh["BP"] = np.concatenate([bp(inp["ssm_b_re"][0]), bp(inp["ssm_b_im"][0])], 1)

    def cz(cc):
        o = np.zeros((2, 64, 32, 2, 16), f)
        c5 = cc.reshape(32, 2, 16, 64)
        for gp in range(2):
            o[gp, :, :, gp, :] = c5[:, gp, :, :].transpose(2, 0, 1)
        return o.reshape(128, 1024)

    sh["CZ"] = np.concatenate([cz(inp["ssm_c_re"][0]), cz(inp["ssm_c_im"][0])], 1)
    return sh


def prep_core(inp, sh, b, r):
    f = np.float32
    x = inp["x"][b]
    meta = inp["meta_tokens"]
    m = dict(sh)
    if r == 0:
        m["xpre"] = np.ascontiguousarray(x[0:NPRE], dtype=f)
        m["xown"] = np.ascontiguousarray(np.concatenate([meta, x[0:NOWN]], 0), dtype=f)
        pb, pscale = -60.0, 0.0
    else:
        m["xpre"] = np.ascontiguousarray(np.concatenate([meta, x[0:NPRE - 16]], 0), dtype=f)
        m["xown"] = np.ascontiguousarray(x[NPRE - 16:4096], dtype=f)
        pb, pscale = 0.0, 1.0
    v = sh["vecs"].copy()
    v[:, V_PBIAS] = pb
    v[:, V_PSCALE] = pscale
    m["vecs"] = v
    return m


_NC_CACHE = {}


def kernel(**inputs):
    inp = {k: np.asarray(v) for k, v in inputs.items()}
    if "nc" not in _NC_CACHE:
        _NC_CACHE["nc"] = build_program()
    nc = _NC_CACHE["nc"]
    sh = prep_shared(inp)
    in_maps = []
    for c in range(8):
        in_maps.append(prep_core(inp, sh, c // 2, c % 2))
    res = run_bass_kernel_spmd(nc, in_maps, core_ids=list(range(8)))
    out = np.zeros((4, 4096, D), np.float32)
    for c in range(8):
        b, r = c // 2, c % 2
        out[b, r * NOWN:(r + 1) * NOWN] = res.results[c]["out"]
    return out
```

```python
import bisect
from contextlib import ExitStack
import numpy as np
import ml_dtypes
import concourse.bass as bass
import concourse.mybir as mybir
from concourse.bass_utils import run_bass_kernel_spmd

F32 = mybir.dt.float32
BF16 = mybir.dt.bfloat16
I32 = mybir.dt.int32
AF = mybir.ActivationFunctionType
ALU = mybir.AluOpType

DSIZE = {F32: 4, BF16: 2, I32: 4}


class Acc:
    __slots__ = ("eng", "idx", "dtok")

    def __init__(self, eng, idx, dtok):
        self.eng = eng
        self.idx = idx
        self.dtok = dtok


class Buf:
    __slots__ = ("name", "w", "rs")

    def __init__(self, name):
        self.name = name
        self.w = None
        self.rs = {}


class EngS:
    def __init__(self, name, nslots=0):
        self.name = name
        self.recs = []
        self.ops = []
        self.nops = 0
        self.count = 0
        self.sig_idx = []
        self.sig_val = []
        self.seen = {}
        self.nslots = nslots
        self.dslot = 0
        self.dvals = [0] * nslots


class Prog:
    COMPUTE = ("pe", "act", "dve", "pool")

    def __init__(self, nc, sp_slots=40, pool_slots=24):
        self.nc = nc
        self.E = {n: EngS(n) for n in self.COMPUTE}
        self.E["sp"] = EngS("sp", sp_slots)
        self.E["pool"].nslots = pool_slots
        self.E["pool"].dvals = [0] * pool_slots
        self.ndma = 0

    def resolve(self, acc):
        if acc.dtok is not None:
            return acc.dtok
        e = self.E[acc.eng]
        i = bisect.bisect_left(e.sig_idx, acc.idx)
        if i < len(e.sig_idx):
            return (e.name, e.sig_val[i])
        e.count += 1
        e.ops[-1][1] = True
        e.sig_idx.append(e.nops - 1)
        e.sig_val.append(e.count)
        return (e.name, e.count)

    def _waits(self, E, deps):
        cur = E.nops
        for acc in deps:
            if acc.dtok is None and acc.eng == E.name:
                if E.name == "pe":
                    continue
                if cur - acc.idx > 3:
                    continue
            key, val = self.resolve(acc)
            if E.seen.get(key, 0) < val:
                E.recs.append(("wait", key, val))
                E.seen[key] = val

    @staticmethod
    def _deps(reads, writes):
        deps = []
        for b in reads:
            if b.w is not None:
                deps.append(b.w)
        for b in writes:
            if b.w is not None:
                deps.append(b.w)
            deps.extend(b.rs.values())
        return deps

    def op(self, eng, fn, reads=(), writes=(), signal=False):
        E = self.E[eng]
        self._waits(E, self._deps(reads, writes))
        rec = [fn, False]
        E.recs.append(("op", rec))
        E.ops.append(rec)
        acc = Acc(eng, E.nops, None)
        E.nops += 1
        if signal or eng != "pe":
            E.count += 1
            rec[1] = True
            E.sig_idx.append(E.nops - 1)
            E.sig_val.append(E.count)
        for b in reads:
            b.rs[eng] = acc
        for b in writes:
            b.w = acc
            b.rs = {}
        return acc

    def dma(self, q, out, in_, reads=(), writes=()):
        Q = self.E[q]
        self._waits(Q, self._deps(reads, writes))
        slot = Q.dslot
        Q.dslot = (Q.dslot + 1) % Q.nslots
        key = ("D", q, slot)
        prev = Q.dvals[slot]
        if prev > 0 and Q.seen.get(key, 0) < prev:
            Q.recs.append(("wait", key, prev))
            Q.seen[key] = prev
        val = prev + 16
        Q.dvals[slot] = val
        Q.recs.append(("dma", out, in_, key, val))
        acc = Acc(q, None, (key, val))
        for b in reads:
            b.rs[key] = acc
        for b in writes:
            b.w = acc
            b.rs = {}
        self.ndma += 1
        return acc

    def barrier(self):
        toks = []
        for n in self.COMPUTE:
            e = self.E[n]
            if e.nops > 0:
                toks.append(self.resolve(Acc(n, e.nops - 1, None)))
        for q in ("sp", "pool"):
            Q = self.E[q]
            for s in range(Q.nslots):
                if Q.dvals[s] > 0:
                    toks.append((("D", q, s), Q.dvals[s]))
        for n, E in self.E.items():
            for key, val in toks:
                if key == n:
                    continue
                if E.seen.get(key, 0) < val:
                    E.recs.append(("wait", key, val))
                    E.seen[key] = val

    def finish(self):
        self.barrier()

    def replay(self, stack):
        nc = self.nc
        sems = {}
        for n in self.COMPUTE:
            sems[n] = stack.enter_context(nc.semaphore("s_" + n))
        for q in ("sp", "pool"):
            for s in range(self.E[q].nslots):
                sems[("D", q, s)] = stack.enter_context(nc.semaphore("d_%s_%d" % (q, s)))
        block = stack.enter_context(nc.Block())

        def run(name):
            def f(eng):
                E = self.E[name]
                for r in E.recs:
                    if r[0] == "wait":
                        eng.wait_ge(sems[r[1]], r[2])
                    elif r[0] == "op":
                        ins = r[1][0](eng)
                        if r[1][1]:
                            ins.then_inc(sems[name], 1)
                    else:
                        eng.dma_start(out=r[1], in_=r[2]).then_inc(sems[r[3]], 16)
            return f

        block.sync(run("sp"))
        block.tensor(run("pe"))
        block.scalar(run("act"))
        block.vector(run("dve"))
        block.gpsimd(run("pool"))


class Arena:
    LO = 16512
    HI = 229344

    def __init__(self, nc):
        self.nc = nc
        self.lo = self.LO
        self.hi = self.HI
        self.n = 0

    @staticmethod
    def _size(shape, dtype):
        n = 1
        for s in shape[1:]:
            n *= s
        return (n * DSIZE[dtype] + 63) // 64 * 64

    def left(self, name, shape, dtype):
        sz = self._size(shape, dtype)
        off = self.lo
        self.lo += sz
        assert self.lo <= self.hi, ("sbuf overflow", name, self.lo, self.hi)
        self.n += 1
        return self.nc.alloc_sbuf_tensor_at("%s_%d" % (name, self.n), list(shape), dtype, offset=off)

    def right(self, name, shape, dtype):
        sz = self._size(shape, dtype)
        self.hi -= sz
        assert self.lo <= self.hi, ("sbuf overflow", name, self.lo, self.hi)
        self.n += 1
        return self.nc.alloc_sbuf_tensor_at("%s_%d" % (name, self.n), list(shape), dtype, offset=self.hi)

D = 2048
NPRE = 2048
NLEAD = 16
NOWN = 2048
NO = NLEAD + NOWN
NK = NPRE + NO
DFF = 5632
NJ = DFF // 128
CH_PRE = NPRE // 8
CH_OWN = NO // 8
EPS = 1e-6
TWO_PI = 2.0 * np.pi

V_GMIX = 0
V_GFFN = 16
V_GSSM = 32
V_GATT = 40
V_BGLU = 48
V_DSK = 56
V_CW = 64
V_CB = V_CW + 264
V_PBIAS = V_CB + 88
V_PSCALE = V_PBIAS + 1
V_ZERO = V_PSCALE + 1
V_ONE = V_ZERO + 1
V_EPS = V_ONE + 1
V_NEGPI = V_EPS + 1
V_BAND = V_NEGPI + 1
NV = V_BAND + 4


class BufMap:
    def __init__(self):
        self.d = {}

    def __call__(self, *key):
        b = self.d.get(key)
        if b is None:
            b = Buf(str(key))
            self.d[key] = b
        return b


def bcast_last(ap2d, n):
    a = [list(x) for x in ap2d.ap]
    return bass.AP(ap2d.tensor, ap2d.offset, a + [[0, n]])


class Ctx:
    pass


def derive_trig(C, n, lam_re_d, lam_im_d, ldt_d, T, TI, tag):
    P, B, vecs = C.P, C.B, C.vecs
    b = [B("dT", tag, i) for i in range(11)]
    bi = B("dTI", tag)
    bv = B("vecs")

    def tt(o, a, c, op):
        P.op("dve", lambda e: e.tensor_tensor(out=T[o][:], in0=T[a][:], in1=T[c][:], op=op), reads=[b[a], b[c]], writes=[b[o]])

    def ts(o, a, s1, s2, op0, op1=None):
        if op1 is None:
            P.op("dve", lambda e: e.tensor_scalar(out=T[o][:], in0=T[a][:], scalar1=s1, scalar2=None, op0=op0), reads=[b[a]], writes=[b[o]])
        else:
            P.op("dve", lambda e: e.tensor_scalar(out=T[o][:], in0=T[a][:], scalar1=s1, scalar2=s2, op0=op0, op1=op1), reads=[b[a]], writes=[b[o]])

    def act(o, a, func, scale=1.0, bias=None):
        if bias is None:
            P.op("act", lambda e: e.activation(out=T[o][:], in_=T[a][:], func=func, scale=scale), reads=[b[a]], writes=[b[o]])
        else:
            P.op("act", lambda e: e.activation(out=T[o][:], in_=T[a][:], func=func, scale=scale, bias=bias), reads=[b[a], bv], writes=[b[o]])

    P.dma("sp", T[0][:], lam_re_d, writes=[b[0]])
    P.dma("sp", T[1][:], lam_im_d, writes=[b[1]])
    P.dma("sp", T[2][:], ldt_d, writes=[b[2]])
    ts(0, 0, -1e-4, None, ALU.min)
    act(2, 2, AF.Exp)
    tt(3, 0, 2, ALU.mult)
    act(3, 3, AF.Exp)
    tt(4, 1, 2, ALU.mult)
    ts(4, 4, 1.0 / TWO_PI, 64.5, ALU.mult, ALU.add)

    def sin_of(r, o, t2):
        P.op("dve", lambda e: e.tensor_copy(out=TI[:], in_=T[r][:]), reads=[b[r]], writes=[bi])
        P.op("dve", lambda e: e.tensor_copy(out=T[o][:], in_=TI[:]), reads=[bi], writes=[b[o]])
        tt(o, r, o, ALU.subtract)
        ts(t2, o, 0.0, None, ALU.is_lt)
        tt(o, o, t2, ALU.add)
        ts(o, o, TWO_PI, None, ALU.mult)
        act(o, o, AF.Sin, 1.0, vecs[:, V_NEGPI:V_NEGPI + 1])

    sin_of(4, 5, 6)
    ts(4, 4, 0.25, None, ALU.add)
    sin_of(4, 6, 7)
    tt(7, 3, 6, ALU.mult)
    tt(8, 3, 5, ALU.mult)
    tt(4, 0, 0, ALU.mult)
    tt(5, 1, 1, ALU.mult)
    tt(4, 4, 5, ALU.add)
    P.op("dve", lambda e: e.reciprocal(out=T[4][:], in_=T[4][:]), reads=[b[4]], writes=[b[4]])
    ts(3, 7, -1.0, None, ALU.add)
    tt(5, 3, 0, ALU.mult)
    tt(6, 8, 1, ALU.mult)
    tt(5, 5, 6, ALU.add)
    tt(5, 5, 4, ALU.mult)
    tt(6, 8, 0, ALU.mult)
    tt(9, 3, 1, ALU.mult)
    tt(6, 6, 9, ALU.subtract)
    tt(6, 6, 4, ALU.mult)
    return b


def cmul_pool(C, eng, outr, outi, ar, ai, br, bi_, t1, t2, rd, wr):
    P = C.P
    P.op(eng, lambda e: e.tensor_tensor(out=t1, in0=ar, in1=br, op=ALU.mult), reads=rd, writes=[wr[2]])
    P.op(eng, lambda e: e.tensor_tensor(out=t2, in0=ai, in1=bi_, op=ALU.mult), reads=rd, writes=[wr[3]])
    P.op(eng, lambda e: e.tensor_tensor(out=outr, in0=t1, in1=t2, op=ALU.subtract), reads=[wr[2], wr[3]], writes=[wr[0]])
    P.op(eng, lambda e: e.tensor_tensor(out=t1, in0=ar, in1=bi_, op=ALU.mult), reads=rd, writes=[wr[2]])
    P.op(eng, lambda e: e.tensor_tensor(out=t2, in0=ai, in1=br, op=ALU.mult), reads=rd, writes=[wr[3]])
    P.op(eng, lambda e: e.tensor_tensor(out=outi, in0=t1, in1=t2, op=ALU.add), reads=[wr[2], wr[3]], writes=[wr[1]])


def derive_X(C, LG, ssmX_d, BX_d, base_off):
    P, B, nc = C.P, C.B, C.nc
    n = 256
    T = [nc.alloc_sbuf_tensor_at("dX%d" % i, [128, n], F32, offset=base_off + i * n * 4) for i in range(11)]
    TI = nc.alloc_sbuf_tensor_at("dXi", [128, n], I32, offset=base_off + 11 * n * 4)
    Bre = nc.alloc_sbuf_tensor_at("dXbr", [128, n], F32, offset=base_off + 12 * n * 4)
    Bim = nc.alloc_sbuf_tensor_at("dXbi", [128, n], F32, offset=base_off + 13 * n * 4)
    used = 14 * n * 4
    allb = []
    for h in range(4):
        c0 = 256 * h
        b = derive_trig(C, n, ssmX_d[:, c0:c0 + n], ssmX_d[:, 1024 + c0:1024 + c0 + n], ssmX_d[:, 2048 + c0:2048 + c0 + n], T, TI, "X")
        bbr, bbi = B("dXbr"), B("dXbi")
        P.dma("sp", Bre[:], BX_d[:, c0:c0 + n], writes=[bbr])
        P.dma("sp", Bim[:], BX_d[:, 1024 + c0:1024 + c0 + n], writes=[bbi])
        cur = (5, 6)
        nxt = (4, 9)
        for k in range(8):
            j = 7 - k
            lgr = LG[:, 0, j, 2 * h:2 * h + 2, :].rearrange("p a b -> p (a b)")
            lgi = LG[:, 1, j, 2 * h:2 * h + 2, :].rearrange("p a b -> p (a b)")
            cmul_pool(C, "pool", lgr, lgi, T[cur[0]][:], T[cur[1]][:], Bre[:], Bim[:], T[0][:], T[1][:],
                      [b[cur[0]], b[cur[1]], bbr, bbi], [B("LG"), B("LG"), b[0], b[1]])
            if k < 7:
                cmul_pool(C, "pool", T[nxt[0]][:], T[nxt[1]][:], T[cur[0]][:], T[cur[1]][:], T[7][:], T[8][:], T[2][:], T[3][:],
                          [b[cur[0]], b[cur[1]], b[7], b[8]], [b[nxt[0]], b[nxt[1]], b[2], b[3]])
                cur, nxt = nxt, cur
        allb = b + [bbr, bbi, B("dTI", "X")]
    return allb, used


def g_matmuls(C, uT, nch, GS, LG, uTm):
    P, B, ps, bank, vecs = C.P, C.B, C.ps, C.bank, C.vecs
    for t in range(8):
        for q in range(4):
            P.op("act", lambda e, t=t, q=q: e.activation(out=uTm[q][:, :, 0:nch], in_=uT[:, t, :, 0:nch], func=AF.Copy,
                                                          scale=vecs[:, V_BAND + q:V_BAND + q + 1]),
                 reads=[B("uT"), B("vecs")], writes=[B("uTm", q)])
        for q in range(4):
            pair = 4 * t + q
            for part in range(2):
                bk = bank()
                for j in range(8):
                    P.op("pe", lambda e, j=j, t=t, q=q, part=part, bk=bk: e.matmul(
                        ps[:, bk, 0:nch], lhsT=LG[:, part, j, t, :], rhs=uTm[q][:, j, 0:nch], start=(j == 0), stop=(j == 7)),
                        reads=[B("LG"), B("uTm", q)], writes=[B("ps", bk)])
                P.op("dve", lambda e, bk=bk, part=part, pair=pair: e.tensor_copy(out=GS[:, part, pair, 1:1 + nch], in_=ps[:, bk, 0:nch]),
                     reads=[B("ps", bk)], writes=[B("GSall")])


def scan_steps(C, GS, S2, M1, M2, T1, T2, nsteps, keep_hist):
    P, B = C.P, C.B
    for s in range(1, nsteps + 1):
        a = S2[(s - 1) % 2]
        o = S2[s % 2]
        ba, bo = B("S2", (s - 1) % 2), B("S2", s % 2)
        P.op("pool", lambda e, a=a: e.tensor_tensor(out=T1[:], in0=M1[:], in1=a[:], op=ALU.mult), reads=[B("M12"), ba], writes=[B("scT1")])
        P.op("pool", lambda e, a=a: e.tensor_tensor(out=T2[:, 0, :], in0=M2[:, 0, :], in1=a[:, 1, :], op=ALU.mult), reads=[B("M12"), ba], writes=[B("scT2")])
        P.op("pool", lambda e, a=a: e.tensor_tensor(out=T2[:, 1, :], in0=M2[:, 1, :], in1=a[:, 0, :], op=ALU.mult), reads=[B("M12"), ba], writes=[B("scT2")])
        P.op("pool", lambda e: e.tensor_tensor(out=T1[:], in0=T1[:], in1=T2[:], op=ALU.add), reads=[B("scT1"), B("scT2")], writes=[B("scT1")])
        P.op("pool", lambda e, o=o, s=s: e.tensor_tensor(out=o[:], in0=T1[:], in1=GS[:, :, :, s], op=ALU.add), reads=[B("scT1"), B("GSall")], writes=[bo])
        if keep_hist:
            P.op("act", lambda e, o=o, s=s: e.activation(out=GS[:, :, :, s], in_=o[:], func=AF.Copy), reads=[bo], writes=[B("GSh", s)])
        yield s


def derive_P_small(C, ssmP_d, EP, FP, M1, M2):
    P, B, A = C.P, C.B, C.A
    n = 32
    T = [A.right("dP%d" % i, [128, n], F32) for i in range(11)]
    TI = A.right("dPi", [128, n], I32)
    b = derive_trig(C, n, ssmP_d[:, 0:32], ssmP_d[:, 32:64], ssmP_d[:, 64:96], T, TI, "P")
    be = B("EP")
    P.op("dve", lambda e: e.memset(EP[:, 0, 0, :], 1.0), writes=[be])
    P.op("dve", lambda e: e.memset(EP[:, 1, 0, :], 0.0), writes=[be])
    P.op("dve", lambda e: e.tensor_copy(out=EP[:, 0, 1, :], in_=T[7][:]), reads=[b[7]], writes=[be])
    P.op("dve", lambda e: e.tensor_copy(out=EP[:, 1, 1, :], in_=T[8][:]), reads=[b[8]], writes=[be])
    P.op("dve", lambda e: e.tensor_copy(out=FP[:, 0, :], in_=T[5][:]), reads=[b[5]], writes=[B("FP")])
    P.op("dve", lambda e: e.tensor_copy(out=FP[:, 1, :], in_=T[6][:]), reads=[b[6]], writes=[B("FP")])
    for k in range(1, 8):
        cmul_pool(C, "dve", EP[:, 0, k + 1, :], EP[:, 1, k + 1, :], EP[:, 0, k, :], EP[:, 1, k, :], T[7][:], T[8][:], T[0][:], T[1][:],
                  [be, b[7], b[8]], [be, be, b[0], b[1]])
    bm = B("M12")
    P.op("dve", lambda e: e.tensor_copy(out=M1[:, 0, :], in_=EP[:, 0, 8, :]), reads=[be], writes=[bm])
    P.op("dve", lambda e: e.tensor_copy(out=M1[:, 1, :], in_=EP[:, 0, 8, :]), reads=[be], writes=[bm])
    P.op("dve", lambda e: e.tensor_scalar(out=M2[:, 0, :], in0=EP[:, 1, 8, :], scalar1=-1.0, scalar2=None, op0=ALU.mult), reads=[be], writes=[bm])
    P.op("dve", lambda e: e.tensor_copy(out=M2[:, 1, :], in_=EP[:, 1, 8, :]), reads=[be], writes=[bm])


def derive_LZ_K(C, EP, FP, BP_d, CZ_d, LZ, KTi, BbP, tmp_off):
    P, B, nc, ps, bank = C.P, C.B, C.nc, C.ps, C.bank
    n = 1024
    Cre = nc.alloc_sbuf_tensor_at("dZcr", [128, 32, 32], F32, offset=tmp_off)
    Cim = nc.alloc_sbuf_tensor_at("dZci", [128, 32, 32], F32, offset=tmp_off + 4096)
    t1 = nc.alloc_sbuf_tensor_at("dZt1", [128, 32, 32], F32, offset=tmp_off + 8192)
    t2 = nc.alloc_sbuf_tensor_at("dZt2", [128, 32, 32], F32, offset=tmp_off + 12288)
    bcr, bci, bt1, bt2 = B("dZcr"), B("dZci"), B("dZt1"), B("dZt2")
    be, bl = B("EP"), B("LZ")
    P.dma("sp", Cre[:].rearrange("p a b -> p (a b)"), CZ_d[:, 0:1024], writes=[bcr])
    P.dma("sp", Cim[:].rearrange("p a b -> p (a b)"), CZ_d[:, 1024:2048], writes=[bci])
    eng = "pool"
    for k in range(9):
        er = bcast_last(EP[:, 0, k, :], 32)
        ei = bcast_last(EP[:, 1, k, :], 32)
        P.op(eng, lambda e, er=er: e.tensor_tensor(out=t1[:], in0=Cre[:], in1=er, op=ALU.mult), reads=[bcr, be], writes=[bt1])
        P.op(eng, lambda e, ei=ei: e.tensor_tensor(out=t2[:], in0=Cim[:], in1=ei, op=ALU.mult), reads=[bci, be], writes=[bt2])
        P.op(eng, lambda e, k=k: e.tensor_tensor(out=LZ[:, 0, k, :, :], in0=t1[:], in1=t2[:], op=ALU.subtract), reads=[bt1, bt2], writes=[bl])
        P.op(eng, lambda e, ei=ei: e.tensor_tensor(out=t1[:], in0=Cre[:], in1=ei, op=ALU.mult), reads=[bcr, be], writes=[bt1])
        P.op(eng, lambda e, er=er: e.tensor_tensor(out=t2[:], in0=Cim[:], in1=er, op=ALU.mult), reads=[bci, be], writes=[bt2])
        P.op(eng, lambda e: e.tensor_tensor(out=t1[:], in0=t1[:], in1=t2[:], op=ALU.add), reads=[bt1, bt2], writes=[bt1])
        P.op(eng, lambda e, k=k: e.tensor_scalar(out=LZ[:, 1, k, :, :], in0=t1[:], scalar1=-1.0, scalar2=None, op0=ALU.mult), reads=[bt1], writes=[bl])
    bb = B("BbP")
    P.dma("sp", Cre[:].rearrange("p a b -> p (a b)"), BP_d[:, 0:1024], reads=[], writes=[bcr])
    P.dma("sp", Cim[:].rearrange("p a b -> p (a b)"), BP_d[:, 1024:2048], reads=[], writes=[bci])
    fr = bcast_last(FP[:, 0, :], 32)
    fi = bcast_last(FP[:, 1, :], 32)
    bf = B("FP")
    P.op(eng, lambda e: e.tensor_tensor(out=t1[:], in0=Cre[:], in1=fr, op=ALU.mult), reads=[bcr, bf], writes=[bt1])
    P.op(eng, lambda e: e.tensor_tensor(out=t2[:], in0=Cim[:], in1=fi, op=ALU.mult), reads=[bci, bf], writes=[bt2])
    P.op(eng, lambda e: e.tensor_tensor(out=BbP[:, 0, :, :], in0=t1[:], in1=t2[:], op=ALU.subtract), reads=[bt1, bt2], writes=[bb])
    P.op(eng, lambda e: e.tensor_tensor(out=t1[:], in0=Cre[:], in1=fi, op=ALU.mult), reads=[bcr, bf], writes=[bt1])
    P.op(eng, lambda e: e.tensor_tensor(out=t2[:], in0=Cim[:], in1=fr, op=ALU.mult), reads=[bci, bf], writes=[bt2])
    P.op(eng, lambda e: e.tensor_tensor(out=BbP[:, 1, :, :], in0=t1[:], in1=t2[:], op=ALU.add), reads=[bt1, bt2], writes=[bb])
    bk_ = B("KTi")
    P.op("pool", lambda e: e.memset(KTi[:].rearrange("p a b c -> p (a b c)"), 0.0), writes=[bk_])
    for pair in range(32):
        t, q = pair // 4, pair % 4
        bk = bank()
        for part in range(2):
            P.op("pe", lambda e, pair=pair, part=part, q=q, bk=bk: e.matmul(
                ps[32 * q:32 * q + 32, bk, 0:256], lhsT=BbP[:, part, pair, :], rhs=LZ[:, part, 0:8, pair, :],
                start=(part == 0), stop=(part == 1), tile_position=(0, 32 * q)), reads=[bb, bl], writes=[B("ps", bk)])
        P.op("dve", lambda e, t=t, q=q, bk=bk: e.tensor_copy(
            out=KTi[32 * q:32 * q + 32, t, :, 32 * q:32 * q + 32],
            in_=ps[32 * q:32 * q + 32, bk, 0:256].rearrange("p (a b) -> p a b", a=8)), reads=[B("ps", bk)], writes=[bk_])
    return 16384


def build_program(debug=False, stop_after=None):
    nc = bass.Bass("TRN2", target_bir_lowering=False)
    P = Prog(nc)
    A = Arena(nc)
    B = BufMap()

    def din(name, shape, dt=F32):
        return nc.dram_tensor(name, list(shape), dt, kind="ExternalInput").ap()

    skind = "ExternalOutput" if debug else "Internal"

    def dscr(name, shape, dt):
        return nc.dram_tensor(name, list(shape), dt, kind=skind).ap()

    xpre = din("xpre", [NPRE, D])
    xown = din("xown", [NO, D])
    w_in = din("w_in", [D, 4096])
    w_glu = din("w_glu", [1024, 1024])
    w_out = din("w_out", [D, D])
    w_up = din("w_up", [D, 2 * DFF])
    w_down = din("w_down", [DFF, D])
    vecs_d = din("vecs", [128, NV])
    gfin_d = din("gfin", [128, D])
    cbf_d = din("cbf", [128, 640], BF16)
    maskd_d = din("maskd", [128, 128])
    ssmP_d = din("ssmP", [128, 96])
    ssmX_d = din("ssmX", [128, 3072])
    BX_d = din("BX", [128, 2048])
    BP_d = din("BP", [128, 2048])
    CZ_d = din("CZ", [128, 2048])
    out_d = nc.dram_tensor("out", [NOWN, D], F32, kind="ExternalOutput").ap()

    KTd = dscr("KTd", [8, 128, NK], BF16)
    Vd = dscr("Vd", [NK, 1024], BF16)
    QTd = dscr("QTd", [8, 128, NO], BF16)
    uTd = dscr("uTd", [128, 8 * 8 * CH_OWN], BF16)
    yad = dscr("yad", [8, 128, NO], BF16)
    ysd = dscr("ysd", [8, 128, NO], BF16)
    h1d = dscr("h1d", [NO, D], F32)

    vecs = A.right("vecs", [128, NV], F32)
    cbf = A.right("cbf", [128, 640], BF16)
    maskd = A.right("maskd", [128, 128], F32)
    gamB = A.right("gamB", [128, 16, 128], BF16)
    ident = cbf[:, 0:128]
    tri = cbf[:, 128:256]
    ones = cbf[:, 256:384]
    zer = cbf[:, 384:512]
    stri = cbf[:, 512:640]

    def vcol(c, n=128):
        return vecs[0:n, c:c + 1]

    ps = nc.alloc_psum_tensor("ps", [128, 8, 512], F32)
    psT = ps[:, 6:8, :].bitcast(BF16).rearrange("p b (k t) -> p (b k) t", t=128)
    psctr = [0]

    def bank():
        b = psctr[0] % 6
        psctr[0] += 1
        return b

    def finish_prog():
        P.finish()
        with ExitStack() as st_:
            P.replay(st_)
        nc._prog = P
        return nc

    P.dma("sp", vecs[:], vecs_d[:, :], writes=[B("vecs")])
    P.dma("sp", cbf[:], cbf_d[:, :], writes=[B("cbf")])
    P.dma("sp", maskd[:], maskd_d[:, :], writes=[B("maskd")])

    def set_gam(col):
        P.op("dve", lambda e: e.tensor_copy(out=gamB[:], in_=bcast_last(vecs[:, col:col + 16], 128)),
             reads=[B("vecs")], writes=[B("gamB")])

    RM = A.hi

    class NormCtx:
        pass

    def make_norm_ctx():
        c = NormCtx()
        c.xt = [A.left("xt", [128, D], F32)] * 2
        c.xn = [A.left("xn", [128, D], BF16) for _ in range(2)]
        c.st = A.left("nst", [128, 8], F32)
        c.i = 0
        return c

    def norm_transpose(nctx, src_ap, nrows, hnT, hcol0, hbuf, from_dram=True, keep=None):
        i = nctx.i % 2
        nctx.i += 1
        xn = nctx.xn[i]
        st = nctx.st
        if from_dram:
            xt = nctx.xt[i]
            bx = B("xt", id(nctx))
            P.dma("sp", xt[0:nrows, :], src_ap, writes=[bx])
        else:
            xt, bx = keep
        bxn = B("xn", id(nctx), i)
        bst = B("nst", id(nctx), i)
        c0 = 4 * i
        P.op("dve", lambda e: e.memset(st[0:nrows, c0:c0 + 1], 0.0), writes=[bst])
        P.op("act", lambda e: e.activation(out=xn[0:nrows, :], in_=xt[0:nrows, :], func=AF.Square,
                                           accum_out=st[0:nrows, c0:c0 + 1]), reads=[bx], writes=[bxn, bst])
        P.op("act", lambda e: e.activation(out=st[0:nrows, c0 + 1:c0 + 2], in_=st[0:nrows, c0:c0 + 1], func=AF.Ln,
                                           scale=1.0 / D, bias=vcol(V_EPS, nrows)), reads=[bst, B("vecs")], writes=[bst])
        P.op("act", lambda e: e.activation(out=st[0:nrows, c0 + 2:c0 + 3], in_=st[0:nrows, c0 + 1:c0 + 2], func=AF.Exp,
                                           scale=-0.5), reads=[bst], writes=[bst])
        P.op("dve", lambda e: e.tensor_scalar(out=xn[0:nrows, :], in0=xt[0:nrows, :], scalar1=st[0:nrows, c0 + 2:c0 + 3],
                                              scalar2=None, op0=ALU.mult), reads=[bx, bst], writes=[bxn])
        for k in range(16):
            P.op("pe", lambda e, k=k: e.transpose(out=psT[:, k, 0:nrows], in_=xn[0:nrows, k * 128:(k + 1) * 128],
                                                  identity=ident[0:nrows, 0:nrows]),
                 reads=[bxn, B("cbf")], writes=[B("psT")])
        P.op("dve", lambda e: e.tensor_tensor(out=hnT[:, :, hcol0:hcol0 + nrows], in0=psT[:, :, 0:nrows],
                                              in1=gamB[:, :, 0:nrows], op=ALU.mult),
             reads=[B("psT"), B("gamB")], writes=[hbuf])

    class Ring:
        def __init__(self, name, shape, n):
            self.t = [A.left(name, shape, BF16) for _ in range(n)]
            self.n = n
            self.name = name
            self.i = 0

        def next(self):
            i = self.i % self.n
            self.i += 1
            return self.t[i], B(self.name, id(self), i)

    w_in_v = w_in.rearrange("(k p) c -> p k c", p=128)

    LM = A.lo
    nctx = make_norm_ctx()
    hnT = [A.left("hnT", [128, 16, 528], BF16) for _ in range(2)]
    wring = Ring("win", [128, 16, 256], 3)
    kst = [A.left("kst", [128, 528], BF16) for _ in range(2)]
    vst = [A.left("vst", [128, 256], BF16) for _ in range(3)]
    vctr = [0]
    uT_own = A.left("uTown", [128, 8, 8, CH_OWN], BF16)
    uT_pre = uT_own
    LG = A.left("LG", [128, 2, 8, 8, 128], BF16)
    R1M = A.lo
    kctr = [0]

    set_gam(V_GMIX)

    C = Ctx()
    C.P, C.A, C.B, C.nc, C.ps, C.bank, C.vecs, C.psT = P, A, B, nc, ps, bank, vecs, psT
    GS = A.right("GS", [128, 2, 32, CH_OWN + 1], BF16)
    S2 = [A.right("S2", [128, 2, 32], F32) for _ in range(2)]
    M1 = A.right("M1", [128, 2, 32], F32)
    M2 = A.right("M2", [128, 2, 32], F32)
    scT1 = A.right("scT1", [128, 2, 32], F32)
    scT2 = A.right("scT2", [128, 2, 32], F32)
    Hpf = A.right("Hpf", [128, 2, 32], F32)
    EP = A.right("EP", [128, 2, 9, 32], F32)
    FP = A.right("FP", [128, 2, 32], F32)
    dX_bufs, dX_used = derive_X(C, LG, ssmX_d, BX_d, R1M)
    uTm = [nc.alloc_sbuf_tensor_at("uTm%d" % q, [128, 8, CH_OWN], BF16, offset=R1M + q * 8 * CH_OWN * 2 + (q * 64)) for q in range(4)]
    r1_size = max(dX_used, 4 * (8 * CH_OWN * 2 + 64))
    A.lo = R1M + r1_size
    assert A.lo <= A.hi, ("R1 overflow", A.lo, A.hi)
    derive_P_small(C, ssmP_d, EP, FP, M1, M2)
    scan_gen = [None]

    def pump(nst):
        g = scan_gen[0]
        if g is None:
            return
        for _ in range(nst):
            try:
                next(g)
            except StopIteration:
                scan_gen[0] = None
                return

    def after_prefix():
        for q in range(4):
            P.op("act", lambda e, q=q: e.activation(out=uTm[q][:, 0, 0:1], in_=vecs[:, V_ZERO:V_ZERO + 1], func=AF.Copy),
                 reads=dX_bufs + [B("vecs")], writes=[B("uTm", q)])
        P.op("pool", lambda e: e.memset(GS[:, :, :, 0:1], 0.0), writes=[B("GSall")])
        P.op("pool", lambda e: e.memset(S2[0][:], 0.0), writes=[B("S2", 0)])
        g_matmuls(C, uT_pre, CH_PRE, GS, LG, uTm)
        scan_gen[0] = scan_steps(C, GS, S2, M1, M2, scT1, scT2, CH_PRE, False)

    def group_subtiles(kind, gi):
        subs = []
        if kind == "pre":
            for s in range(4):
                r0 = 512 * gi + 128 * s
                subs.append((xpre[r0:r0 + 128, :], 128, 16 + 128 * s, r0))
        else:
            if gi == 0:
                subs.append((xown[0:16, :], 16, 0, NPRE))
            for s in range(4):
                r0 = 16 + 512 * gi + 128 * s
                subs.append((xown[r0:r0 + 128, :], 128, 16 + 128 * s, NPRE + r0))
        return subs

    def emit_norm(kind, gi, hi):
        for (src, nrows, hcol0, _) in group_subtiles(kind, gi):
            norm_transpose(nctx, src, nrows, hnT[hi], hcol0, B("hnT", hi))

    groups = [("pre", g) for g in range(4)] + [("own", g) for g in range(4)]
    wsched = []
    for gidx, (kind, gi) in enumerate(groups):
        blocks = [0, 1, 2, 3, 8, 9, 10, 11, 12, 13, 14, 15] if kind == "pre" else list(range(16))
        for cb in blocks:
            wsched.append((gidx, cb))
    wtiles = {}
    wnext = [0]

    def issue_w():
        if wnext[0] >= len(wsched):
            return
        gidx, cb = wsched[wnext[0]]
        t, b = wring.next()
        P.dma("pool", t[:, :, :], w_in_v[:, :, cb * 256:(cb + 1) * 256], writes=[b])
        wtiles[wnext[0]] = (t, b)
        wnext[0] += 1

    for _ in range(3):
        issue_w()
    emit_norm(*groups[0], 0)
    wi = 0
    for gidx, (kind, gi) in enumerate(groups):
        hi = gidx % 2
        hT = hnT[hi]
        hb = B("hnT", hi)
        if gidx + 1 < len(groups):
            emit_norm(*groups[gidx + 1], (gidx + 1) % 2)
        subs = group_subtiles(kind, gi)
        lead = (kind == "own" and gi == 0)
        ranges = ([(0, 16)] if lead else []) + [(16, 512)]
        blocks = [0, 1, 2, 3, 8, 9, 10, 11, 12, 13, 14, 15] if kind == "pre" else list(range(16))
        uT = uT_pre if kind == "pre" else uT_own
        ub = B("uT")
        for cb in blocks:
            wt, wb = wtiles.pop(wi)
            wi += 1
            if cb < 12:
                for m in range(2):
                    oc = cb * 2 + m
                    for (c0, n) in ranges:
                        bk = bank()
                        for k in range(16):
                            P.op("pe", lambda e, k=k, m=m, c0=c0, n=n, bk=bk, wt=wt, hT=hT: e.matmul(
                                ps[:, bk, 0:n], lhsT=wt[:, k, m * 128:(m + 1) * 128], rhs=hT[:, k, c0:c0 + n],
                                start=(k == 0), stop=(k == 15)), reads=[wb, hb], writes=[B("ps", bk)])
                        if oc < 8:
                            if kind == "pre":
                                ch0 = 64 * gi
                            else:
                                ch0 = 0 if c0 == 0 else 2 + 64 * gi
                            nch = n // 8
                            P.op("act", lambda e, bk=bk, n=n, oc=oc, ch0=ch0, nch=nch, uT=uT: e.activation(
                                out=uT[:, oc, :, ch0:ch0 + nch], in_=ps[:, bk, 0:n].rearrange("p (c j) -> p j c", j=8),
                                func=AF.Copy), reads=[B("ps", bk)], writes=[ub])
                        else:
                            si = kctr[0] % 2
                            kctr[0] += 1
                            ks = kst[si]
                            P.op("act", lambda e, bk=bk, n=n, ks=ks: e.activation(out=ks[:, 0:n], in_=ps[:, bk, 0:n],
                                                                                 func=AF.Copy),
                                 reads=[B("ps", bk)], writes=[B("kst", si)])
                            if oc < 16:
                                hp = oc - 8
                                qc0 = 0 if c0 == 0 else 16 + 512 * gi
                                P.dma("sp", QTd[hp, :, qc0:qc0 + n], ks[:, 0:n], reads=[B("kst", si)],
                                      writes=[B("QTd", hp, gi, c0)])
                            else:
                                hp = oc - 16
                                if kind == "pre":
                                    kc0 = 512 * gi
                                else:
                                    kc0 = NPRE if c0 == 0 else NPRE + 16 + 512 * gi
                                P.dma("sp", KTd[hp, :, kc0:kc0 + n], ks[:, 0:n], reads=[B("kst", si)],
                                      writes=[B("KTd", hp, kind, gi, c0)])
            else:
                half = cb - 12
                for si, (_, nrows, hcol0, krow0) in enumerate(subs):
                    bk = bank()
                    vi = si if not lead else si
                    for k in range(16):
                        P.op("pe", lambda e, k=k, bk=bk, nrows=nrows, hcol0=hcol0, wt=wt, hT=hT: e.matmul(
                            ps[0:nrows, bk, 0:256], lhsT=hT[:, k, hcol0:hcol0 + nrows], rhs=wt[:, k, :],
                            start=(k == 0), stop=(k == 15)), reads=[wb, hb], writes=[B("ps", bk)])
                    vi = vctr[0] % 3
                    vctr[0] += 1
                    vs = vst[vi]
                    P.op("act", lambda e, bk=bk, nrows=nrows, vs=vs: e.activation(
                        out=vs[0:nrows, :], in_=ps[0:nrows, bk, 0:256], func=AF.Copy),
                        reads=[B("ps", bk)], writes=[B("vst", vi)])
                    P.dma("sp", Vd[krow0:krow0 + nrows, half * 256:(half + 1) * 256], vs[0:nrows, :], reads=[B("vst", vi)],
                          writes=[B("Vd", krow0, half)])
            issue_w()
            if gidx >= 4:
                pump(4)
        if gidx == 3:
            after_prefix()
    if stop_after == "inproj":
        P.dma("sp", uTd[:, :], uT_own[:].rearrange("p a b c -> p (a b c)"), reads=[B("uT", "own")], writes=[B("uTd")])
        return finish_prog()

    pump(10 ** 9)
    P.op("act", lambda e: e.activation(out=Hpf[:], in_=S2[CH_PRE % 2][:], func=AF.Copy, scale=vecs[:, V_PSCALE:V_PSCALE + 1]),
         reads=[B("S2", CH_PRE % 2), B("vecs")], writes=[B("Hpf")])
    g_matmuls(C, uT_own, CH_OWN, GS, LG, uTm)
    P.op("act", lambda e: e.activation(out=GS[:, :, :, 0], in_=Hpf[:], func=AF.Copy), reads=[B("Hpf")], writes=[B("GSall")])
    P.op("act", lambda e: e.activation(out=S2[0][:], in_=Hpf[:], func=AF.Copy), reads=[B("Hpf")], writes=[B("S2", 0)])
    P.dma("sp", uTd[:, :], uT_own[:].rearrange("p a b c -> p (a b c)"), reads=[B("uT")], writes=[B("uTd")])
    scan_gen[0] = scan_steps(C, GS, S2, M1, M2, scT1, scT2, CH_OWN, True)
    if stop_after == "gown":
        pump(10 ** 9)
        return finish_prog()

    P.barrier()
    A.lo = LM
    LZ = A.right("LZ", [128, 2, 9, 32, 32], BF16)
    KTi = A.right("KTi", [128, 8, 8, 128], BF16)
    BbP = A.right("BbP", [128, 2, 32, 32], BF16)
    tmpz = A.left("tmpz", [128, 4096], F32)
    derive_LZ_K(C, EP, FP, BP_d, CZ_d, LZ, KTi, BbP, LM)
    KTb = [A.left("KTb", [128, NK], BF16) for _ in range(2)]
    Vb = [A.left("Vb", [128, 33, 128], BF16) for _ in range(2)]
    QTb = [A.left("QTb", [128, NO], BF16) for _ in range(2)]
    pbuf = [A.left("pb", [128, 2, 512], F32) for _ in range(3)]
    spb = [A.left("spb", [128, 2, 512], BF16) for _ in range(2)]
    eb = [A.left("eb", [128, 2, 512], F32) for _ in range(2)]
    wb_ = [A.left("wb", [128, 2, 512], BF16) for _ in range(2)]
    yst = [A.left("yst", [128, 512], BF16) for _ in range(2)]
    SCALE = 0.125

    def load_hp(hp, i):
        P.dma("sp", KTb[i][:, :], KTd[hp, :, :], reads=[B("KTd", hp, k_, g_, c_) for (k_, g_, c_) in ktd_keys], writes=[B("KTb", i)])
        P.dma("sp", Vb[i][:, 0:16, :], Vd[0:NPRE, hp * 128:(hp + 1) * 128].rearrange("(b p) c -> p b c", p=128),
              reads=vd_bufs, writes=[B("Vb", i)])
        P.dma("sp", Vb[i][0:16, 16, :], Vd[NPRE:NPRE + 16, hp * 128:(hp + 1) * 128], reads=vd_bufs, writes=[B("Vb", i)])
        P.dma("sp", Vb[i][:, 17:33, :], Vd[NPRE + 16:NK, hp * 128:(hp + 1) * 128].rearrange("(b p) c -> p b c", p=128),
              reads=vd_bufs, writes=[B("Vb", i)])
        P.dma("sp", QTb[i][:, :], QTd[hp, :, :], reads=[B("QTd", hp, g_, c_) for (g_, c_) in qtd_keys], writes=[B("QTb", i)])

    ktd_keys = [("pre", g, 16) for g in range(4)] + [("own", g, 16) for g in range(4)] + [("own", 0, 0)]
    qtd_keys = [(g, 16) for g in range(4)] + [(0, 0)]
    vd_bufs = [b for k, b in B.d.items() if k[0] == "Vd"]

    def tiles_for_group(g):
        tl = []
        if g < 0:
            tl.append((NPRE, 16, 16, 0, (0, 16), False))
        else:
            for i in (3, 2, 1, 0):
                ob = 4 * g + i
                tl.append((NPRE + 16 + 128 * ob, 128, 17 + ob, 128 * i, (128 * i, 128 * i + 128), False))
            for ob in range(4 * g - 1, -1, -1):
                tl.append((NPRE + 16 + 128 * ob, 128, 17 + ob, 0, None, False))
            tl.append((NPRE, 16, 16, 0, None, False))
        for pb in range(15, -1, -1):
            tl.append((128 * pb, 128, pb, 0, None, True))
        return tl

    qgroups = [(-1, 0, 16)] + [(g, 16 + 512 * g, 512) for g in range(4)]
    ogrp = [0]

    def bc_heads(ap2d, n):
        a = [list(x) for x in ap2d.ap]
        return bass.AP(ap2d.tensor, ap2d.offset, [a[0], [0, 2], a[1]])

    def attn_group(hp, bi, g, qc0, ncols):
        tl = tiles_for_group(g)
        nt = len(tl)
        ob = 6 + (ogrp[0] % 2)
        ogrp[0] += 1
        kt, vb, qt = KTb[bi], Vb[bi], QTb[bi]
        bkt, bvb, bqt = B("KTb", bi), B("Vb", bi), B("QTb", bi)
        for e_ in range(2):
            P.op("pe", lambda e, e_=e_: e.matmul(ps[:, 4 + e_, 0:ncols], lhsT=zer[:, 0:128], rhs=kt[:, 0:ncols],
                                                   start=True, stop=False, skip_group_check=True),
                 reads=[B("cbf"), bkt], writes=[B("ps", 4 + e_)])
            P.op("pe", lambda e, e_=e_: e.matmul(ps[64 * e_:64 * e_ + 64, ob, 0:ncols], lhsT=zer[:, 0:64], rhs=kt[:, 0:ncols],
                                                   start=True, stop=False, skip_group_check=True, tile_position=(0, 64 * e_)),
                 reads=[B("cbf"), bkt], writes=[B("ps", ob)])

        def pe_S(ti):
            kc0, nk, vs, c0, mk, ispre = tl[ti]
            sl = ti % 2
            for e_ in range(2):
                pe_ = 64 * e_
                P.op("pe", lambda e, e_=e_, pe_=pe_, kc0=kc0, nk=nk, c0=c0, sl=sl: e.matmul(
                    ps[0:nk, 2 * sl + e_, c0:ncols], lhsT=kt[pe_:pe_ + 64, kc0:kc0 + nk], rhs=qt[pe_:pe_ + 64, qc0 + c0:qc0 + ncols],
                    start=True, stop=True), reads=[bkt, bqt], writes=[B("ps", 2 * sl + e_)], signal=(e_ == 1))

        def act_p(ti):
            kc0, nk, vs, c0, mk, ispre = tl[ti]
            sl = ti % 2
            pbt, bpb = pbuf[ti % 3], B("pb", ti % 3)
            bcol = V_PBIAS if ispre else V_ZERO
            P.op("act", lambda e, nk=nk, c0=c0, sl=sl, pbt=pbt, bcol=bcol: e.activation(
                out=pbt[0:nk, :, c0:ncols], in_=ps[0:nk, 2 * sl:2 * sl + 2, c0:ncols], func=AF.Exp, scale=SCALE,
                bias=vecs[0:nk, bcol:bcol + 1]), reads=[B("ps", 2 * sl), B("ps", 2 * sl + 1), B("vecs")], writes=[bpb])
            if mk is not None:
                m0, m1 = mk
                P.op("dve", lambda e, nk=nk, m0=m0, m1=m1, pbt=pbt: e.tensor_tensor(
                    out=pbt[0:nk, :, m0:m1], in0=pbt[0:nk, :, m0:m1], in1=bc_heads(maskd[0:nk, 0:m1 - m0], m1 - m0), op=ALU.mult),
                    reads=[bpb, B("maskd")], writes=[bpb])

        def act_sp(ti):
            kc0, nk, vs, c0, mk, ispre = tl[ti]
            pbt, bpb = pbuf[ti % 3], B("pb", ti % 3)
            sp_, bsp = spb[ti % 2], B("spb", ti % 2)
            P.op("act", lambda e, nk=nk, c0=c0, pbt=pbt, sp_=sp_: e.activation(
                out=sp_[0:nk, :, c0:ncols], in_=pbt[0:nk, :, c0:ncols], func=AF.Ln, scale=1.0, bias=vecs[0:nk, V_ONE:V_ONE + 1]),
                reads=[bpb, B("vecs")], writes=[bsp])

        def pe_A(ti):
            kc0, nk, vs, c0, mk, ispre = tl[ti]
            sp_, bsp = spb[ti % 2], B("spb", ti % 2)
            for e_ in range(2):
                if ti >= 1:
                    kc0b, nkb, _, c0b, _, _ = tl[ti - 1]
                    spo, bspo = spb[(ti - 1) % 2], B("spb", (ti - 1) % 2)
                    P.op("pe", lambda e, e_=e_, nkb=nkb, c0b=c0b, spo=spo: e.matmul(
                        ps[:, 4 + e_, c0b:ncols], lhsT=stri[0:nkb, :], rhs=spo[0:nkb, e_, c0b:ncols], start=False, stop=False,
                        skip_group_check=True), reads=[B("cbf"), bspo], writes=[B("ps", 4 + e_)])
                P.op("pe", lambda e, e_=e_, nk=nk, c0=c0, sp_=sp_: e.matmul(
                    ps[0:nk, 4 + e_, c0:ncols], lhsT=tri[0:nk, 0:nk], rhs=sp_[0:nk, e_, c0:ncols], start=False, stop=(ti == nt - 1),
                    skip_group_check=True), reads=[B("cbf"), bsp], writes=[B("ps", 4 + e_)], signal=True)

        def act_e(ti):
            kc0, nk, vs, c0, mk, ispre = tl[ti]
            ebt, beb = eb[ti % 2], B("eb", ti % 2)
            for e_ in range(2):
                P.op("act", lambda e, e_=e_, nk=nk, c0=c0, ebt=ebt: e.activation(
                    out=ebt[0:nk, e_, c0:ncols], in_=ps[0:nk, 4 + e_, c0:ncols], func=AF.Exp, scale=-1.0),
                    reads=[B("ps", 4 + e_)], writes=[beb])
            pbt, bpb = pbuf[ti % 3], B("pb", ti % 3)
            wbt, bwb = wb_[ti % 2], B("wb", ti % 2)
            P.op("dve", lambda e, nk=nk, c0=c0, pbt=pbt, ebt=ebt, wbt=wbt: e.tensor_tensor(
                out=wbt[0:nk, :, c0:ncols], in0=pbt[0:nk, :, c0:ncols], in1=ebt[0:nk, :, c0:ncols], op=ALU.mult),
                reads=[bpb, beb], writes=[bwb])

        def pe_PV(ti):
            kc0, nk, vs, c0, mk, ispre = tl[ti]
            wbt, bwb = wb_[ti % 2], B("wb", ti % 2)
            for e_ in range(2):
                P.op("pe", lambda e, e_=e_, nk=nk, vs=vs, c0=c0, wbt=wbt: e.matmul(
                    ps[64 * e_:64 * e_ + 64, ob, c0:ncols], lhsT=vb[0:nk, vs, 64 * e_:64 * e_ + 64], rhs=wbt[0:nk, e_, c0:ncols],
                    start=False, stop=(ti == nt - 1), skip_group_check=True, tile_position=(0, 64 * e_)),
                    reads=[bvb, bwb], writes=[B("ps", ob)])

        pe_S(0)
        if nt > 1:
            pe_S(1)
        act_p(0)
        for it in range(nt + 2):
            if it + 2 < nt:
                pe_S(it + 2)
            if 0 <= it - 2 < nt:
                pe_PV(it - 2)
            if it < nt:
                act_sp(it)
            if it + 1 < nt:
                act_p(it + 1)
            if it < nt:
                pe_A(it)
                act_e(it)
            pump(1)
        ys = yst[ogrp[0] % 2]
        bys = B("yst", ogrp[0] % 2)
        P.op("dve", lambda e: e.tensor_copy(out=ys[:, 0:ncols], in_=ps[:, ob, 0:ncols]), reads=[B("ps", ob)], writes=[bys])
        P.dma("sp", yad[hp, :, qc0:qc0 + ncols], ys[:, 0:ncols], reads=[bys], writes=[B("yad", hp, g)])

    load_hp(0, 0)
    for hp in range(8):
        if hp + 1 < 8:
            load_hp(hp + 1, (hp + 1) % 2)
        for (g, qc0, ncols) in qgroups:
            attn_group(hp, hp % 2, g, qc0, ncols)
    pump(10 ** 9)
    P.op("act", lambda e: e.activation(out=scT2[:, 0, 0:1], in_=vecs[:, V_ZERO:V_ZERO + 1], func=AF.Copy),
         reads=[B("GSh", s_) for s_ in range(1, CH_OWN + 1)] + [B("GSall"), B("vecs")], writes=[B("GShist"), B("scT2")])
    if stop_after == "attn":
        return finish_prog()

    P.barrier()
    A.lo = LM
    uTr = A.left("uTr", [128, 8, 8, CH_OWN], BF16)
    ygT = A.left("ygT", [128, 8, NO], BF16)
    wglu = A.left("wglu", [128, 8, 1024], BF16)
    yr = [A.left("yr", [128, CH_OWN], F32) for _ in range(2)]
    x2 = [A.left("x2", [128, CH_OWN], F32) for _ in range(2)]
    sgb = [A.left("sgb", [128, 512], F32) for _ in range(2)]
    ysb = [A.left("ysb", [128, 512], BF16) for _ in range(2)]
    P.dma("sp", uTr[:].rearrange("p a b c -> p (a b c)"), uTd[:, :], reads=[B("uTd")], writes=[B("uTr")])
    P.dma("pool", wglu[:, :, :], w_glu.rearrange("(k p) c -> p k c", p=128), writes=[B("wglu")])
    NC_ = CH_OWN
    GC1, GC2 = 1.5957691216057308, 0.044715
    cnt = 0
    for t in range(8):
        for j in range(8):
            bk = bank()
            nmm = (j + 1) + 8
            im = 0
            for ji in range(j + 1):
                P.op("pe", lambda e, t=t, j=j, ji=ji, bk=bk, im=im: e.matmul(
                    ps[:, bk, 0:NC_], lhsT=KTi[:, t, j - ji, :], rhs=uTr[:, t, ji, :], start=(im == 0), stop=False, skip_group_check=True),
                    reads=[B("KTi"), B("uTr")], writes=[B("ps", bk)])
                im += 1
            for q in range(4):
                pair = 4 * t + q
                for part in range(2):
                    last = (q == 3 and part == 1)
                    P.op("pe", lambda e, t=t, j=j, q=q, part=part, pair=pair, bk=bk, last=last: e.matmul(
                        ps[32 * q:32 * q + 32, bk, 0:NC_], lhsT=LZ[:, part, j + 1, pair, :], rhs=GS[:, part, pair, 0:NC_],
                        start=False, stop=last, skip_group_check=True, tile_position=(0, 32 * q)),
                        reads=[B("LZ"), B("GShist")], writes=[B("ps", bk)])
            i2 = cnt % 2
            cnt += 1
            yrt, x2t = yr[i2], x2[i2]
            byr, bx2 = B("yr", i2), B("x2", i2)
            P.op("dve", lambda e, t=t, j=j, bk=bk, yrt=yrt: e.scalar_tensor_tensor(
                out=yrt[:, :], in0=uTr[:, t, j, :], scalar=vecs[:, V_DSK + t:V_DSK + t + 1], in1=ps[:, bk, 0:NC_],
                op0=ALU.mult, op1=ALU.add), reads=[B("uTr"), B("vecs"), B("ps", bk)], writes=[byr])
            P.op("pool", lambda e, yrt=yrt, x2t=x2t: e.tensor_tensor(out=x2t[:, :], in0=yrt[:, :], in1=yrt[:, :], op=ALU.mult),
                 reads=[byr], writes=[bx2])
            P.op("pool", lambda e, x2t=x2t: e.tensor_scalar(out=x2t[:, :], in0=x2t[:, :], scalar1=GC2, scalar2=1.0, op0=ALU.mult, op1=ALU.add),
                 reads=[bx2], writes=[bx2])
            P.op("pool", lambda e, yrt=yrt, x2t=x2t: e.tensor_tensor(out=x2t[:, :], in0=x2t[:, :], in1=yrt[:, :], op=ALU.mult),
                 reads=[bx2, byr], writes=[bx2])
            P.op("act", lambda e, x2t=x2t: e.activation(out=x2t[:, :], in_=x2t[:, :], func=AF.Sigmoid, scale=GC1), reads=[bx2], writes=[bx2])
            P.op("dve", lambda e, t=t, j=j, yrt=yrt, x2t=x2t: e.tensor_tensor(
                out=ygT[:, t, :].rearrange("p (c j) -> p j c", j=8)[:, j, :], in0=yrt[:, :], in1=x2t[:, :], op=ALU.mult),
                reads=[byr, bx2], writes=[B("ygT")])
    colblocks = [(512 * i, 512) for i in range(4)] + [(2048, 16)]
    cnt = 0
    for (cb0, n) in colblocks:
        for m in range(8):
            bk = bank()
            for k in range(8):
                P.op("pe", lambda e, k=k, m=m, bk=bk, cb0=cb0, n=n: e.matmul(
                    ps[:, bk, 0:n], lhsT=wglu[:, k, m * 128:(m + 1) * 128], rhs=ygT[:, k, cb0:cb0 + n], start=(k == 0), stop=(k == 7)),
                    reads=[B("wglu"), B("ygT")], writes=[B("ps", bk)])
            i2 = cnt % 2
            cnt += 1
            sg, yb = sgb[i2], ysb[i2]
            P.op("act", lambda e, bk=bk, n=n, m=m, sg=sg: e.activation(out=sg[:, 0:n], in_=ps[:, bk, 0:n], func=AF.Sigmoid,
                                                                     bias=vecs[:, V_BGLU + m:V_BGLU + m + 1], scale=1.0),
                 reads=[B("ps", bk), B("vecs")], writes=[B("sgb", i2)])
            P.op("dve", lambda e, n=n, m=m, cb0=cb0, sg=sg, yb=yb: e.tensor_tensor(out=yb[:, 0:n], in0=ygT[:, m, cb0:cb0 + n], in1=sg[:, 0:n], op=ALU.mult),
                 reads=[B("ygT"), B("sgb", i2)], writes=[B("ysb", i2)])
            P.dma("sp", ysd[m, :, cb0:cb0 + n], yb[:, 0:n], reads=[B("ysb", i2)], writes=[B("ysd", m, cb0)])
    if stop_after == "ssm":
        return finish_prog()

    P.barrier()
    A.lo = LM
    A.hi = RM
    wout = A.left("wout", [128, 16, D], BF16)
    ytile = [A.left("ytile", [128, 16, 128], BF16) for _ in range(2)]
    sqt = [A.left("sqt", [128, 16, 128], BF16) for _ in range(2)]
    xres = [A.left("xres", [128, D], F32) for _ in range(2)]
    h1t = [A.left("h1t", [128, D], F32) for _ in range(2)]
    mst = A.left("mst", [128, 16], F32)
    w_out_v = w_out.rearrange("(k p) c -> p k c", p=128)
    for hh in range(4):
        P.dma("pool", wout[:, 4 * hh:4 * hh + 4, :], w_out_v[:, 4 * hh:4 * hh + 4, :], writes=[B("wout", hh)])
    for k in range(16):
        gc = (V_GSSM + k) if k < 8 else (V_GATT + k - 8)
        P.op("act", lambda e, k=k, gc=gc: e.activation(out=wout[:, k, :], in_=wout[:, k, :], func=AF.Copy, scale=vecs[:, gc:gc + 1]),
             reads=[B("wout", k // 4), B("vecs")], writes=[B("wout", k // 4)])
    ysd_bufs = [b for kk, b in B.d.items() if kk[0] == "ysd"]
    yad_bufs = [b for kk, b in B.d.items() if kk[0] == "yad"]
    wout_bufs = [B("wout", hh) for hh in range(4)]
    tiles = [(0, 16)] + [(16 + 128 * i, 128) for i in range(16)]
    for ti, (col0, nt) in enumerate(tiles):
        i2 = ti % 2
        yt, sq, xr, h1 = ytile[i2], sqt[i2], xres[i2], h1t[i2]
        byt, bsq, bxr, bh1 = B("ytile", i2), B("sqt", i2), B("xres", i2), B("h1t", i2)
        P.dma("sp", yt[:, 0:8, 0:nt], ysd[:, :, col0:col0 + nt].rearrange("h p c -> p h c"), reads=ysd_bufs, writes=[byt])
        P.dma("sp", yt[:, 8:16, 0:nt], yad[:, :, col0:col0 + nt].rearrange("h p c -> p h c"), reads=yad_bufs, writes=[byt])
        P.dma("sp", xr[0:nt, :], xown[col0:col0 + nt, :], writes=[bxr])
        P.op("dve", lambda e, yt=yt, sq=sq, nt=nt: e.tensor_tensor(out=sq[:, :, 0:nt], in0=yt[:, :, 0:nt], in1=yt[:, :, 0:nt], op=ALU.mult),
             reads=[byt], writes=[bsq])
        bk = bank()
        for half in range(2):
            for h in range(8):
                P.op("pe", lambda e, half=half, h=h, bk=bk, sq=sq, nt=nt: e.matmul(
                    ps[0:nt, bk, half:half + 1], lhsT=sq[:, 8 * half + h, 0:nt], rhs=ones[:, 0:1], start=(h == 0), stop=(h == 7),
                    skip_group_check=True), reads=[bsq, B("cbf")], writes=[B("ps", bk)])
        bms = B("mst", i2)
        c0 = 8 * i2
        P.op("act", lambda e, bk=bk, nt=nt, c0=c0: e.activation(out=mst[0:nt, c0:c0 + 2], in_=ps[0:nt, bk, 0:2], func=AF.Ln,
                                                              scale=1.0 / 1024, bias=vecs[0:nt, V_EPS:V_EPS + 1]),
             reads=[B("ps", bk), B("vecs")], writes=[bms])
        P.op("act", lambda e, nt=nt, c0=c0: e.activation(out=mst[0:nt, c0 + 2:c0 + 4], in_=mst[0:nt, c0:c0 + 2], func=AF.Exp, scale=-0.5),
             reads=[bms], writes=[bms])
        for n4 in range(4):
            bs, ba = bank(), bank()
            for half, bkk in ((0, bs), (1, ba)):
                for h in range(8):
                    P.op("pe", lambda e, half=half, h=h, bkk=bkk, n4=n4, yt=yt, nt=nt: e.matmul(
                        ps[0:nt, bkk, :], lhsT=yt[:, 8 * half + h, 0:nt], rhs=wout[:, 8 * half + h, n4 * 512:(n4 + 1) * 512],
                        start=(h == 0), stop=(h == 7)), reads=[byt] + wout_bufs, writes=[B("ps", bkk)])
            P.op("dve", lambda e, bs=bs, n4=n4, nt=nt, c0=c0, xr=xr, h1=h1: e.scalar_tensor_tensor(
                out=h1[0:nt, n4 * 512:(n4 + 1) * 512], in0=ps[0:nt, bs, :], scalar=mst[0:nt, c0 + 2:c0 + 3],
                in1=xr[0:nt, n4 * 512:(n4 + 1) * 512], op0=ALU.mult, op1=ALU.add),
                reads=[B("ps", bs), bms, bxr], writes=[bh1])
            P.op("dve", lambda e, ba=ba, n4=n4, nt=nt, c0=c0, h1=h1: e.scalar_tensor_tensor(
                out=h1[0:nt, n4 * 512:(n4 + 1) * 512], in0=ps[0:nt, ba, :], scalar=mst[0:nt, c0 + 3:c0 + 4],
                in1=h1[0:nt, n4 * 512:(n4 + 1) * 512], op0=ALU.mult, op1=ALU.add),
                reads=[B("ps", ba), bms, bh1], writes=[bh1])
        P.dma("sp", h1d[col0:col0 + nt, :], h1[0:nt, :], reads=[bh1], writes=[B("h1d", ti)])
    if stop_after == "mix":
        return finish_prog()

    P.barrier()
    A.lo = LM
    set_gam(V_GFFN)
    gfin = A.right("gfin", [128, D], F32)
    P.dma("sp", gfin[:], gfin_d[:, :], writes=[B("gfin")])
    nctx2 = NormCtx()
    nctx2.xt = [None, None]
    nctx2.xn = [A.left("xn2", [128, D], BF16) for _ in range(2)]
    nctx2.st = A.left("nst2", [128, 8], F32)
    nctx2.i = 0
    h1g = [A.left("h1g", [128, D], F32) for _ in range(4)]
    h1L = A.left("h1L", [16, D], F32)
    hn2T = A.left("hn2T", [128, 16, 528], BF16)
    actT = A.left("actT", [128, NJ, 512], BF16)
    wu = [A.left("wu", [128, 16, 256], BF16) for _ in range(2)]
    wd = [A.left("wd", [128, 22, 512], BF16) for _ in range(2)]
    rawb = [A.left("rawb", [128, 514], F32) for _ in range(4)]
    cvb = [A.left("cvb", [128, 512], F32) for _ in range(4)]
    carry = A.left("carry", [128, 88, 2], F32)
    fst = A.left("fst", [128, 16], F32)
    w_up_v = w_up.rearrange("(k p) c -> p k c", p=128)
    w_down_v = w_down.rearrange("(j p) c -> p j c", p=128)
    h1d_bufs = [b for kk, b in B.d.items() if kk[0] == "h1d"]

    wu_tiles = {}
    wu_next = [0]
    WU_TOTAL = 4 * NJ

    def issue_wu():
        i = wu_next[0]
        if i >= WU_TOTAL:
            return
        jj = i % NJ
        t_ = wu[i % 2]
        b_ = B("wu", i % 2)
        P.dma("pool", t_[:, :, 0:128], w_up_v[:, :, jj * 128:(jj + 1) * 128], writes=[b_])
        P.dma("pool", t_[:, :, 128:256], w_up_v[:, :, DFF + jj * 128:DFF + (jj + 1) * 128], writes=[b_])
        wu_tiles[i] = (t_, b_)
        wu_next[0] += 1

    wd_tiles = {}
    wd_next = [0]
    WD_TOTAL = 4 * 8

    def issue_wd():
        i = wd_next[0]
        if i >= WD_TOTAL:
            return
        n4, half = (i % 8) // 2, i % 2
        t_ = wd[i % 2]
        b_ = B("wd", i % 2)
        P.dma("pool", t_[:, :, :], w_down_v[:, 22 * half:22 * half + 22, n4 * 512:(n4 + 1) * 512], writes=[b_])
        wd_tiles[i] = (t_, b_)
        wd_next[0] += 1

    issue_wu()
    issue_wu()
    issue_wd()
    issue_wd()
    wui = 0
    wdi = 0
    rctr = [0]
    for gi in range(4):
        if gi == 0:
            P.dma("sp", h1L[0:16, :], h1d[0:16, :], reads=h1d_bufs, writes=[B("h1L")])
            norm_transpose(nctx2, None, 16, hn2T, 0, B("hn2T"), from_dram=False, keep=(h1L, B("h1L")))
        for s in range(4):
            r0 = 16 + 512 * gi + 128 * s
            P.dma("sp", h1g[s][:, :], h1d[r0:r0 + 128, :], reads=h1d_bufs, writes=[B("h1g", s)])
            norm_transpose(nctx2, None, 128, hn2T, 16 + 128 * s, B("hn2T"), from_dram=False, keep=(h1g[s], B("h1g", s)))
        for jj in range(NJ):
            wt, wb2 = wu_tiles.pop(wui)
            wui += 1
            cvs = []
            for gv in range(2):
                ch = gv * NJ + jj
                bk = bank()
                for k in range(16):
                    P.op("pe", lambda e, k=k, gv=gv, bk=bk, wt=wt: e.matmul(
                        ps[:, bk, :], lhsT=wt[:, k, 128 * gv:128 * gv + 128], rhs=hn2T[:, k, 16:528], start=(k == 0), stop=(k == 15)),
                        reads=[wb2, B("hn2T")], writes=[B("ps", bk)])
                ri = rctr[0] % 4
                rctr[0] += 1
                rb, cv = rawb[ri], cvb[ri]
                brb, bcv = B("rawb", ri), B("cvb", ri)
                if gi == 0:
                    bkl = bank()
                    for k in range(16):
                        P.op("pe", lambda e, k=k, gv=gv, bkl=bkl, wt=wt: e.matmul(
                            ps[:, bkl, 0:2], lhsT=wt[:, k, 128 * gv:128 * gv + 128], rhs=hn2T[:, k, 14:16], start=(k == 0), stop=(k == 15)),
                            reads=[wb2, B("hn2T")], writes=[B("ps", bkl)])
                    P.op("act", lambda e, bkl=bkl, rb=rb: e.activation(out=rb[:, 0:2], in_=ps[:, bkl, 0:2], func=AF.Copy),
                         reads=[B("ps", bkl)], writes=[brb])
                else:
                    P.op("pool", lambda e, ch=ch, rb=rb: e.tensor_copy(out=rb[:, 0:2], in_=carry[:, ch, :]), reads=[B("carry", ch)], writes=[brb])
                P.op("act", lambda e, bk=bk, rb=rb: e.activation(out=rb[:, 2:514], in_=ps[:, bk, :], func=AF.Copy),
                     reads=[B("ps", bk)], writes=[brb])
                if gi < 3:
                    P.op("pool", lambda e, ch=ch, rb=rb: e.tensor_copy(out=carry[:, ch, :], in_=rb[:, 512:514]), reads=[brb], writes=[B("carry", ch)])
                w0 = vecs[:, V_CW + 3 * ch + 0:V_CW + 3 * ch + 1]
                w1 = vecs[:, V_CW + 3 * ch + 1:V_CW + 3 * ch + 2]
                w2 = vecs[:, V_CW + 3 * ch + 2:V_CW + 3 * ch + 3]
                cb_ = vecs[:, V_CB + ch:V_CB + ch + 1]
                P.op("dve", lambda e, rb=rb, cv=cv, w2=w2, cb_=cb_: e.tensor_scalar(out=cv[:, :], in0=rb[:, 2:514], scalar1=w2, scalar2=cb_,
                                                                                  op0=ALU.mult, op1=ALU.add),
                     reads=[brb, B("vecs")], writes=[bcv])
                P.op("dve", lambda e, rb=rb, cv=cv, w1=w1: e.scalar_tensor_tensor(out=cv[:, :], in0=rb[:, 1:513], scalar=w1, in1=cv[:, :],
                                                                                op0=ALU.mult, op1=ALU.add),
                     reads=[brb, bcv, B("vecs")], writes=[bcv])
                P.op("dve", lambda e, rb=rb, cv=cv, w0=w0: e.scalar_tensor_tensor(out=cv[:, :], in0=rb[:, 0:512], scalar=w0, in1=cv[:, :],
                                                                                op0=ALU.mult, op1=ALU.add),
                     reads=[brb, bcv, B("vecs")], writes=[bcv])
                cvs.append((cv, bcv, rb, brb))
            (cg, bcg, rbg, brbg), (cvv, bcvv, _, _) = cvs
            P.op("act", lambda e, cg=cg, rbg=rbg: e.activation(out=rbg[:, 0:512], in_=cg[:, :], func=AF.Silu), reads=[bcg], writes=[brbg])
            P.op("dve", lambda e, jj=jj, rbg=rbg, cvv=cvv: e.tensor_tensor(out=actT[:, jj, :], in0=rbg[:, 0:512], in1=cvv[:, :], op=ALU.mult),
                 reads=[brbg, bcvv], writes=[B("actT", jj)])
            issue_wu()
        act_bufs = [B("actT", jj) for jj in range(NJ)]
        for n4 in range(4):
            bks = [bank() for _ in range(4)]
            for half in range(2):
                wt, wb2 = wd_tiles.pop(wdi)
                wdi += 1
                for s in range(4):
                    for j2 in range(22):
                        jj = 22 * half + j2
                        P.op("pe", lambda e, s=s, j2=j2, jj=jj, half=half, wt=wt, bks=bks: e.matmul(
                            ps[:, bks[s], :], lhsT=actT[:, jj, 128 * s:128 * s + 128], rhs=wt[:, j2, :],
                            start=(jj == 0), stop=(jj == NJ - 1), skip_group_check=True),
                            reads=[wb2, B("actT", jj)], writes=[B("ps", bks[s])])
                issue_wd()
            for s in range(4):
                P.op("dve", lambda e, s=s, n4=n4, bks=bks: e.tensor_tensor(
                    out=h1g[s][:, n4 * 512:(n4 + 1) * 512], in0=ps[:, bks[s], :], in1=h1g[s][:, n4 * 512:(n4 + 1) * 512], op=ALU.add),
                    reads=[B("ps", bks[s]), B("h1g", s)], writes=[B("h1g", s)])
        for s in range(4):
            c0 = 4 * s
            bfs = B("fst", s)
            xn_ = nctx2.xn[s % 2]
            bxn_ = B("xn", id(nctx2), s % 2)
            P.op("dve", lambda e, c0=c0: e.memset(fst[:, c0:c0 + 1], 0.0), writes=[bfs])
            P.op("act", lambda e, s=s, c0=c0, xn_=xn_: e.activation(out=xn_[:, :], in_=h1g[s][:, :], func=AF.Square, accum_out=fst[:, c0:c0 + 1]),
                 reads=[B("h1g", s)], writes=[bxn_, bfs])
            P.op("act", lambda e, c0=c0: e.activation(out=fst[:, c0 + 1:c0 + 2], in_=fst[:, c0:c0 + 1], func=AF.Ln, scale=1.0 / D,
                                                     bias=vecs[:, V_EPS:V_EPS + 1]), reads=[bfs, B("vecs")], writes=[bfs])
            P.op("act", lambda e, c0=c0: e.activation(out=fst[:, c0 + 2:c0 + 3], in_=fst[:, c0 + 1:c0 + 2], func=AF.Exp, scale=-0.5),
                 reads=[bfs], writes=[bfs])
            P.op("dve", lambda e, s=s, c0=c0: e.scalar_tensor_tensor(out=h1g[s][:, :], in0=h1g[s][:, :], scalar=fst[:, c0 + 2:c0 + 3], in1=gfin[:, :],
                                                                   op0=ALU.mult, op1=ALU.mult),
                 reads=[B("h1g", s), bfs, B("gfin")], writes=[B("h1g", s)])
            r0 = 512 * gi + 128 * s
            P.dma("sp", out_d[r0:r0 + 128, :], h1g[s][:, :], reads=[B("h1g", s)], writes=[B("out", gi, s)])
    return finish_prog()


def _bf16(a):
    return np.asarray(a).astype(ml_dtypes.bfloat16)


def prep_shared(inp):
    f = np.float32
    sh = {}
    sh["w_in"] = np.ascontiguousarray(inp["w_in"][0], dtype=f)
    sh["w_glu"] = np.ascontiguousarray(inp["w_glu"][0], dtype=f)
    sh["w_out"] = np.ascontiguousarray(inp["w_out"][0], dtype=f)
    sh["w_up"] = np.ascontiguousarray(inp["w_up"][0], dtype=f)
    sh["w_down"] = np.ascontiguousarray(inp["w_down"][0], dtype=f)
    vecs = np.zeros((128, NV), f)
    vecs[:, V_GMIX:V_GMIX + 16] = inp["norm_mix_g"][0].reshape(16, 128).T
    vecs[:, V_GFFN:V_GFFN + 16] = inp["norm_ffn_g"][0].reshape(16, 128).T
    vecs[:, V_GSSM:V_GSSM + 8] = inp["g_ssm_out"][0].reshape(8, 128).T
    vecs[:, V_GATT:V_GATT + 8] = inp["g_attn_out"][0].reshape(8, 128).T
    vecs[:, V_BGLU:V_BGLU + 8] = inp["b_glu"][0].reshape(8, 128).T
    vecs[:, V_DSK:V_DSK + 8] = inp["ssm_d"][0].reshape(8, 128).T
    cw = inp["conv_w"][0]
    vecs[:, V_CW:V_CW + 264] = cw.reshape(3, 88, 128).transpose(2, 1, 0).reshape(128, 264)
    vecs[:, V_CB:V_CB + 88] = inp["conv_b"][0].reshape(88, 128).T
    vecs[:, V_ZERO] = 0.0
    vecs[:, V_ONE] = 1.0
    vecs[:, V_EPS] = EPS
    vecs[:, V_NEGPI] = -np.pi
    for q in range(4):
        vecs[32 * q:32 * q + 32, V_BAND + q] = 1.0
    sh["vecs"] = vecs
    sh["gfin"] = np.ascontiguousarray(np.broadcast_to(inp["norm_final_g"][None, :], (128, D)), dtype=f)
    cb = np.zeros((128, 640), f)
    cb[:, 0:128] = np.eye(128)
    kk = np.arange(128)
    cb[:, 128:256] = (kk[:, None] >= kk[None, :])
    cb[:, 256:384] = 1.0
    cb[:, 512:640] = (kk[:, None] < kk[None, :])
    sh["cbf"] = _bf16(cb)
    sh["maskd"] = (kk[:, None] < kk[None, :]).astype(f)
    lre = inp["ssm_lambda_re"][0]
    lim = inp["ssm_lambda_im"][0]
    ldt = inp["ssm_log_dt"][0]

    def playout(a):
        return a.reshape(32, 2, 64).transpose(1, 2, 0).reshape(128, 32)

    ldt_gp = np.broadcast_to(ldt[:, None], (64, 64))
    sh["ssmP"] = np.ascontiguousarray(np.concatenate([playout(lre), playout(lim), playout(ldt_gp)], 1), dtype=f)

    def xlayout(a):
        b = a.reshape(8, 4, 2, 64)
        b = b.transpose(1, 0, 2, 3).reshape(4, 1, 8, 128)
        b = np.broadcast_to(b, (4, 32, 8, 128)).reshape(128, 1024)
        return b

    sh["ssmX"] = np.ascontiguousarray(np.concatenate([xlayout(lre), xlayout(lim), xlayout(ldt_gp)], 1), dtype=f)

    def bx(bb):
        o = np.zeros((4, 2, 16, 8, 2, 64), f)
        b6 = bb.reshape(8, 4, 2, 64, 16)
        for gp in range(2):
            o[:, gp, :, :, gp, :] = b6[:, :, gp, :, :].transpose(1, 3, 0, 2)
        return o.reshape(128, 1024)

    sh["BX"] = np.concatenate([bx(inp["ssm_b_re"][0]), bx(inp["ssm_b_im"][0])], 1)

    def bp(bb):
        o = np.zeros((2, 64, 32, 2, 16), f)
        b5 = bb.reshape(32, 2, 64, 16)
        for gp in range(2):
            o[gp, :, :, gp, :] = b5[:, gp, :, :].transpose(1, 0, 2)
        return o.reshape(128, 1024)

    sh["BP"] = np.concatenate([bp(inp["ssm_b_re"][0]), bp(inp["ssm_b_im"][0])], 1)

    def cz(cc):
        o = np.zeros((2, 64, 32, 2, 16), f)
        c5 = cc.reshape(32, 2, 16, 64)
        for gp in range(2):
            o[gp, :, :, gp, :] = c5[:, gp, :, :].transpose(2, 0, 1)
        return o.reshape(128, 1024)

    sh["CZ"] = np.concatenate([cz(inp["ssm_c_re"][0]), cz(inp["ssm_c_im"][0])], 1)
    return sh


def prep_core(inp, sh, b, r):
    f = np.float32
    x = inp["x"][b]
    meta = inp["meta_tokens"]
    m = dict(sh)
    if r == 0:
        m["xpre"] = np.ascontiguousarray(x[0:NPRE], dtype=f)
        m["xown"] = np.ascontiguousarray(np.concatenate([meta, x[0:NOWN]], 0), dtype=f)
        pb, pscale = -60.0, 0.0
    else:
        m["xpre"] = np.ascontiguousarray(np.concatenate([meta, x[0:NPRE - 16]], 0), dtype=f)
        m["xown"] = np.ascontiguousarray(x[NPRE - 16:4096], dtype=f)
        pb, pscale = 0.0, 1.0
    v = sh["vecs"].copy()
    v[:, V_PBIAS] = pb
    v[:, V_PSCALE] = pscale
    m["vecs"] = v
    return m


_NC_CACHE = {}


def kernel(**inputs):
    inp = {k: np.asarray(v) for k, v in inputs.items()}
    if "nc" not in _NC_CACHE:
        _NC_CACHE["nc"] = build_program()
    nc = _NC_CACHE["nc"]
    sh = prep_shared(inp)
    in_maps = []
    for c in range(8):
        in_maps.append(prep_core(inp, sh, c // 2, c % 2))
    res = run_bass_kernel_spmd(nc, in_maps, core_ids=list(range(8)))
    out = np.zeros((4, 4096, D), np.float32)
    for c in range(8):
        b, r = c // 2, c % 2
        out[b, r * NOWN:(r + 1) * NOWN] = res.results[c]["out"]
    return out
```

```python
import bisect
from contextlib import ExitStack
import numpy as np
import ml_dtypes
import concourse.bass as bass
import concourse.mybir as mybir
from concourse.bass_utils import run_bass_kernel_spmd

F32 = mybir.dt.float32
BF16 = mybir.dt.bfloat16
I32 = mybir.dt.int32
AF = mybir.ActivationFunctionType
ALU = mybir.AluOpType

DSIZE = {F32: 4, BF16: 2, I32: 4}


class Acc:
    __slots__ = ("eng", "idx", "dtok")

    def __init__(self, eng, idx, dtok):
        self.eng = eng
        self.idx = idx
        self.dtok = dtok


class Buf:
    __slots__ = ("name", "w", "rs")

    def __init__(self, name):
        self.name = name
        self.w = None
        self.rs = {}


class EngS:
    def __init__(self, name, nslots=0):
        self.name = name
        self.recs = []
        self.ops = []
        self.nops = 0
        self.count = 0
        self.sig_idx = []
        self.sig_val = []
        self.seen = {}
        self.nslots = nslots
        self.dslot = 0
        self.dvals = [0] * nslots


class Prog:
    COMPUTE = ("pe", "act", "dve", "pool")

    def __init__(self, nc, sp_slots=40, pool_slots=24):
        self.nc = nc
        self.E = {n: EngS(n) for n in self.COMPUTE}
        self.E["sp"] = EngS("sp", sp_slots)
        self.E["pool"].nslots = pool_slots
        self.E["pool"].dvals = [0] * pool_slots
        self.ndma = 0

    def resolve(self, acc):
        if acc.dtok is not None:
            return acc.dtok
        e = self.E[acc.eng]
        i = bisect.bisect_left(e.sig_idx, acc.idx)
        if i < len(e.sig_idx):
            return (e.name, e.sig_val[i])
        e.count += 1
        e.ops[-1][1] = True
        e.sig_idx.append(e.nops - 1)
        e.sig_val.append(e.count)
        return (e.name, e.count)

    def _waits(self, E, deps):
        cur = E.nops
        for acc in deps:
            if acc.dtok is None and acc.eng == E.name:
                if E.name == "pe":
                    continue
                if cur - acc.idx > 3:
                    continue
            key, val = self.resolve(acc)
            if E.seen.get(key, 0) < val:
                E.recs.append(("wait", key, val))
                E.seen[key] = val

    @staticmethod
    def _deps(reads, writes):
        deps = []
        for b in reads:
            if b.w is not None:
                deps.append(b.w)
        for b in writes:
            if b.w is not None:
                deps.append(b.w)
            deps.extend(b.rs.values())
        return deps

    def op(self, eng, fn, reads=(), writes=(), signal=False):
        E = self.E[eng]
        self._waits(E, self._deps(reads, writes))
        rec = [fn, False]
        E.recs.append(("op", rec))
        E.ops.append(rec)
        acc = Acc(eng, E.nops, None)
        E.nops += 1
        if signal or eng != "pe":
            E.count += 1
            rec[1] = True
            E.sig_idx.append(E.nops - 1)
            E.sig_val.append(E.count)
        for b in reads:
            b.rs[eng] = acc
        for b in writes:
            b.w = acc
            b.rs = {}
        return acc

    def dma(self, q, out, in_, reads=(), writes=()):
        Q = self.E[q]
        self._waits(Q, self._deps(reads, writes))
        slot = Q.dslot
        Q.dslot = (Q.dslot + 1) % Q.nslots
        key = ("D", q, slot)
        prev = Q.dvals[slot]
        if prev > 0 and Q.seen.get(key, 0) < prev:
            Q.recs.append(("wait", key, prev))
            Q.seen[key] = prev
        val = prev + 16
        Q.dvals[slot] = val
        Q.recs.append(("dma", out, in_, key, val))
        acc = Acc(q, None, (key, val))
        for b in reads:
            b.rs[key] = acc
        for b in writes:
            b.w = acc
            b.rs = {}
        self.ndma += 1
        return acc

    def barrier(self):
        toks = []
        for n in self.COMPUTE:
            e = self.E[n]
            if e.nops > 0:
                toks.append(self.resolve(Acc(n, e.nops - 1, None)))
        for q in ("sp", "pool"):
            Q = self.E[q]
            for s in range(Q.nslots):
                if Q.dvals[s] > 0:
                    toks.append((("D", q, s), Q.dvals[s]))
        for n, E in self.E.items():
            for key, val in toks:
                if key == n:
                    continue
                if E.seen.get(key, 0) < val:
                    E.recs.append(("wait", key, val))
                    E.seen[key] = val

    def finish(self):
        self.barrier()

    def replay(self, stack):
        nc = self.nc
        sems = {}
        for n in self.COMPUTE:
            sems[n] = stack.enter_context(nc.semaphore("s_" + n))
        for q in ("sp", "pool"):
            for s in range(self.E[q].nslots):
                sems[("D", q, s)] = stack.enter_context(nc.semaphore("d_%s_%d" % (q, s)))
        block = stack.enter_context(nc.Block())

        def run(name):
            def f(eng):
                E = self.E[name]
                for r in E.recs:
                    if r[0] == "wait":
                        eng.wait_ge(sems[r[1]], r[2])
                    elif r[0] == "op":
                        ins = r[1][0](eng)
                        if r[1][1]:
                            ins.then_inc(sems[name], 1)
                    else:
                        eng.dma_start(out=r[1], in_=r[2]).then_inc(sems[r[3]], 16)
            return f

        block.sync(run("sp"))
        block.tensor(run("pe"))
        block.scalar(run("act"))
        block.vector(run("dve"))
        block.gpsimd(run("pool"))


class Arena:
    LO = 16512
    HI = 229344

    def __init__(self, nc):
        self.nc = nc
        self.lo = self.LO
        self.hi = self.HI
        self.n = 0

    @staticmethod
    def _size(shape, dtype):
        n = 1
        for s in shape[1:]:
            n *= s
        return (n * DSIZE[dtype] + 63) // 64 * 64

    def left(self, name, shape, dtype):
        sz = self._size(shape, dtype)
        off = self.lo
        self.lo += sz
        assert self.lo <= self.hi, ("sbuf overflow", name, self.lo, self.hi)
        self.n += 1
        return self.nc.alloc_sbuf_tensor_at("%s_%d" % (name, self.n), list(shape), dtype, offset=off)

    def right(self, name, shape, dtype):
        sz = self._size(shape, dtype)
        self.hi -= sz
        assert self.lo <= self.hi, ("sbuf overflow", name, self.lo, self.hi)
        self.n += 1
        return self.nc.alloc_sbuf_tensor_at("%s_%d" % (name, self.n), list(shape), dtype, offset=self.hi)

D = 2048
NPRE = 2048
NLEAD = 16
NOWN = 2048
NO = NLEAD + NOWN
NK = NPRE + NO
DFF = 5632
NJ = DFF // 128
CH_PRE = NPRE // 8
CH_OWN = NO // 8
EPS = 1e-6
TWO_PI = 2.0 * np.pi

V_GMIX = 0
V_GFFN = 16
V_GSSM = 32
V_GATT = 40
V_BGLU = 48
V_DSK = 56
V_CW = 64
V_CB = V_CW + 264
V_PBIAS = V_CB + 88
V_PSCALE = V_PBIAS + 1
V_ZERO = V_PSCALE + 1
V_ONE = V_ZERO + 1
V_EPS = V_ONE + 1
V_NEGPI = V_EPS + 1
V_BAND = V_NEGPI + 1
NV = V_BAND + 4


class BufMap:
    def __init__(self):
        self.d = {}

    def __call__(self, *key):
        b = self.d.get(key)
        if b is None:
            b = Buf(str(key))
            self.d[key] = b
        return b


def bcast_last(ap2d, n):
    a = [list(x) for x in ap2d.ap]
    return bass.AP(ap2d.tensor, ap2d.offset, a + [[0, n]])


class Ctx:
    pass


def derive_trig(C, n, lam_re_d, lam_im_d, ldt_d, T, TI, tag):
    P, B, vecs = C.P, C.B, C.vecs
    b = [B("dT", tag, i) for i in range(11)]
    bi = B("dTI", tag)
    bv = B("vecs")

    def tt(o, a, c, op):
        P.op("dve", lambda e: e.tensor_tensor(out=T[o][:], in0=T[a][:], in1=T[c][:], op=op), reads=[b[a], b[c]], writes=[b[o]])

    def ts(o, a, s1, s2, op0, op1=None):
        if op1 is None:
            P.op("dve", lambda e: e.tensor_scalar(out=T[o][:], in0=T[a][:], scalar1=s1, scalar2=None, op0=op0), reads=[b[a]], writes=[b[o]])
        else:
            P.op("dve", lambda e: e.tensor_scalar(out=T[o][:], in0=T[a][:], scalar1=s1, scalar2=s2, op0=op0, op1=op1), reads=[b[a]], writes=[b[o]])

    def act(o, a, func, scale=1.0, bias=None):
        if bias is None:
            P.op("act", lambda e: e.activation(out=T[o][:], in_=T[a][:], func=func, scale=scale), reads=[b[a]], writes=[b[o]])
        else:
            P.op("act", lambda e: e.activation(out=T[o][:], in_=T[a][:], func=func, scale=scale, bias=bias), reads=[b[a], bv], writes=[b[o]])

    P.dma("sp", T[0][:], lam_re_d, writes=[b[0]])
    P.dma("sp", T[1][:], lam_im_d, writes=[b[1]])
    P.dma("sp", T[2][:], ldt_d, writes=[b[2]])
    ts(0, 0, -1e-4, None, ALU.min)
    act(2, 2, AF.Exp)
    tt(3, 0, 2, ALU.mult)
    act(3, 3, AF.Exp)
    tt(4, 1, 2, ALU.mult)
    ts(4, 4, 1.0 / TWO_PI, 64.5, ALU.mult, ALU.add)

    def sin_of(r, o, t2):
        P.op("dve", lambda e: e.tensor_copy(out=TI[:], in_=T[r][:]), reads=[b[r]], writes=[bi])
        P.op("dve", lambda e: e.tensor_copy(out=T[o][:], in_=TI[:]), reads=[bi], writes=[b[o]])
        tt(o, r, o, ALU.subtract)
        ts(t2, o, 0.0, None, ALU.is_lt)
        tt(o, o, t2, ALU.add)
        ts(o, o, TWO_PI, None, ALU.mult)
        act(o, o, AF.Sin, 1.0, vecs[:, V_NEGPI:V_NEGPI + 1])

    sin_of(4, 5, 6)
    ts(4, 4, 0.25, None, ALU.add)
    sin_of(4, 6, 7)
    tt(7, 3, 6, ALU.mult)
    tt(8, 3, 5, ALU.mult)
    tt(4, 0, 0, ALU.mult)
    tt(5, 1, 1, ALU.mult)
    tt(4, 4, 5, ALU.add)
    P.op("dve", lambda e: e.reciprocal(out=T[4][:], in_=T[4][:]), reads=[b[4]], writes=[b[4]])
    ts(3, 7, -1.0, None, ALU.add)
    tt(5, 3, 0, ALU.mult)
    tt(6, 8, 1, ALU.mult)
    tt(5, 5, 6, ALU.add)
    tt(5, 5, 4, ALU.mult)
    tt(6, 8, 0, ALU.mult)
    tt(9, 3, 1, ALU.mult)
    tt(6, 6, 9, ALU.subtract)
    tt(6, 6, 4, ALU.mult)
    return b


def cmul_pool(C, eng, outr, outi, ar, ai, br, bi_, t1, t2, rd, wr):
    P = C.P
    P.op(eng, lambda e: e.tensor_tensor(out=t1, in0=ar, in1=br, op=ALU.mult), reads=rd, writes=[wr[2]])
    P.op(eng, lambda e: e.tensor_tensor(out=t2, in0=ai, in1=bi_, op=ALU.mult), reads=rd, writes=[wr[3]])
    P.op(eng, lambda e: e.tensor_tensor(out=outr, in0=t1, in1=t2, op=ALU.subtract), reads=[wr[2], wr[3]], writes=[wr[0]])
    P.op(eng, lambda e: e.tensor_tensor(out=t1, in0=ar, in1=bi_, op=ALU.mult), reads=rd, writes=[wr[2]])
    P.op(eng, lambda e: e.tensor_tensor(out=t2, in0=ai, in1=br, op=ALU.mult), reads=rd, writes=[wr[3]])
    P.op(eng, lambda e: e.tensor_tensor(out=outi, in0=t1, in1=t2, op=ALU.add), reads=[wr[2], wr[3]], writes=[wr[1]])


def derive_X(C, LG, ssmX_d, BX_d, base_off, outb):
    P, B, nc = C.P, C.B, C.nc
    n = 256
    T = [nc.alloc_sbuf_tensor_at("dX%d" % i, [128, n], F32, offset=base_off + i * n * 4) for i in range(11)]
    TI = nc.alloc_sbuf_tensor_at("dXi", [128, n], I32, offset=base_off + 11 * n * 4)
    Bre = nc.alloc_sbuf_tensor_at("dXbr", [128, n], F32, offset=base_off + 12 * n * 4)
    Bim = nc.alloc_sbuf_tensor_at("dXbi", [128, n], F32, offset=base_off + 13 * n * 4)
    for h in range(4):
        c0 = 256 * h
        b = derive_trig(C, n, ssmX_d[:, c0:c0 + n], ssmX_d[:, 1024 + c0:1024 + c0 + n], ssmX_d[:, 2048 + c0:2048 + c0 + n], T, TI, "X")
        bbr, bbi = B("dXbr"), B("dXbi")
        P.dma("sp", Bre[:], BX_d[:, c0:c0 + n], writes=[bbr])
        P.dma("sp", Bim[:], BX_d[:, 1024 + c0:1024 + c0 + n], writes=[bbi])
        cur = (5, 6)
        nxt = (4, 9)
        for k in range(8):
            j = 7 - k
            lgr = LG[:, 0, j, 2 * h:2 * h + 2, :].rearrange("p a b -> p (a b)")
            lgi = LG[:, 1, j, 2 * h:2 * h + 2, :].rearrange("p a b -> p (a b)")
            cmul_pool(C, "dve", lgr, lgi, T[cur[0]][:], T[cur[1]][:], Bre[:], Bim[:], T[0][:], T[1][:],
                      [b[cur[0]], b[cur[1]], bbr, bbi], [B("LG"), B("LG"), b[0], b[1]])
            if k < 7:
                cmul_pool(C, "dve", T[nxt[0]][:], T[nxt[1]][:], T[cur[0]][:], T[cur[1]][:], T[7][:], T[8][:], T[2][:], T[3][:],
                          [b[cur[0]], b[cur[1]], b[7], b[8]], [b[nxt[0]], b[nxt[1]], b[2], b[3]])
                cur, nxt = nxt, cur
        outb[:] = b + [bbr, bbi, B("dTI", "X")]
        yield h


def g_matmuls(C, uT, nch, GS, LG, uTm):
    P, B, ps, bank, vecs = C.P, C.B, C.ps, C.bank, C.vecs
    for t in range(8):
        for q in range(4):
            P.op("act", lambda e, t=t, q=q: e.activation(out=uTm[q][:, :, 0:nch], in_=uT[:, t, :, 0:nch], func=AF.Copy,
                                                          scale=vecs[:, V_BAND + q:V_BAND + q + 1]),
                 reads=[B("uT"), B("vecs")], writes=[B("uTm", q)])
        for q in range(4):
            pair = 4 * t + q
            for part in range(2):
                bk = bank()
                for j in range(8):
                    P.op("pe", lambda e, j=j, t=t, q=q, part=part, bk=bk: e.matmul(
                        ps[:, bk, 0:nch], lhsT=LG[:, part, j, t, :], rhs=uTm[q][:, j, 0:nch], start=(j == 0), stop=(j == 7)),
                        reads=[B("LG"), B("uTm", q)], writes=[B("ps", bk)])
                P.op("dve", lambda e, bk=bk, part=part, pair=pair: e.tensor_copy(out=GS[:, part, pair, 1:1 + nch], in_=ps[:, bk, 0:nch]),
                     reads=[B("ps", bk)], writes=[B("GSall")])


def scan_steps(C, GS, S2, M1, M2, T1, T2, nsteps, keep_hist, eng="pool"):
    P, B = C.P, C.B
    for s in range(1, nsteps + 1):
        a = S2[(s - 1) % 2]
        o = S2[s % 2]
        ba, bo = B("S2", (s - 1) % 2), B("S2", s % 2)
        P.op(eng, lambda e, a=a: e.tensor_tensor(out=T1[:], in0=M1[:], in1=a[:], op=ALU.mult), reads=[B("M12"), ba], writes=[B("scT1")])
        P.op(eng, lambda e, a=a: e.tensor_tensor(out=T2[:, 0, :], in0=M2[:, 0, :], in1=a[:, 1, :], op=ALU.mult), reads=[B("M12"), ba], writes=[B("scT2")])
        P.op(eng, lambda e, a=a: e.tensor_tensor(out=T2[:, 1, :], in0=M2[:, 1, :], in1=a[:, 0, :], op=ALU.mult), reads=[B("M12"), ba], writes=[B("scT2")])
        P.op(eng, lambda e: e.tensor_tensor(out=T1[:], in0=T1[:], in1=T2[:], op=ALU.add), reads=[B("scT1"), B("scT2")], writes=[B("scT1")])
        P.op(eng, lambda e, o=o, s=s: e.tensor_tensor(out=o[:], in0=T1[:], in1=GS[:, :, :, s], op=ALU.add), reads=[B("scT1"), B("GSall")], writes=[bo])
        if keep_hist:
            P.op("act", lambda e, o=o, s=s: e.activation(out=GS[:, :, :, s], in_=o[:], func=AF.Copy), reads=[bo], writes=[B("GSh", s)])
        yield s


def derive_P_small(C, ssmP_d, EP, FP, M1, M2):
    P, B, A = C.P, C.B, C.A
    n = 32
    T = [A.right("dP%d" % i, [128, n], F32) for i in range(11)]
    TI = A.right("dPi", [128, n], I32)
    b = derive_trig(C, n, ssmP_d[:, 0:32], ssmP_d[:, 32:64], ssmP_d[:, 64:96], T, TI, "P")
    be = B("EP")
    P.op("dve", lambda e: e.memset(EP[:, 0, 0, :], 1.0), writes=[be])
    P.op("dve", lambda e: e.memset(EP[:, 1, 0, :], 0.0), writes=[be])
    P.op("dve", lambda e: e.tensor_copy(out=EP[:, 0, 1, :], in_=T[7][:]), reads=[b[7]], writes=[be])
    P.op("dve", lambda e: e.tensor_copy(out=EP[:, 1, 1, :], in_=T[8][:]), reads=[b[8]], writes=[be])
    P.op("dve", lambda e: e.tensor_copy(out=FP[:, 0, :], in_=T[5][:]), reads=[b[5]], writes=[B("FP")])
    P.op("dve", lambda e: e.tensor_copy(out=FP[:, 1, :], in_=T[6][:]), reads=[b[6]], writes=[B("FP")])
    for k in range(1, 8):
        cmul_pool(C, "dve", EP[:, 0, k + 1, :], EP[:, 1, k + 1, :], EP[:, 0, k, :], EP[:, 1, k, :], T[7][:], T[8][:], T[0][:], T[1][:],
                  [be, b[7], b[8]], [be, be, b[0], b[1]])
    bm = B("M12")
    P.op("dve", lambda e: e.tensor_copy(out=M1[:, 0, :], in_=EP[:, 0, 8, :]), reads=[be], writes=[bm])
    P.op("dve", lambda e: e.tensor_copy(out=M1[:, 1, :], in_=EP[:, 0, 8, :]), reads=[be], writes=[bm])
    P.op("dve", lambda e: e.tensor_scalar(out=M2[:, 0, :], in0=EP[:, 1, 8, :], scalar1=-1.0, scalar2=None, op0=ALU.mult), reads=[be], writes=[bm])
    P.op("dve", lambda e: e.tensor_copy(out=M2[:, 1, :], in_=EP[:, 1, 8, :]), reads=[be], writes=[bm])


def derive_LZ_K(C, EP, FP, BP_d, CZ_d, LZ, KTi, BbP, tmp_off):
    P, B, nc, ps, bank = C.P, C.B, C.nc, C.ps, C.bank
    n = 1024
    Cre = nc.alloc_sbuf_tensor_at("dZcr", [128, 32, 32], F32, offset=tmp_off)
    Cim = nc.alloc_sbuf_tensor_at("dZci", [128, 32, 32], F32, offset=tmp_off + 4096)
    t1 = nc.alloc_sbuf_tensor_at("dZt1", [128, 32, 32], F32, offset=tmp_off + 8192)
    t2 = nc.alloc_sbuf_tensor_at("dZt2", [128, 32, 32], F32, offset=tmp_off + 12288)
    bcr, bci, bt1, bt2 = B("dZcr"), B("dZci"), B("dZt1"), B("dZt2")
    be, bl = B("EP"), B("LZ")
    P.dma("sp", Cre[:].rearrange("p a b -> p (a b)"), CZ_d[:, 0:1024], writes=[bcr])
    P.dma("sp", Cim[:].rearrange("p a b -> p (a b)"), CZ_d[:, 1024:2048], writes=[bci])
    eng = "pool"
    for k in range(9):
        er = bcast_last(EP[:, 0, k, :], 32)
        ei = bcast_last(EP[:, 1, k, :], 32)
        P.op(eng, lambda e, er=er: e.tensor_tensor(out=t1[:], in0=Cre[:], in1=er, op=ALU.mult), reads=[bcr, be], writes=[bt1])
        P.op(eng, lambda e, ei=ei: e.tensor_tensor(out=t2[:], in0=Cim[:], in1=ei, op=ALU.mult), reads=[bci, be], writes=[bt2])
        P.op(eng, lambda e, k=k: e.tensor_tensor(out=LZ[:, 0, k, :, :], in0=t1[:], in1=t2[:], op=ALU.subtract), reads=[bt1, bt2], writes=[bl])
        P.op(eng, lambda e, ei=ei: e.tensor_tensor(out=t1[:], in0=Cre[:], in1=ei, op=ALU.mult), reads=[bcr, be], writes=[bt1])
        P.op(eng, lambda e, er=er: e.tensor_tensor(out=t2[:], in0=Cim[:], in1=er, op=ALU.mult), reads=[bci, be], writes=[bt2])
        P.op(eng, lambda e: e.tensor_tensor(out=t1[:], in0=t1[:], in1=t2[:], op=ALU.add), reads=[bt1, bt2], writes=[bt1])
        P.op(eng, lambda e, k=k: e.tensor_scalar(out=LZ[:, 1, k, :, :], in0=t1[:], scalar1=-1.0, scalar2=None, op0=ALU.mult), reads=[bt1], writes=[bl])
    bb = B("BbP")
    P.dma("sp", Cre[:].rearrange("p a b -> p (a b)"), BP_d[:, 0:1024], reads=[], writes=[bcr])
    P.dma("sp", Cim[:].rearrange("p a b -> p (a b)"), BP_d[:, 1024:2048], reads=[], writes=[bci])
    fr = bcast_last(FP[:, 0, :], 32)
    fi = bcast_last(FP[:, 1, :], 32)
    bf = B("FP")
    P.op(eng, lambda e: e.tensor_tensor(out=t1[:], in0=Cre[:], in1=fr, op=ALU.mult), reads=[bcr, bf], writes=[bt1])
    P.op(eng, lambda e: e.tensor_tensor(out=t2[:], in0=Cim[:], in1=fi, op=ALU.mult), reads=[bci, bf], writes=[bt2])
    P.op(eng, lambda e: e.tensor_tensor(out=BbP[:, 0, :, :], in0=t1[:], in1=t2[:], op=ALU.subtract), reads=[bt1, bt2], writes=[bb])
    P.op(eng, lambda e: e.tensor_tensor(out=t1[:], in0=Cre[:], in1=fi, op=ALU.mult), reads=[bcr, bf], writes=[bt1])
    P.op(eng, lambda e: e.tensor_tensor(out=t2[:], in0=Cim[:], in1=fr, op=ALU.mult), reads=[bci, bf], writes=[bt2])
    P.op(eng, lambda e: e.tensor_tensor(out=BbP[:, 1, :, :], in0=t1[:], in1=t2[:], op=ALU.add), reads=[bt1, bt2], writes=[bb])
    bk_ = B("KTi")
    P.op("pool", lambda e: e.memset(KTi[:].rearrange("p a b c -> p (a b c)"), 0.0), writes=[bk_])
    for pair in range(32):
        t, q = pair // 4, pair % 4
        bk = bank()
        for part in range(2):
            P.op("pe", lambda e, pair=pair, part=part, q=q, bk=bk: e.matmul(
                ps[32 * q:32 * q + 32, bk, 0:256], lhsT=BbP[:, part, pair, :], rhs=LZ[:, part, 0:8, pair, :],
                start=(part == 0), stop=(part == 1), tile_position=(0, 32 * q)), reads=[bb, bl], writes=[B("ps", bk)])
        P.op("dve", lambda e, t=t, q=q, bk=bk: e.tensor_copy(
            out=KTi[32 * q:32 * q + 32, t, :, 32 * q:32 * q + 32],
            in_=ps[32 * q:32 * q + 32, bk, 0:256].rearrange("p (a b) -> p a b", a=8)), reads=[B("ps", bk)], writes=[bk_])
    return 16384


def build_program(debug=False, stop_after=None):
    nc = bass.Bass("TRN2", target_bir_lowering=False)
    P = Prog(nc)
    A = Arena(nc)
    B = BufMap()

    def din(name, shape, dt=F32):
        return nc.dram_tensor(name, list(shape), dt, kind="ExternalInput").ap()

    skind = "ExternalOutput" if debug else "Internal"

    def dscr(name, shape, dt):
        return nc.dram_tensor(name, list(shape), dt, kind=skind).ap()

    xpre = din("xpre", [NPRE, D])
    xown = din("xown", [NO, D])
    w_in = din("w_in", [D, 4096])
    w_glu = din("w_glu", [1024, 1024])
    w_out = din("w_out", [D, D])
    w_up = din("w_up", [D, 2 * DFF])
    w_down = din("w_down", [DFF, D])
    vecs_d = din("vecs", [128, NV])
    gfin_d = din("gfin", [128, D])
    cbf_d = din("cbf", [128, 640], BF16)
    maskd_d = din("maskd", [128, 128])
    ssmP_d = din("ssmP", [128, 96])
    ssmX_d = din("ssmX", [128, 3072])
    BX_d = din("BX", [128, 2048])
    BP_d = din("BP", [128, 2048])
    CZ_d = din("CZ", [128, 2048])
    out_d = nc.dram_tensor("out", [NOWN, D], F32, kind="ExternalOutput").ap()

    KTd = dscr("KTd", [8, 128, NK], BF16)
    Vd = dscr("Vd", [NK, 1024], BF16)
    QTd = dscr("QTd", [8, 128, NO], BF16)
    uTd = dscr("uTd", [128, 8 * 8 * CH_OWN], BF16)
    yad = dscr("yad", [8, 128, NO], BF16)
    ysd = dscr("ysd", [8, 128, NO], BF16)
    h1d = dscr("h1d", [NO, D], F32)
    winb = nc.dram_tensor("winb", [16, 128, 16 * 256], BF16, kind="Internal").ap()
    wupb = nc.dram_tensor("wupb", [22, 128, 16 * 512], BF16, kind="Internal").ap()
    wdnb = nc.dram_tensor("wdnb", [16, 128, 11 * 512], BF16, kind="Internal").ap()
    wc_done = set()

    def wload(name, dram, idx, tflat, b, parts):
        key = (name, idx)
        if key not in wc_done:
            for (dst, src) in parts:
                P.dma("pool", dst, src, writes=[b])
            P.dma("sp", dram[idx, :, :], tflat, reads=[b], writes=[B("wc", name, idx)])
            wc_done.add(key)
        else:
            P.dma("sp", tflat, dram[idx, :, :], reads=[B("wc", name, idx)], writes=[b])


    vecs = A.right("vecs", [128, NV], F32)
    cbf = A.right("cbf", [128, 640], BF16)
    maskd = A.right("maskd", [128, 128], F32)
    gamB = A.right("gamB", [128, 16, 128], BF16)
    ident = cbf[:, 0:128]
    tri = cbf[:, 128:256]
    ones = cbf[:, 256:384]
    zer = cbf[:, 384:512]
    stri = cbf[:, 512:640]

    def vcol(c, n=128):
        return vecs[0:n, c:c + 1]

    ps = nc.alloc_psum_tensor("ps", [128, 8, 512], F32)
    psT = ps[:, 6:8, :].bitcast(BF16).rearrange("p b (k t) -> p (b k) t", t=128)
    psctr = [0]

    def bank():
        b = psctr[0] % 6
        psctr[0] += 1
        return b

    def finish_prog():
        P.finish()
        with ExitStack() as st_:
            P.replay(st_)
        nc._prog = P
        return nc

    P.dma("sp", vecs[:], vecs_d[:, :], writes=[B("vecs")])
    P.dma("sp", cbf[:], cbf_d[:, :], writes=[B("cbf")])
    P.dma("sp", maskd[:], maskd_d[:, :], writes=[B("maskd")])

    def set_gam(col):
        P.op("dve", lambda e: e.tensor_copy(out=gamB[:], in_=bcast_last(vecs[:, col:col + 16], 128)),
             reads=[B("vecs")], writes=[B("gamB")])

    RM = A.hi

    class NormCtx:
        pass

    def make_norm_ctx():
        c = NormCtx()
        c.xt = [A.left("xt", [128, D], F32)] * 2
        c.xn = [A.left("xn", [128, D], BF16) for _ in range(2)]
        c.st = A.left("nst", [128, 8], F32)
        c.i = 0
        return c

    def norm_transpose(nctx, src_ap, nrows, hnT, hcol0, hbuf, from_dram=True, keep=None):
        i = nctx.i % 2
        nctx.i += 1
        xn = nctx.xn[i]
        st = nctx.st
        if from_dram:
            xt = nctx.xt[i]
            bx = B("xt", id(nctx))
            P.dma("sp", xt[0:nrows, :], src_ap, writes=[bx])
        else:
            xt, bx = keep
        bxn = B("xn", id(nctx), i)
        bst = B("nst", id(nctx), i)
        c0 = 4 * i
        P.op("dve", lambda e: e.memset(st[0:nrows, c0:c0 + 1], 0.0), writes=[bst])
        P.op("act", lambda e: e.activation(out=xn[0:nrows, :], in_=xt[0:nrows, :], func=AF.Square,
                                           accum_out=st[0:nrows, c0:c0 + 1]), reads=[bx], writes=[bxn, bst])
        P.op("act", lambda e: e.activation(out=st[0:nrows, c0 + 1:c0 + 2], in_=st[0:nrows, c0:c0 + 1], func=AF.Ln,
                                           scale=1.0 / D, bias=vcol(V_EPS, nrows)), reads=[bst, B("vecs")], writes=[bst])
        P.op("act", lambda e: e.activation(out=st[0:nrows, c0 + 2:c0 + 3], in_=st[0:nrows, c0 + 1:c0 + 2], func=AF.Exp,
                                           scale=-0.5), reads=[bst], writes=[bst])
        P.op("dve", lambda e: e.tensor_scalar(out=xn[0:nrows, :], in0=xt[0:nrows, :], scalar1=st[0:nrows, c0 + 2:c0 + 3],
                                              scalar2=None, op0=ALU.mult), reads=[bx, bst], writes=[bxn])
        for k in range(16):
            P.op("pe", lambda e, k=k: e.transpose(out=psT[:, k, 0:nrows], in_=xn[0:nrows, k * 128:(k + 1) * 128],
                                                  identity=ident[0:nrows, 0:nrows]),
                 reads=[bxn, B("cbf")], writes=[B("psT")])
        P.op("dve", lambda e: e.tensor_tensor(out=hnT[:, :, hcol0:hcol0 + nrows], in0=psT[:, :, 0:nrows],
                                              in1=gamB[:, :, 0:nrows], op=ALU.mult),
             reads=[B("psT"), B("gamB")], writes=[hbuf])

    class Ring:
        def __init__(self, name, shape, n):
            self.t = [A.left(name, shape, BF16) for _ in range(n)]
            self.n = n
            self.name = name
            self.i = 0

        def next(self):
            i = self.i % self.n
            self.i += 1
            return self.t[i], B(self.name, id(self), i)

    w_in_v = w_in.rearrange("(k p) c -> p k c", p=128)

    LM = A.lo
    nctx = make_norm_ctx()
    hnT = [A.left("hnT", [128, 16, 528], BF16) for _ in range(2)]
    wring = Ring("win", [128, 16, 256], 3)
    kst = [A.left("kst", [128, 528], BF16) for _ in range(2)]
    vst = [A.left("vst", [128, 256], BF16) for _ in range(3)]
    vctr = [0]
    uT_own = A.left("uTown", [128, 8, 8, CH_OWN], BF16)
    uT_pre = uT_own
    LG = A.left("LG", [128, 2, 8, 8, 128], BF16)
    R1M = A.lo
    kctr = [0]

    set_gam(V_GMIX)

    scan_gen = [None]

    def pump(nst):
        g = scan_gen[0]
        if g is None:
            return
        for _ in range(nst):
            try:
                next(g)
            except StopIteration:
                scan_gen[0] = None
                return

    def after_prefix():
        for q in range(4):
            P.op("act", lambda e, q=q: e.activation(out=uTm[q][:, 0, 0:1], in_=vecs[:, V_ZERO:V_ZERO + 1], func=AF.Copy),
                 reads=dX_bufs + [B("vecs")], writes=[B("uTm", q)])
        P.op("pool", lambda e: e.memset(GS[:, :, :, 0:1], 0.0), writes=[B("GSall")])
        P.op("pool", lambda e: e.memset(S2[0][:], 0.0), writes=[B("S2", 0)])
        g_matmuls(C, uT_pre, CH_PRE, GS, LG, uTm)
        scan_gen[0] = scan_steps(C, GS, S2, M1, M2, scT1, scT2, CH_PRE, False, eng="dve")

    def group_subtiles(kind, gi):
        subs = []
        if kind == "pre":
            for s in range(4):
                r0 = 512 * gi + 128 * s
                subs.append((xpre[r0:r0 + 128, :], 128, 16 + 128 * s, r0))
        else:
            if gi == 0:
                subs.append((xown[0:16, :], 16, 0, NPRE))
            for s in range(4):
                r0 = 16 + 512 * gi + 128 * s
                subs.append((xown[r0:r0 + 128, :], 128, 16 + 128 * s, NPRE + r0))
        return subs

    def emit_norm(kind, gi, hi):
        for (src, nrows, hcol0, _) in group_subtiles(kind, gi):
            norm_transpose(nctx, src, nrows, hnT[hi], hcol0, B("hnT", hi))

    groups = [("pre", g) for g in range(4)] + [("own", g) for g in range(4)]
    wsched = []
    for gidx, (kind, gi) in enumerate(groups):
        blocks = [0, 1, 2, 3, 8, 9, 10, 11, 12, 13, 14, 15] if kind == "pre" else list(range(16))
        for cb in blocks:
            wsched.append((gidx, cb))
    wtiles = {}
    wnext = [0]

    def issue_w():
        if wnext[0] >= len(wsched):
            return
        gidx, cb = wsched[wnext[0]]
        t, b = wring.next()
        wload("win", winb, cb, t[:, :, :].rearrange("p a b -> p (a b)"), b, [(t[:, :, :], w_in_v[:, :, cb * 256:(cb + 1) * 256])])
        wtiles[wnext[0]] = (t, b)
        wnext[0] += 1

    for _ in range(3):
        issue_w()
    emit_norm(*groups[0], 0)
    C = Ctx()
    C.P, C.A, C.B, C.nc, C.ps, C.bank, C.vecs, C.psT = P, A, B, nc, ps, bank, vecs, psT
    GS = A.right("GS", [128, 2, 32, CH_OWN + 1], BF16)
    S2 = [A.right("S2", [128, 2, 32], F32) for _ in range(2)]
    M1 = A.right("M1", [128, 2, 32], F32)
    M2 = A.right("M2", [128, 2, 32], F32)
    scT1 = A.right("scT1", [128, 2, 32], F32)
    scT2 = A.right("scT2", [128, 2, 32], F32)
    Hpf = A.right("Hpf", [128, 2, 32], F32)
    EP = A.right("EP", [128, 2, 9, 32], F32)
    FP = A.right("FP", [128, 2, 32], F32)
    dX_bufs = []
    dX_used = 14 * 256 * 4
    dx_gen = derive_X(C, LG, ssmX_d, BX_d, R1M, dX_bufs)
    uTm = [nc.alloc_sbuf_tensor_at("uTm%d" % q, [128, 8, CH_OWN], BF16, offset=R1M + q * 8 * CH_OWN * 2 + (q * 64)) for q in range(4)]
    r1_size = max(dX_used, 4 * (8 * CH_OWN * 2 + 64))
    A.lo = R1M + r1_size
    assert A.lo <= A.hi, ("R1 overflow", A.lo, A.hi)
    wi = 0
    for gidx, (kind, gi) in enumerate(groups):
        hi = gidx % 2
        hT = hnT[hi]
        hb = B("hnT", hi)
        if gidx + 1 < len(groups):
            emit_norm(*groups[gidx + 1], (gidx + 1) % 2)
        next(dx_gen, None)
        if gidx == 0:
            derive_P_small(C, ssmP_d, EP, FP, M1, M2)
            assert A.lo <= A.hi, ("R1 overflow 2", A.lo, A.hi)
        subs = group_subtiles(kind, gi)
        lead = (kind == "own" and gi == 0)
        ranges = ([(0, 16)] if lead else []) + [(16, 512)]
        blocks = [0, 1, 2, 3, 8, 9, 10, 11, 12, 13, 14, 15] if kind == "pre" else list(range(16))
        uT = uT_pre if kind == "pre" else uT_own
        ub = B("uT")
        for cb in blocks:
            wt, wb = wtiles.pop(wi)
            wi += 1
            if cb < 12:
                for m in range(2):
                    oc = cb * 2 + m
                    for (c0, n) in ranges:
                        bk = bank()
                        for k in range(16):
                            P.op("pe", lambda e, k=k, m=m, c0=c0, n=n, bk=bk, wt=wt, hT=hT: e.matmul(
                                ps[:, bk, 0:n], lhsT=wt[:, k, m * 128:(m + 1) * 128], rhs=hT[:, k, c0:c0 + n],
                                start=(k == 0), stop=(k == 15)), reads=[wb, hb], writes=[B("ps", bk)])
                        if oc < 8:
                            if kind == "pre":
                                ch0 = 64 * gi
                            else:
                                ch0 = 0 if c0 == 0 else 2 + 64 * gi
                            nch = n // 8
                            P.op("act", lambda e, bk=bk, n=n, oc=oc, ch0=ch0, nch=nch, uT=uT: e.activation(
                                out=uT[:, oc, :, ch0:ch0 + nch], in_=ps[:, bk, 0:n].rearrange("p (c j) -> p j c", j=8),
                                func=AF.Copy), reads=[B("ps", bk)], writes=[ub])
                        else:
                            si = kctr[0] % 2
                            kctr[0] += 1
                            ks = kst[si]
                            P.op("act", lambda e, bk=bk, n=n, ks=ks: e.activation(out=ks[:, 0:n], in_=ps[:, bk, 0:n],
                                                                                 func=AF.Copy),
                                 reads=[B("ps", bk)], writes=[B("kst", si)])
                            if oc < 16:
                                hp = oc - 8
                                qc0 = 0 if c0 == 0 else 16 + 512 * gi
                                P.dma("sp", QTd[hp, :, qc0:qc0 + n], ks[:, 0:n], reads=[B("kst", si)],
                                      writes=[B("QTd", hp, gi, c0)])
                            else:
                                hp = oc - 16
                                if kind == "pre":
                                    kc0 = 512 * gi
                                else:
                                    kc0 = NPRE if c0 == 0 else NPRE + 16 + 512 * gi
                                P.dma("sp", KTd[hp, :, kc0:kc0 + n], ks[:, 0:n], reads=[B("kst", si)],
                                      writes=[B("KTd", hp, kind, gi, c0)])
            else:
                half = cb - 12
                for si, (_, nrows, hcol0, krow0) in enumerate(subs):
                    bk = bank()
                    vi = si if not lead else si
                    for k in range(16):
                        P.op("pe", lambda e, k=k, bk=bk, nrows=nrows, hcol0=hcol0, wt=wt, hT=hT: e.matmul(
                            ps[0:nrows, bk, 0:256], lhsT=hT[:, k, hcol0:hcol0 + nrows], rhs=wt[:, k, :],
                            start=(k == 0), stop=(k == 15)), reads=[wb, hb], writes=[B("ps", bk)])
                    vi = vctr[0] % 3
                    vctr[0] += 1
                    vs = vst[vi]
                    P.op("act", lambda e, bk=bk, nrows=nrows, vs=vs: e.activation(
                        out=vs[0:nrows, :], in_=ps[0:nrows, bk, 0:256], func=AF.Copy),
                        reads=[B("ps", bk)], writes=[B("vst", vi)])
                    P.dma("sp", Vd[krow0:krow0 + nrows, half * 256:(half + 1) * 256], vs[0:nrows, :], reads=[B("vst", vi)],
                          writes=[B("Vd", krow0, half)])
            issue_w()
            if gidx >= 4:
                pump(4)
        if gidx == 3:
            after_prefix()
    if stop_after == "inproj":
        P.dma("sp", uTd[:, :], uT_own[:].rearrange("p a b c -> p (a b c)"), reads=[B("uT", "own")], writes=[B("uTd")])
        return finish_prog()

    pump(10 ** 9)
    P.op("act", lambda e: e.activation(out=Hpf[:], in_=S2[CH_PRE % 2][:], func=AF.Copy, scale=vecs[:, V_PSCALE:V_PSCALE + 1]),
         reads=[B("S2", CH_PRE % 2), B("vecs")], writes=[B("Hpf")])
    g_matmuls(C, uT_own, CH_OWN, GS, LG, uTm)
    P.op("act", lambda e: e.activation(out=GS[:, :, :, 0], in_=Hpf[:], func=AF.Copy), reads=[B("Hpf")], writes=[B("GSall")])
    P.op("act", lambda e: e.activation(out=S2[0][:], in_=Hpf[:], func=AF.Copy), reads=[B("Hpf")], writes=[B("S2", 0)])
    P.dma("sp", uTd[:, :], uT_own[:].rearrange("p a b c -> p (a b c)"), reads=[B("uT")], writes=[B("uTd")])
    scan_gen[0] = scan_steps(C, GS, S2, M1, M2, scT1, scT2, CH_OWN, True)
    if stop_after == "gown":
        pump(10 ** 9)
        return finish_prog()

    P.barrier()
    A.lo = LM
    LZ = A.right("LZ", [128, 2, 9, 32, 32], BF16)
    KTi = A.right("KTi", [128, 8, 8, 128], BF16)
    BbP = A.right("BbP", [128, 2, 32, 32], BF16)
    tmpz = A.left("tmpz", [128, 4096], F32)
    derive_LZ_K(C, EP, FP, BP_d, CZ_d, LZ, KTi, BbP, LM)
    KTb = [A.left("KTb", [128, NK], BF16) for _ in range(2)]
    Vb = [A.left("Vb", [128, 33, 128], BF16) for _ in range(2)]
    QTb = [A.left("QTb", [128, NO], BF16) for _ in range(2)]
    pbuf = [A.left("pb", [128, 2, 512], F32) for _ in range(3)]
    spb = [A.left("spb", [128, 2, 512], BF16) for _ in range(2)]
    eb = [A.left("eb", [128, 2, 512], F32) for _ in range(2)]
    wb_ = [A.left("wb", [128, 2, 512], BF16) for _ in range(2)]
    yst = [A.left("yst", [128, 512], BF16) for _ in range(2)]
    SCALE = 0.125

    def load_hp(hp, i):
        P.dma("sp", KTb[i][:, :], KTd[hp, :, :], reads=[B("KTd", hp, k_, g_, c_) for (k_, g_, c_) in ktd_keys], writes=[B("KTb", i)])
        P.dma("sp", Vb[i][:, 0:16, :], Vd[0:NPRE, hp * 128:(hp + 1) * 128].rearrange("(b p) c -> p b c", p=128),
              reads=vd_bufs, writes=[B("Vb", i)])
        P.dma("sp", Vb[i][0:16, 16, :], Vd[NPRE:NPRE + 16, hp * 128:(hp + 1) * 128], reads=vd_bufs, writes=[B("Vb", i)])
        P.dma("sp", Vb[i][:, 17:33, :], Vd[NPRE + 16:NK, hp * 128:(hp + 1) * 128].rearrange("(b p) c -> p b c", p=128),
              reads=vd_bufs, writes=[B("Vb", i)])
        P.dma("sp", QTb[i][:, :], QTd[hp, :, :], reads=[B("QTd", hp, g_, c_) for (g_, c_) in qtd_keys], writes=[B("QTb", i)])

    ktd_keys = [("pre", g, 16) for g in range(4)] + [("own", g, 16) for g in range(4)] + [("own", 0, 0)]
    qtd_keys = [(g, 16) for g in range(4)] + [(0, 0)]
    vd_bufs = [b for k, b in B.d.items() if k[0] == "Vd"]

    def tiles_for_group(g):
        tl = []
        if g < 0:
            tl.append((NPRE, 16, 16, 0, (0, 16), False))
        else:
            for i in (3, 2, 1, 0):
                ob = 4 * g + i
                tl.append((NPRE + 16 + 128 * ob, 128, 17 + ob, 128 * i, (128 * i, 128 * i + 128), False))
            for ob in range(4 * g - 1, -1, -1):
                tl.append((NPRE + 16 + 128 * ob, 128, 17 + ob, 0, None, False))
            tl.append((NPRE, 16, 16, 0, None, False))
        for pb in range(15, -1, -1):
            tl.append((128 * pb, 128, pb, 0, None, True))
        return tl

    qgroups = [(-1, 0, 16)] + [(g, 16 + 512 * g, 512) for g in range(4)]
    ogrp = [0]

    def bc_heads(ap2d, n):
        a = [list(x) for x in ap2d.ap]
        return bass.AP(ap2d.tensor, ap2d.offset, [a[0], [0, 2], a[1]])

    def attn_group(hp, bi, g, qc0, ncols):
        tl = tiles_for_group(g)
        nt = len(tl)
        ob = 6 + (ogrp[0] % 2)
        ogrp[0] += 1
        kt, vb, qt = KTb[bi], Vb[bi], QTb[bi]
        bkt, bvb, bqt = B("KTb", bi), B("Vb", bi), B("QTb", bi)
        for e_ in range(2):
            P.op("pe", lambda e, e_=e_: e.matmul(ps[:, 4 + e_, 0:ncols], lhsT=zer[:, 0:128], rhs=kt[:, 0:ncols],
                                                   start=True, stop=False, skip_group_check=True),
                 reads=[B("cbf"), bkt], writes=[B("ps", 4 + e_)])
            P.op("pe", lambda e, e_=e_: e.matmul(ps[64 * e_:64 * e_ + 64, ob, 0:ncols], lhsT=zer[:, 0:64], rhs=kt[:, 0:ncols],
                                                   start=True, stop=False, skip_group_check=True, tile_position=(0, 64 * e_)),
                 reads=[B("cbf"), bkt], writes=[B("ps", ob)])

        def pe_S(ti):
            kc0, nk, vs, c0, mk, ispre = tl[ti]
            sl = ti % 2
            for e_ in range(2):
                pe_ = 64 * e_
                P.op("pe", lambda e, e_=e_, pe_=pe_, kc0=kc0, nk=nk, c0=c0, sl=sl: e.matmul(
                    ps[0:nk, 2 * sl + e_, c0:ncols], lhsT=kt[pe_:pe_ + 64, kc0:kc0 + nk], rhs=qt[pe_:pe_ + 64, qc0 + c0:qc0 + ncols],
                    start=True, stop=True), reads=[bkt, bqt], writes=[B("ps", 2 * sl + e_)], signal=(e_ == 1))

        def act_p(ti):
            kc0, nk, vs, c0, mk, ispre = tl[ti]
            sl = ti % 2
            pbt, bpb = pbuf[ti % 3], B("pb", ti % 3)
            bcol = V_PBIAS if ispre else V_ZERO
            P.op("act", lambda e, nk=nk, c0=c0, sl=sl, pbt=pbt, bcol=bcol: e.activation(
                out=pbt[0:nk, :, c0:ncols], in_=ps[0:nk, 2 * sl:2 * sl + 2, c0:ncols], func=AF.Exp, scale=SCALE,
                bias=vecs[0:nk, bcol:bcol + 1]), reads=[B("ps", 2 * sl), B("ps", 2 * sl + 1), B("vecs")], writes=[bpb])
            if mk is not None:
                m0, m1 = mk
                P.op("dve", lambda e, nk=nk, m0=m0, m1=m1, pbt=pbt: e.tensor_tensor(
                    out=pbt[0:nk, :, m0:m1], in0=pbt[0:nk, :, m0:m1], in1=bc_heads(maskd[0:nk, 0:m1 - m0], m1 - m0), op=ALU.mult),
                    reads=[bpb, B("maskd")], writes=[bpb])

        def act_sp(ti):
            kc0, nk, vs, c0, mk, ispre = tl[ti]
            pbt, bpb = pbuf[ti % 3], B("pb", ti % 3)
            sp_, bsp = spb[ti % 2], B("spb", ti % 2)
            P.op("act", lambda e, nk=nk, c0=c0, pbt=pbt, sp_=sp_: e.activation(
                out=sp_[0:nk, :, c0:ncols], in_=pbt[0:nk, :, c0:ncols], func=AF.Ln, scale=1.0, bias=vecs[0:nk, V_ONE:V_ONE + 1]),
                reads=[bpb, B("vecs")], writes=[bsp])

        def pe_A(ti):
            kc0, nk, vs, c0, mk, ispre = tl[ti]
            sp_, bsp = spb[ti % 2], B("spb", ti % 2)
            for e_ in range(2):
                if ti >= 1:
                    kc0b, nkb, _, c0b, _, _ = tl[ti - 1]
                    spo, bspo = spb[(ti - 1) % 2], B("spb", (ti - 1) % 2)
                    P.op("pe", lambda e, e_=e_, nkb=nkb, c0b=c0b, spo=spo: e.matmul(
                        ps[:, 4 + e_, c0b:ncols], lhsT=stri[0:nkb, :], rhs=spo[0:nkb, e_, c0b:ncols], start=False, stop=False,
                        skip_group_check=True), reads=[B("cbf"), bspo], writes=[B("ps", 4 + e_)])
                P.op("pe", lambda e, e_=e_, nk=nk, c0=c0, sp_=sp_: e.matmul(
                    ps[0:nk, 4 + e_, c0:ncols], lhsT=tri[0:nk, 0:nk], rhs=sp_[0:nk, e_, c0:ncols], start=False, stop=(ti == nt - 1),
                    skip_group_check=True), reads=[B("cbf"), bsp], writes=[B("ps", 4 + e_)], signal=True)

        def act_e(ti):
            kc0, nk, vs, c0, mk, ispre = tl[ti]
            ebt, beb = eb[ti % 2], B("eb", ti % 2)
            P.op("act", lambda e, nk=nk, c0=c0, ebt=ebt: e.activation(
                out=ebt[0:nk, :, c0:ncols], in_=ps[0:nk, 4:6, c0:ncols], func=AF.Exp, scale=-1.0),
                reads=[B("ps", 4), B("ps", 5)], writes=[beb])
            pbt, bpb = pbuf[ti % 3], B("pb", ti % 3)
            wbt, bwb = wb_[ti % 2], B("wb", ti % 2)
            P.op("dve", lambda e, nk=nk, c0=c0, pbt=pbt, ebt=ebt, wbt=wbt: e.tensor_tensor(
                out=wbt[0:nk, :, c0:ncols], in0=pbt[0:nk, :, c0:ncols], in1=ebt[0:nk, :, c0:ncols], op=ALU.mult),
                reads=[bpb, beb], writes=[bwb])

        def pe_PV(ti):
            kc0, nk, vs, c0, mk, ispre = tl[ti]
            wbt, bwb = wb_[ti % 2], B("wb", ti % 2)
            for e_ in range(2):
                P.op("pe", lambda e, e_=e_, nk=nk, vs=vs, c0=c0, wbt=wbt: e.matmul(
                    ps[64 * e_:64 * e_ + 64, ob, c0:ncols], lhsT=vb[0:nk, vs, 64 * e_:64 * e_ + 64], rhs=wbt[0:nk, e_, c0:ncols],
                    start=False, stop=(ti == nt - 1), skip_group_check=True, tile_position=(0, 64 * e_)),
                    reads=[bvb, bwb], writes=[B("ps", ob)])

        pe_S(0)
        if nt > 1:
            pe_S(1)
        act_p(0)
        for it in range(nt + 2):
            if it + 2 < nt:
                pe_S(it + 2)
            if 0 <= it - 2 < nt:
                pe_PV(it - 2)
            if it < nt:
                act_sp(it)
            if it + 1 < nt:
                act_p(it + 1)
            if it < nt:
                pe_A(it)
                act_e(it)
            pump(1)
        ys = yst[ogrp[0] % 2]
        bys = B("yst", ogrp[0] % 2)
        P.op("dve", lambda e: e.tensor_copy(out=ys[:, 0:ncols], in_=ps[:, ob, 0:ncols]), reads=[B("ps", ob)], writes=[bys])
        P.dma("sp", yad[hp, :, qc0:qc0 + ncols], ys[:, 0:ncols], reads=[bys], writes=[B("yad", hp, g)])

    load_hp(0, 0)
    for hp in range(8):
        if hp + 1 < 8:
            load_hp(hp + 1, (hp + 1) % 2)
        for (g, qc0, ncols) in qgroups:
            attn_group(hp, hp % 2, g, qc0, ncols)
    pump(10 ** 9)
    P.op("act", lambda e: e.activation(out=scT2[:, 0, 0:1], in_=vecs[:, V_ZERO:V_ZERO + 1], func=AF.Copy),
         reads=[B("GSh", s_) for s_ in range(1, CH_OWN + 1)] + [B("GSall"), B("vecs")], writes=[B("GShist"), B("scT2")])
    if stop_after == "attn":
        return finish_prog()

    P.barrier()
    A.lo = LM
    uTr = A.left("uTr", [128, 8, 8, CH_OWN], BF16)
    ygT = A.left("ygT", [128, 8, NO], BF16)
    wglu = A.left("wglu", [128, 8, 1024], BF16)
    yr = [A.left("yr", [128, CH_OWN], F32) for _ in range(2)]
    x2 = [A.left("x2", [128, CH_OWN], F32) for _ in range(2)]
    sgb = [A.left("sgb", [128, 512], F32) for _ in range(2)]
    ysb = [A.left("ysb", [128, 512], BF16) for _ in range(2)]
    P.dma("sp", uTr[:].rearrange("p a b c -> p (a b c)"), uTd[:, :], reads=[B("uTd")], writes=[B("uTr")])
    P.dma("pool", wglu[:, :, :], w_glu.rearrange("(k p) c -> p k c", p=128), writes=[B("wglu")])
    NC_ = CH_OWN
    GC1, GC2 = 1.5957691216057308, 0.044715
    cnt = 0
    for t in range(8):
        for j in range(8):
            bk = bank()
            nmm = (j + 1) + 8
            im = 0
            for ji in range(j + 1):
                P.op("pe", lambda e, t=t, j=j, ji=ji, bk=bk, im=im: e.matmul(
                    ps[:, bk, 0:NC_], lhsT=KTi[:, t, j - ji, :], rhs=uTr[:, t, ji, :], start=(im == 0), stop=False, skip_group_check=True),
                    reads=[B("KTi"), B("uTr")], writes=[B("ps", bk)])
                im += 1
            for q in range(4):
                pair = 4 * t + q
                for part in range(2):
                    last = (q == 3 and part == 1)
                    P.op("pe", lambda e, t=t, j=j, q=q, part=part, pair=pair, bk=bk, last=last: e.matmul(
                        ps[32 * q:32 * q + 32, bk, 0:NC_], lhsT=LZ[:, part, j + 1, pair, :], rhs=GS[:, part, pair, 0:NC_],
                        start=False, stop=last, skip_group_check=True, tile_position=(0, 32 * q)),
                        reads=[B("LZ"), B("GShist")], writes=[B("ps", bk)])
            i2 = cnt % 2
            cnt += 1
            yrt, x2t = yr[i2], x2[i2]
            byr, bx2 = B("yr", i2), B("x2", i2)
            P.op("dve", lambda e, t=t, j=j, bk=bk, yrt=yrt: e.scalar_tensor_tensor(
                out=yrt[:, :], in0=uTr[:, t, j, :], scalar=vecs[:, V_DSK + t:V_DSK + t + 1], in1=ps[:, bk, 0:NC_],
                op0=ALU.mult, op1=ALU.add), reads=[B("uTr"), B("vecs"), B("ps", bk)], writes=[byr])
            P.op("pool", lambda e, yrt=yrt, x2t=x2t: e.tensor_tensor(out=x2t[:, :], in0=yrt[:, :], in1=yrt[:, :], op=ALU.mult),
                 reads=[byr], writes=[bx2])
            P.op("pool", lambda e, x2t=x2t: e.tensor_scalar(out=x2t[:, :], in0=x2t[:, :], scalar1=GC2, scalar2=1.0, op0=ALU.mult, op1=ALU.add),
                 reads=[bx2], writes=[bx2])
            P.op("pool", lambda e, yrt=yrt, x2t=x2t: e.tensor_tensor(out=x2t[:, :], in0=x2t[:, :], in1=yrt[:, :], op=ALU.mult),
                 reads=[bx2, byr], writes=[bx2])
            P.op("act", lambda e, x2t=x2t: e.activation(out=x2t[:, :], in_=x2t[:, :], func=AF.Sigmoid, scale=GC1), reads=[bx2], writes=[bx2])
            P.op("dve", lambda e, t=t, j=j, yrt=yrt, x2t=x2t: e.tensor_tensor(
                out=ygT[:, t, :].rearrange("p (c j) -> p j c", j=8)[:, j, :], in0=yrt[:, :], in1=x2t[:, :], op=ALU.mult),
                reads=[byr, bx2], writes=[B("ygT")])
    colblocks = [(512 * i, 512) for i in range(4)] + [(2048, 16)]
    cnt = 0
    for (cb0, n) in colblocks:
        for m in range(8):
            bk = bank()
            for k in range(8):
                P.op("pe", lambda e, k=k, m=m, bk=bk, cb0=cb0, n=n: e.matmul(
                    ps[:, bk, 0:n], lhsT=wglu[:, k, m * 128:(m + 1) * 128], rhs=ygT[:, k, cb0:cb0 + n], start=(k == 0), stop=(k == 7)),
                    reads=[B("wglu"), B("ygT")], writes=[B("ps", bk)])
            i2 = cnt % 2
            cnt += 1
            sg, yb = sgb[i2], ysb[i2]
            P.op("act", lambda e, bk=bk, n=n, m=m, sg=sg: e.activation(out=sg[:, 0:n], in_=ps[:, bk, 0:n], func=AF.Sigmoid,
                                                                     bias=vecs[:, V_BGLU + m:V_BGLU + m + 1], scale=1.0),
                 reads=[B("ps", bk), B("vecs")], writes=[B("sgb", i2)])
            P.op("dve", lambda e, n=n, m=m, cb0=cb0, sg=sg, yb=yb: e.tensor_tensor(out=yb[:, 0:n], in0=ygT[:, m, cb0:cb0 + n], in1=sg[:, 0:n], op=ALU.mult),
                 reads=[B("ygT"), B("sgb", i2)], writes=[B("ysb", i2)])
            P.dma("sp", ysd[m, :, cb0:cb0 + n], yb[:, 0:n], reads=[B("ysb", i2)], writes=[B("ysd", m, cb0)])
    if stop_after == "ssm":
        return finish_prog()

    P.barrier()
    A.lo = LM
    A.hi = RM
    wout = A.left("wout", [128, 16, D], BF16)
    ytile = [A.left("ytile", [128, 16, 128], BF16) for _ in range(2)]
    sqt = [A.left("sqt", [128, 16, 128], BF16) for _ in range(2)]
    xres = [A.left("xres", [128, D], F32) for _ in range(2)]
    h1t = [A.left("h1t", [128, D], F32) for _ in range(2)]
    mst = A.left("mst", [128, 16], F32)
    w_out_v = w_out.rearrange("(k p) c -> p k c", p=128)
    for hh in range(4):
        P.dma("pool", wout[:, 4 * hh:4 * hh + 4, :], w_out_v[:, 4 * hh:4 * hh + 4, :], writes=[B("wout", hh)])
    for k in range(16):
        gc = (V_GSSM + k) if k < 8 else (V_GATT + k - 8)
        P.op("act", lambda e, k=k, gc=gc: e.activation(out=wout[:, k, :], in_=wout[:, k, :], func=AF.Copy, scale=vecs[:, gc:gc + 1]),
             reads=[B("wout", k // 4), B("vecs")], writes=[B("wout", k // 4)])
    ysd_bufs = [b for kk, b in B.d.items() if kk[0] == "ysd"]
    yad_bufs = [b for kk, b in B.d.items() if kk[0] == "yad"]
    wout_bufs = [B("wout", hh) for hh in range(4)]
    tiles = [(0, 16)] + [(16 + 128 * i, 128) for i in range(16)]
    for ti, (col0, nt) in enumerate(tiles):
        i2 = ti % 2
        yt, sq, xr, h1 = ytile[i2], sqt[i2], xres[i2], h1t[i2]
        byt, bsq, bxr, bh1 = B("ytile", i2), B("sqt", i2), B("xres", i2), B("h1t", i2)
        P.dma("sp", yt[:, 0:8, 0:nt], ysd[:, :, col0:col0 + nt].rearrange("h p c -> p h c"), reads=ysd_bufs, writes=[byt])
        P.dma("sp", yt[:, 8:16, 0:nt], yad[:, :, col0:col0 + nt].rearrange("h p c -> p h c"), reads=yad_bufs, writes=[byt])
        P.dma("sp", xr[0:nt, :], xown[col0:col0 + nt, :], writes=[bxr])
        P.op("dve", lambda e, yt=yt, sq=sq, nt=nt: e.tensor_tensor(out=sq[:, :, 0:nt], in0=yt[:, :, 0:nt], in1=yt[:, :, 0:nt], op=ALU.mult),
             reads=[byt], writes=[bsq])
        bk = bank()
        for half in range(2):
            for h in range(8):
                P.op("pe", lambda e, half=half, h=h, bk=bk, sq=sq, nt=nt: e.matmul(
                    ps[0:nt, bk, half:half + 1], lhsT=sq[:, 8 * half + h, 0:nt], rhs=ones[:, 0:1], start=(h == 0), stop=(h == 7),
                    skip_group_check=True), reads=[bsq, B("cbf")], writes=[B("ps", bk)])
        bms = B("mst", i2)
        c0 = 8 * i2
        P.op("act", lambda e, bk=bk, nt=nt, c0=c0: e.activation(out=mst[0:nt, c0:c0 + 2], in_=ps[0:nt, bk, 0:2], func=AF.Ln,
                                                              scale=1.0 / 1024, bias=vecs[0:nt, V_EPS:V_EPS + 1]),
             reads=[B("ps", bk), B("vecs")], writes=[bms])
        P.op("act", lambda e, nt=nt, c0=c0: e.activation(out=mst[0:nt, c0 + 2:c0 + 4], in_=mst[0:nt, c0:c0 + 2], func=AF.Exp, scale=-0.5),
             reads=[bms], writes=[bms])
        for n4 in range(4):
            bs, ba = bank(), bank()
            for half, bkk in ((0, bs), (1, ba)):
                for h in range(8):
                    P.op("pe", lambda e, half=half, h=h, bkk=bkk, n4=n4, yt=yt, nt=nt: e.matmul(
                        ps[0:nt, bkk, :], lhsT=yt[:, 8 * half + h, 0:nt], rhs=wout[:, 8 * half + h, n4 * 512:(n4 + 1) * 512],
                        start=(h == 0), stop=(h == 7)), reads=[byt] + wout_bufs, writes=[B("ps", bkk)])
            P.op("dve", lambda e, bs=bs, n4=n4, nt=nt, c0=c0, xr=xr, h1=h1: e.scalar_tensor_tensor(
                out=h1[0:nt, n4 * 512:(n4 + 1) * 512], in0=ps[0:nt, bs, :], scalar=mst[0:nt, c0 + 2:c0 + 3],
                in1=xr[0:nt, n4 * 512:(n4 + 1) * 512], op0=ALU.mult, op1=ALU.add),
                reads=[B("ps", bs), bms, bxr], writes=[bh1])
            P.op("dve", lambda e, ba=ba, n4=n4, nt=nt, c0=c0, h1=h1: e.scalar_tensor_tensor(
                out=h1[0:nt, n4 * 512:(n4 + 1) * 512], in0=ps[0:nt, ba, :], scalar=mst[0:nt, c0 + 3:c0 + 4],
                in1=h1[0:nt, n4 * 512:(n4 + 1) * 512], op0=ALU.mult, op1=ALU.add),
                reads=[B("ps", ba), bms, bh1], writes=[bh1])
        P.dma("sp", h1d[col0:col0 + nt, :], h1[0:nt, :], reads=[bh1], writes=[B("h1d", ti)])
    if stop_after == "mix":
        return finish_prog()

    P.barrier()
    A.lo = LM
    set_gam(V_GFFN)
    gfin = A.right("gfin", [128, D], F32)
    P.dma("sp", gfin[:], gfin_d[:, :], writes=[B("gfin")])
    nctx2 = NormCtx()
    nctx2.xt = [None, None]
    nctx2.xn = [A.left("xn2", [128, D], BF16) for _ in range(2)]
    nctx2.st = A.left("nst2", [128, 8], F32)
    nctx2.i = 0
    h1g = [A.left("h1g", [128, D], F32) for _ in range(4)]
    h1L = A.left("h1L", [16, D], F32)
    hn2T = A.left("hn2T", [128, 16, 528], BF16)
    actT = A.left("actT", [128, NJ, 512], BF16)
    wu = [A.left("wu", [128, 16, 512], BF16) for _ in range(2)]
    wd = [A.left("wd", [128, 11, 512], BF16) for _ in range(2)]
    rawb = [A.left("rawb", [128, 514], F32) for _ in range(4)]
    cvb = [A.left("cvb", [128, 512], F32) for _ in range(4)]
    carry = A.left("carry", [128, 88, 2], F32)
    fst = A.left("fst", [128, 16], F32)
    w_up_v = w_up.rearrange("(k p) c -> p k c", p=128)
    w_down_v = w_down.rearrange("(j p) c -> p j c", p=128)
    h1d_bufs = [b for kk, b in B.d.items() if kk[0] == "h1d"]

    wu_tiles = {}
    wu_next = [0]
    WU_TOTAL = 4 * (NJ // 2)

    def issue_wu():
        i = wu_next[0]
        if i >= WU_TOTAL:
            return
        jp = i % (NJ // 2)
        t_ = wu[i % 2]
        b_ = B("wu", i % 2)
        wload("wup", wupb, jp, t_[:, :, :].rearrange("p a b -> p (a b)"), b_,
              [(t_[:, :, 0:256], w_up_v[:, :, jp * 256:(jp + 1) * 256]),
               (t_[:, :, 256:512], w_up_v[:, :, DFF + jp * 256:DFF + (jp + 1) * 256])])
        wu_tiles[i] = (t_, b_)
        wu_next[0] += 1

    wd_tiles = {}
    wd_next = [0]
    WD_TOTAL = 4 * 16

    def issue_wd():
        i = wd_next[0]
        if i >= WD_TOTAL:
            return
        n4, qtr = (i % 16) // 4, i % 4
        t_ = wd[i % 2]
        b_ = B("wd", i % 2)
        wload("wdn", wdnb, i % 16, t_[:, :, :].rearrange("p a b -> p (a b)"), b_,
              [(t_[:, :, :], w_down_v[:, 11 * qtr:11 * qtr + 11, n4 * 512:(n4 + 1) * 512])])
        wd_tiles[i] = (t_, b_)
        wd_next[0] += 1

    issue_wu()
    issue_wu()
    issue_wd()
    issue_wd()
    wui = 0
    wdi = 0
    rctr = [0]
    for gi in range(4):
        if gi == 0:
            P.dma("sp", h1L[0:16, :], h1d[0:16, :], reads=h1d_bufs, writes=[B("h1L")])
            norm_transpose(nctx2, None, 16, hn2T, 0, B("hn2T"), from_dram=False, keep=(h1L, B("h1L")))
        for s in range(4):
            r0 = 16 + 512 * gi + 128 * s
            P.dma("sp", h1g[s][:, :], h1d[r0:r0 + 128, :], reads=h1d_bufs, writes=[B("h1g", s)])
            norm_transpose(nctx2, None, 128, hn2T, 16 + 128 * s, B("hn2T"), from_dram=False, keep=(h1g[s], B("h1g", s)))
        for jj in range(NJ):
            uu = jj % 2
            if uu == 0:
                wt, wb2 = wu_tiles.pop(wui)
                wui += 1
            cvs = []
            for gv in range(2):
                ch = gv * NJ + jj
                bk = bank()
                for k in range(16):
                    P.op("pe", lambda e, k=k, gv=gv, bk=bk, wt=wt, uu=uu: e.matmul(
                        ps[:, bk, :], lhsT=wt[:, k, 256 * gv + 128 * uu:256 * gv + 128 * uu + 128], rhs=hn2T[:, k, 16:528], start=(k == 0), stop=(k == 15)),
                        reads=[wb2, B("hn2T")], writes=[B("ps", bk)])
                ri = rctr[0] % 4
                rctr[0] += 1
                rb, cv = rawb[ri], cvb[ri]
                brb, bcv = B("rawb", ri), B("cvb", ri)
                if gi == 0:
                    bkl = bank()
                    for k in range(16):
                        P.op("pe", lambda e, k=k, gv=gv, bkl=bkl, wt=wt, uu=uu: e.matmul(
                            ps[:, bkl, 0:2], lhsT=wt[:, k, 256 * gv + 128 * uu:256 * gv + 128 * uu + 128], rhs=hn2T[:, k, 14:16], start=(k == 0), stop=(k == 15)),
                            reads=[wb2, B("hn2T")], writes=[B("ps", bkl)])
                    P.op("act", lambda e, bkl=bkl, rb=rb: e.activation(out=rb[:, 0:2], in_=ps[:, bkl, 0:2], func=AF.Copy),
                         reads=[B("ps", bkl)], writes=[brb])
                else:
                    P.op("dve", lambda e, ch=ch, rb=rb: e.tensor_copy(out=rb[:, 0:2], in_=carry[:, ch, :]), reads=[B("carry", ch)], writes=[brb])
                P.op("act", lambda e, bk=bk, rb=rb: e.activation(out=rb[:, 2:514], in_=ps[:, bk, :], func=AF.Copy),
                     reads=[B("ps", bk)], writes=[brb])
                if gi < 3:
                    P.op("dve", lambda e, ch=ch, rb=rb: e.tensor_copy(out=carry[:, ch, :], in_=rb[:, 512:514]), reads=[brb], writes=[B("carry", ch)])
                w0 = vecs[:, V_CW + 3 * ch + 0:V_CW + 3 * ch + 1]
                w1 = vecs[:, V_CW + 3 * ch + 1:V_CW + 3 * ch + 2]
                w2 = vecs[:, V_CW + 3 * ch + 2:V_CW + 3 * ch + 3]
                cb_ = vecs[:, V_CB + ch:V_CB + ch + 1]
                P.op("dve", lambda e, rb=rb, cv=cv, w2=w2, cb_=cb_: e.tensor_scalar(out=cv[:, :], in0=rb[:, 2:514], scalar1=w2, scalar2=cb_,
                                                                                  op0=ALU.mult, op1=ALU.add),
                     reads=[brb, B("vecs")], writes=[bcv])
                P.op("dve", lambda e, rb=rb, cv=cv, w1=w1: e.scalar_tensor_tensor(out=cv[:, :], in0=rb[:, 1:513], scalar=w1, in1=cv[:, :],
                                                                                op0=ALU.mult, op1=ALU.add),
                     reads=[brb, bcv, B("vecs")], writes=[bcv])
                P.op("dve", lambda e, rb=rb, cv=cv, w0=w0: e.scalar_tensor_tensor(out=cv[:, :], in0=rb[:, 0:512], scalar=w0, in1=cv[:, :],
                                                                                op0=ALU.mult, op1=ALU.add),
                     reads=[brb, bcv, B("vecs")], writes=[bcv])
                cvs.append((cv, bcv, rb, brb))
            (cg, bcg, rbg, brbg), (cvv, bcvv, _, _) = cvs
            P.op("act", lambda e, cg=cg, rbg=rbg: e.activation(out=rbg[:, 0:512], in_=cg[:, :], func=AF.Silu), reads=[bcg], writes=[brbg])
            P.op("dve", lambda e, jj=jj, rbg=rbg, cvv=cvv: e.tensor_tensor(out=actT[:, jj, :], in0=rbg[:, 0:512], in1=cvv[:, :], op=ALU.mult),
                 reads=[brbg, bcvv], writes=[B("actT", jj)])
            if uu == 1:
                issue_wu()
        act_bufs = [B("actT", jj) for jj in range(NJ)]
        for n4 in range(4):
            bks = [bank() for _ in range(4)]
            for qtr in range(4):
                wt, wb2 = wd_tiles.pop(wdi)
                wdi += 1
                for s in range(4):
                    for j2 in range(11):
                        jj = 11 * qtr + j2
                        P.op("pe", lambda e, s=s, j2=j2, jj=jj, wt=wt, bks=bks: e.matmul(
                            ps[:, bks[s], :], lhsT=actT[:, jj, 128 * s:128 * s + 128], rhs=wt[:, j2, :],
                            start=(jj == 0), stop=(jj == NJ - 1), skip_group_check=True),
                            reads=[wb2, B("actT", jj)], writes=[B("ps", bks[s])])
                issue_wd()
            for s in range(4):
                P.op("dve", lambda e, s=s, n4=n4, bks=bks: e.tensor_tensor(
                    out=h1g[s][:, n4 * 512:(n4 + 1) * 512], in0=ps[:, bks[s], :], in1=h1g[s][:, n4 * 512:(n4 + 1) * 512], op=ALU.add),
                    reads=[B("ps", bks[s]), B("h1g", s)], writes=[B("h1g", s)])
        for s in range(4):
            c0 = 4 * s
            bfs = B("fst", s)
            xn_ = nctx2.xn[s % 2]
            bxn_ = B("xn", id(nctx2), s % 2)
            P.op("dve", lambda e, c0=c0: e.memset(fst[:, c0:c0 + 1], 0.0), writes=[bfs])
            P.op("act", lambda e, s=s, c0=c0, xn_=xn_: e.activation(out=xn_[:, :], in_=h1g[s][:, :], func=AF.Square, accum_out=fst[:, c0:c0 + 1]),
                 reads=[B("h1g", s)], writes=[bxn_, bfs])
            P.op("act", lambda e, c0=c0: e.activation(out=fst[:, c0 + 1:c0 + 2], in_=fst[:, c0:c0 + 1], func=AF.Ln, scale=1.0 / D,
                                                     bias=vecs[:, V_EPS:V_EPS + 1]), reads=[bfs, B("vecs")], writes=[bfs])
            P.op("act", lambda e, c0=c0: e.activation(out=fst[:, c0 + 2:c0 + 3], in_=fst[:, c0 + 1:c0 + 2], func=AF.Exp, scale=-0.5),
                 reads=[bfs], writes=[bfs])
            P.op("dve", lambda e, s=s, c0=c0: e.scalar_tensor_tensor(out=h1g[s][:, :], in0=h1g[s][:, :], scalar=fst[:, c0 + 2:c0 + 3], in1=gfin[:, :],
                                                                   op0=ALU.mult, op1=ALU.mult),
                 reads=[B("h1g", s), bfs, B("gfin")], writes=[B("h1g", s)])
            r0 = 512 * gi + 128 * s
            P.dma("sp", out_d[r0:r0 + 128, :], h1g[s][:, :], reads=[B("h1g", s)], writes=[B("out", gi, s)])
    return finish_prog()


def _bf16(a):
    return np.asarray(a).astype(ml_dtypes.bfloat16)


def prep_shared(inp):
    f = np.float32
    sh = {}
    sh["w_in"] = np.ascontiguousarray(inp["w_in"][0], dtype=f)
    sh["w_glu"] = np.ascontiguousarray(inp["w_glu"][0], dtype=f)
    sh["w_out"] = np.ascontiguousarray(inp["w_out"][0], dtype=f)
    sh["w_up"] = np.ascontiguousarray(inp["w_up"][0], dtype=f)
    sh["w_down"] = np.ascontiguousarray(inp["w_down"][0], dtype=f)
    vecs = np.zeros((128, NV), f)
    vecs[:, V_GMIX:V_GMIX + 16] = inp["norm_mix_g"][0].reshape(16, 128).T
    vecs[:, V_GFFN:V_GFFN + 16] = inp["norm_ffn_g"][0].reshape(16, 128).T
    vecs[:, V_GSSM:V_GSSM + 8] = inp["g_ssm_out"][0].reshape(8, 128).T
    vecs[:, V_GATT:V_GATT + 8] = inp["g_attn_out"][0].reshape(8, 128).T
    vecs[:, V_BGLU:V_BGLU + 8] = inp["b_glu"][0].reshape(8, 128).T
    vecs[:, V_DSK:V_DSK + 8] = inp["ssm_d"][0].reshape(8, 128).T
    cw = inp["conv_w"][0]
    vecs[:, V_CW:V_CW + 264] = cw.reshape(3, 88, 128).transpose(2, 1, 0).reshape(128, 264)
    vecs[:, V_CB:V_CB + 88] = inp["conv_b"][0].reshape(88, 128).T
    vecs[:, V_ZERO] = 0.0
    vecs[:, V_ONE] = 1.0
    vecs[:, V_EPS] = EPS
    vecs[:, V_NEGPI] = -np.pi
    for q in range(4):
        vecs[32 * q:32 * q + 32, V_BAND + q] = 1.0
    sh["vecs"] = vecs
    sh["gfin"] = np.ascontiguousarray(np.broadcast_to(inp["norm_final_g"][None, :], (128, D)), dtype=f)
    cb = np.zeros((128, 640), f)
    cb[:, 0:128] = np.eye(128)
    kk = np.arange(128)
    cb[:, 128:256] = (kk[:, None] >= kk[None, :])
    cb[:, 256:384] = 1.0
    cb[:, 512:640] = (kk[:, None] < kk[None, :])
    sh["cbf"] = _bf16(cb)
    sh["maskd"] = (kk[:, None] < kk[None, :]).astype(f)
    lre = inp["ssm_lambda_re"][0]
    lim = inp["ssm_lambda_im"][0]
    ldt = inp["ssm_log_dt"][0]

    def playout(a):
        return a.reshape(32, 2, 64).transpose(1, 2, 0).reshape(128, 32)

    ldt_gp = np.broadcast_to(ldt[:, None], (64, 64))
    sh["ssmP"] = np.ascontiguousarray(np.concatenate([playout(lre), playout(lim), playout(ldt_gp)], 1), dtype=f)

    def xlayout(a):
        b = a.reshape(8, 4, 2, 64)
        b = b.transpose(1, 0, 2, 3).reshape(4, 1, 8, 128)
        b = np.broadcast_to(b, (4, 32, 8, 128)).reshape(128, 1024)
        return b

    sh["ssmX"] = np.ascontiguousarray(np.concatenate([xlayout(lre), xlayout(lim), xlayout(ldt_gp)], 1), dtype=f)

    def bx(bb):
        o = np.zeros((4, 2, 16, 8, 2, 64), f)
        b6 = bb.reshape(8, 4, 2, 64, 16)
        for gp in range(2):
            o[:, gp, :, :, gp, :] = b6[:, :, gp, :, :].transpose(1, 3, 0, 2)
        return o.reshape(128, 1024)

    sh["BX"] = np.concatenate([bx(inp["ssm_b_re"][0]), bx(inp["ssm_b_im"][0])], 1)

    def bp(bb):
        o = np.zeros((2, 64, 32, 2, 16), f)
        b5 = bb.reshape(32, 2, 64, 16)
        for gp in range(2):
            o[gp, :, :, gp, :] = b5[:, gp, :, :].transpose(1, 0, 2)
        return o.reshape(128, 1024)

    sh["BP"] = np.concatenate([bp(inp["ssm_b_re"][0]), bp(inp["ssm_b_im"][0])], 1)

    def cz(cc):
        o = np.zeros((2, 64, 32, 2, 16), f)
        c5 = cc.reshape(32, 2, 16, 64)
        for gp in range(2):
            o[gp, :, :, gp, :] = c5[:, gp, :, :].transpose(2, 0, 1)
        return o.reshape(128, 1024)

    sh["CZ"] = np.concatenate([cz(inp["ssm_c_re"][0]), cz(inp["ssm_c_im"][0])], 1)
    return sh


def prep_core(inp, sh, b, r):
    f = np.float32
    x = inp["x"][b]
    meta = inp["meta_tokens"]
    m = dict(sh)
    if r == 0:
        m["xpre"] = np.ascontiguousarray(x[0:NPRE], dtype=f)
        m["xown"] = np.ascontiguousarray(np.concatenate([meta, x[0:NOWN]], 0), dtype=f)
        pb, pscale = -60.0, 0.0
    else:
        m["xpre"] = np.ascontiguousarray(np.concatenate([meta, x[0:NPRE - 16]], 0), dtype=f)
        m["xown"] = np.ascontiguousarray(x[NPRE - 16:4096], dtype=f)
        pb, pscale = 0.0, 1.0
    v = sh["vecs"].copy()
    v[:, V_PBIAS] = pb
    v[:, V_PSCALE] = pscale
    m["vecs"] = v
    return m


_NC_CACHE = {}


def kernel(**inputs):
    inp = {k: np.asarray(v) for k, v in inputs.items()}
    if "nc" not in _NC_CACHE:
        _NC_CACHE["nc"] = build_program()
    nc = _NC_CACHE["nc"]
    sh = prep_shared(inp)
    in_maps = []
    for c in range(8):
        in_maps.append(prep_core(inp, sh, c // 2, c % 2))
    res = run_bass_kernel_spmd(nc, in_maps, core_ids=list(range(8)))
    out = np.zeros((4, 4096, D), np.float32)
    for c in range(8):
        b, r = c // 2, c % 2
        out[b, r * NOWN:(r + 1) * NOWN] = res.results[c]["out"]
    return out
```
